# Optimizing a Trainium2 kernel written in Bass

```python
import math
import jax
import jax.numpy as jnp
from jax import lax
import numpy as np

D_MODEL = 1024
BATCH = 2
SEQ = 8192
DEPTH = 1

D_A = 1024
G_A = 8
C_A = D_A // G_A
CHUNK = 128
N_HEADS_B = 8
HD_B = 128
D_B = N_HEADS_B * HD_B
IDX_HEADS = 8
IDX_DIM = 64
INDEX_TOPK_MAX = 256
Q_BLOCK = 128
D_FF = 2816
CONV_W = 3
ALPHA = (2.0 * DEPTH) ** 0.25
BETA = (8.0 * DEPTH) ** -0.25
LN_EPS = 1e-5

SPLITS = (D_A, D_A, D_B, D_B, D_B, IDX_HEADS * IDX_DIM, IDX_DIM, IDX_HEADS, D_MODEL, D_MODEL)
D_IN_PROJ = sum(SPLITS)
SPLIT_POINTS = tuple(int(s) for s in np.cumsum(SPLITS)[:-1])

kernel_name = "hybrid_gmlp_dsa_convffn_deepnorm"


def layer_norm(x, g, b):
    xf = x.astype(jnp.float32)
    mu = jnp.mean(xf, axis=-1, keepdims=True)
    var = jnp.mean(jnp.square(xf - mu), axis=-1, keepdims=True)
    y = (xf - mu) * lax.rsqrt(var + LN_EPS) * g.astype(jnp.float32) + b.astype(jnp.float32)
    return y.astype(x.dtype)


def spatial_gating(u, v, w_s, b_s):
    B, L, _ = u.shape
    nc = L // CHUNK
    vb = v.reshape(B, nc, CHUNK, G_A, C_A)
    causal = jnp.tril(jnp.ones((CHUNK, CHUNK), dtype=bool))
    w = jnp.where(causal[None], w_s, 0)
    mixed = jnp.einsum('gts,bnsgc->bntgc', w, vb) + jnp.transpose(b_s)[None, None, :, :, None]
    return u * mixed.reshape(B, L, D_A)


def dsa_attention(q, k, v, q_idx, k_idx, w_idx):
    B, L, H, Dh = q.shape
    topk = min(INDEX_TOPK_MAX, L // 4)
    nb = L // Q_BLOCK
    scale = 1.0 / math.sqrt(Dh)
    pos = jnp.arange(L, dtype=jnp.int32)
    k_f = k_idx.astype(jnp.float32)

    def to_blocks(a):
        return jnp.moveaxis(a.reshape((B, nb, Q_BLOCK) + a.shape[2:]), 1, 0)

    def block(args):
        qb, qib, wb, tpos = args
        rel = jax.nn.relu(jnp.einsum('bthd,bsd->bths', qib.astype(jnp.float32), k_f))
        score = jnp.einsum('bth,bths->bts', wb.astype(jnp.float32), rel)
        causal = pos[None, None, :] <= tpos[None, :, None]
        score = jnp.where(causal, score, -jnp.inf)
        _, idx = lax.top_k(score, topk)
        k_sel = jax.vmap(lambda kb, ib: kb[ib])(k, idx)
        v_sel = jax.vmap(lambda vb, ib: vb[ib])(v, idx)
        logits = jnp.einsum('bthd,btkhd->bthk', qb, k_sel).astype(jnp.float32) * scale
        valid = (idx <= tpos[None, :, None])[:, :, None, :]
        p = jax.nn.softmax(jnp.where(valid, logits, -jnp.inf), axis=-1)
        return jnp.einsum('bthk,btkhd->bthd', p.astype(v.dtype), v_sel)

    out = lax.map(block, (to_blocks(q), to_blocks(q_idx), to_blocks(w_idx),
                          pos.reshape(nb, Q_BLOCK)))
    return jnp.moveaxis(out, 0, 1).reshape(B, L, H * Dh)


def causal_dwconv(h, w, b):
    L = h.shape[1]
    hp = jnp.pad(h, ((0, 0), (CONV_W - 1, 0), (0, 0)))
    return b + sum(hp[:, j:j + L, :] * w[j] for j in range(CONV_W))


def setup_inputs(seed: int = 0) -> dict:
    key = jax.random.key(seed)
    ks = jax.random.split(key, 20)

    def nrm(k, shape, scale):
        return jax.random.normal(k, shape, jnp.float32) * scale

    return {
        "x": nrm(ks[0], (BATCH, SEQ, D_MODEL), 1.0),
        "w_in": nrm(ks[1], (DEPTH, D_MODEL, D_IN_PROJ), D_MODEL ** -0.5),
        "ln_v_g": 1.0 + nrm(ks[2], (DEPTH, D_A), 0.02),
        "ln_v_b": nrm(ks[3], (DEPTH, D_A), 0.02),
        "w_spatial": nrm(ks[4], (DEPTH, G_A, CHUNK, CHUNK), CHUNK ** -0.5),
        "b_spatial": 1.0 + nrm(ks[5], (DEPTH, G_A, CHUNK), 0.1),
        "ln_kidx_g": 1.0 + nrm(ks[6], (DEPTH, IDX_DIM), 0.02),
        "ln_kidx_b": nrm(ks[7], (DEPTH, IDX_DIM), 0.02),
        "w_branch_a": nrm(ks[8], (DEPTH, D_A, D_MODEL), D_A ** -0.5),
        "w_branch_b": nrm(ks[9], (DEPTH, D_B, D_MODEL), D_B ** -0.5),
        "w_o": nrm(ks[10], (DEPTH, D_MODEL, D_MODEL), BETA * D_MODEL ** -0.5),
        "ln1_g": 1.0 + nrm(ks[11], (DEPTH, D_MODEL), 0.02),
        "ln1_b": nrm(ks[12], (DEPTH, D_MODEL), 0.02),
        "w_up": nrm(ks[13], (DEPTH, D_MODEL, 2 * D_FF), D_MODEL ** -0.5),
        "conv_w": nrm(ks[14], (DEPTH, CONV_W, 2 * D_FF), CONV_W ** -0.5),
        "conv_b": nrm(ks[15], (DEPTH, 2 * D_FF), 0.02),
        "w_down": nrm(ks[16], (DEPTH, D_FF, D_MODEL), BETA * D_FF ** -0.5),
        "ln2_g": 1.0 + nrm(ks[17], (DEPTH, D_MODEL), 0.02),
        "ln2_b": nrm(ks[18], (DEPTH, D_MODEL), 0.02),
    }


def reference(x, w_in, ln_v_g, ln_v_b, w_spatial, b_spatial, ln_kidx_g, ln_kidx_b,
              w_branch_a, w_branch_b, w_o, ln1_g, ln1_b, w_up, conv_w, conv_b,
              w_down, ln2_g, ln2_b):
    B, L, _ = x.shape
    h = x
    for i in range(DEPTH):
        proj = h @ w_in[i]
        u, v, q, k, vv, q_idx, k_idx, w_idx, g_a, g_b = jnp.split(proj, SPLIT_POINTS, axis=-1)

        u = jax.nn.gelu(u)
        v = layer_norm(jax.nn.gelu(v), ln_v_g[i], ln_v_b[i])
        a_out = spatial_gating(u, v, w_spatial[i], b_spatial[i])

        q = q.reshape(B, L, N_HEADS_B, HD_B)
        k = k.reshape(B, L, N_HEADS_B, HD_B)
        vv = vv.reshape(B, L, N_HEADS_B, HD_B)
        q_idx = q_idx.reshape(B, L, IDX_HEADS, IDX_DIM)
        k_idx = layer_norm(k_idx, ln_kidx_g[i], ln_kidx_b[i])
        w_idx = w_idx * (IDX_HEADS ** -0.5 * IDX_DIM ** -0.5)
        b_out = dsa_attention(q, k, vv, q_idx, k_idx, w_idx)

        mix = jax.nn.sigmoid(g_a) * (a_out @ w_branch_a[i]) + jax.nn.sigmoid(g_b) * (b_out @ w_branch_b[i])
        y = mix @ w_o[i]
        h = layer_norm(ALPHA * h + y, ln1_g[i], ln1_b[i])

        up = causal_dwconv(h @ w_up[i], conv_w[i], conv_b[i])
        gate, val = jnp.split(up, 2, axis=-1)
        f = (jax.nn.silu(gate) * val) @ w_down[i]
        h = layer_norm(ALPHA * h + f, ln2_g[i], ln2_b[i])
    return h
```

```python
import contextlib
import numpy as np
import concourse.bass as bass
import concourse.mybir as mybir
from concourse.bass_utils import run_bass_kernel_spmd

F32 = mybir.dt.float32
BF16 = mybir.dt.bfloat16
AF = mybir.ActivationFunctionType
ALU = mybir.AluOpType
AX = mybir.AxisListType

NBIS = 14
ALPHA = 2.0 ** 0.25
EPS = 1e-5
NEG = -1.0e30
NOWN = 2080
C_U, C_V, C_Q, C_K, C_VV, C_QI, C_KI, C_WI, C_GA, C_GB = 0, 1024, 2048, 3072, 4096, 5120, 5632, 5696, 5704, 6728
NBLK_MASK = 4 * 136 + 64


class Sched:
    ENG = ("pe", "act", "dve", "pool", "sp")

    def __init__(self, nc):
        self.nc = nc
        self.stack = contextlib.ExitStack()
        self.ops = {e: [] for e in self.ENG}
        self.sems = {}
        self.ecnt = {e: 0 for e in self.ENG}
        self.dcnt = {}
        self.dlast = {}
        self.lastw = {}
        self.readers = {}
        self.waited = {e: {} for e in self.ENG}
        self.scopes = []
        for e in self.ENG:
            self.sems["E" + e] = self.stack.enter_context(nc.semaphore("sem_e_" + e))

    @contextlib.contextmanager
    def scope(self):
        st = contextlib.ExitStack()
        self.scopes.append(st)
        try:
            yield st
        finally:
            self.scopes.pop()
            st.close()

    def sbuf(self, name, shape, dt):
        self.uid = getattr(self, "uid", 0) + 1
        return self.scopes[-1].enter_context(self.nc.sbuf_tensor("s%d_%s" % (self.uid, name), list(shape), dt))

    def psum(self, name, shape, dt=F32):
        self.uid = getattr(self, "uid", 0) + 1
        return self.scopes[-1].enter_context(self.nc.psum_tensor("p%d_%s" % (self.uid, name), list(shape), dt))

    def _dsem(self, key):
        k = "D" + key
        if k not in self.sems:
            self.sems[k] = self.stack.enter_context(self.nc.semaphore("sem_d_" + key))
            self.dcnt[k] = 0
        return k

    def op(self, eng, fn, r=(), w=(), dma=None):
        deps = []
        for b in list(r) + list(w):
            t = self.lastw.get(b)
            if t is not None:
                deps.append(t)
        for b in w:
            deps.extend(self.readers.get(b, ()))
        if dma is not None:
            k = self._dsem(dma)
            prev = self.dlast.get(k)
            if prev is not None:
                deps.append(prev)
            self.dcnt[k] += 16
            tok = (k, self.dcnt[k], None)
            self.dlast[k] = tok
        else:
            self.ecnt[eng] += 1
            tok = ("E" + eng, self.ecnt[eng], eng)
        waits = {}
        for (s, v, e2) in deps:
            if e2 == eng and eng == "pe":
                continue
            if self.waited[eng].get(s, 0) >= v:
                continue
            waits[s] = max(waits.get(s, 0), v)
        for s, v in waits.items():
            self.waited[eng][s] = v
        self.ops[eng].append((sorted(waits.items()), fn, tok))
        for b in w:
            self.lastw[b] = tok
            self.readers[b] = []
        for b in r:
            self.readers.setdefault(b, []).append(tok)
        return tok

    def wait_all(self, eng):
        waits = {}
        for e in self.ENG:
            if self.ecnt[e] == 0 or (e == eng and e == "pe"):
                continue
            if self.waited[eng].get("E" + e, 0) < self.ecnt[e]:
                waits["E" + e] = self.ecnt[e]
        for k, v in self.dcnt.items():
            if v and self.waited[eng].get(k, 0) < v:
                waits[k] = v
        for s, v in waits.items():
            self.waited[eng][s] = v
        self.ops[eng].append((sorted(waits.items()), None, None))

    def flush(self):
        nc = self.nc
        sems = self.sems
        with nc.Block() as block:
            decos = {"pe": block.tensor, "act": block.scalar, "dve": block.vector,
                     "pool": block.gpsimd, "sp": block.sync}
            for e in self.ENG:
                ops = self.ops[e]

                def body(engobj, ops=ops):
                    for waits, fn, tok in ops:
                        for s, v in waits:
                            engobj.wait_ge(sems[s], v)
                        if fn is None:
                            continue
                        ins = fn(engobj)
                        ins.then_inc(sems[tok[0]], 16 if tok[2] is None else 1)

                decos[e](body)
        self.ops = {e: [] for e in self.ENG}

    def barrier(self):
        for e in self.ENG:
            self.wait_all(e)
        self.lastw.clear()
        self.readers.clear()
        self.flush()

    def emit(self):
        self.flush()
        self.stack.close()


class Rot:
    def __init__(self, items):
        self.items = items
        self.i = 0

    def next(self):
        it = self.items[self.i % len(self.items)]
        self.i += 1
        return it


def build_program():
    nc = bass.Bass("TRN2", target_bir_lowering=False)
    S = Sched(nc)

    def din(name, shape):
        return nc.dram_tensor(name, list(shape), F32, kind="ExternalInput").ap()

    xT_all = din("xT_all", [1024, 8192])
    xT_own = din("xT_own", [1024, NOWN])
    xT_prev = din("xT_prev", [1024, 2048])
    w_in = din("w_in", [1024, 7752])
    w_a = din("w_a", [1024, 1024])
    w_b = din("w_b", [1024, 1024])
    w_o = din("w_o", [1024, 1024])
    w_up = din("w_up", [1024, 5632])
    w_dn = din("w_dn", [2816, 1024])
    wsT_d = din("wsT", [128, 8, 128])
    bs_row = din("bs_row", [1, 1024])
    fvec_d = din("fvec", [128, 208])
    bvec_d = din("bvec", [128, 2560])
    ident_d = din("ident", [128, 128])
    trimask_d = din("trimask", [128, 128])
    pen_last_d = din("pen_last", [128, 512])
    pen_halo_d = din("pen_halo", [32, 8192])
    validH_d = din("validH", [128, 32])
    pow2_d = din("pow2", [128, NBIS])
    sel_d = din("sel", [8, 4, 128])
    outT = nc.dram_tensor("outT", [128, 8, 2048], F32, kind="ExternalOutput").ap()

    KT_s = nc.dram_tensor("KT_s", [128, 8, 8192], BF16).ap()
    V_s = nc.dram_tensor("V_s", [128, 64, 8 * 129], BF16).ap()
    qT_s = nc.dram_tensor("qT_s", [128, 8, NOWN], BF16).ap()
    mT_s = nc.dram_tensor("mT_s", [128, NBLK_MASK, 128], BF16).ap()
    aT_s = nc.dram_tensor("aT_s", [128, 8, NOWN], BF16).ap()
    boT_s = nc.dram_tensor("boT_s", [128, 8, NOWN], BF16).ap()
    h1T_s = nc.dram_tensor("h1T_s", [128, 8, NOWN], F32).ap()

    def wview(w, c0, n):
        return w[:, c0:c0 + n].rearrange("(kc p) n -> p kc n", p=128)

    def MM(out, lhsT, rhs, start, stop, r, w, **kw):
        S.op("pe", lambda e: e.matmul(out, lhsT=lhsT, rhs=rhs, start=start, stop=stop, **kw), r=r, w=w)

    def TR(out, in_, idn, r, w):
        S.op("pe", lambda e: e.transpose(out, in_, idn), r=r, w=w)

    def ACT(out, in_, func, r, w, **kw):
        S.op("act", lambda e: e.activation(out=out, in_=in_, func=func, **kw), r=r, w=w)

    def TS(eng, out, in0, s1, s2, op0, op1, r, w, accum=None):
        if accum is not None:
            S.op(eng, lambda e: e.tensor_scalar(out=out, in0=in0, scalar1=s1, scalar2=s2, op0=op0, op1=op1,
                                                accum_out=accum), r=r, w=w)
        elif op1 is None:
            S.op(eng, lambda e: e.tensor_scalar(out=out, in0=in0, scalar1=s1, scalar2=None, op0=op0), r=r, w=w)
        else:
            S.op(eng, lambda e: e.tensor_scalar(out=out, in0=in0, scalar1=s1, scalar2=s2, op0=op0, op1=op1),
                 r=r, w=w)

    def TT(eng, out, in0, in1, op, r, w):
        S.op(eng, lambda e: e.tensor_tensor(out=out, in0=in0, in1=in1, op=op), r=r, w=w)

    def STT(out, in0, scalar, in1, op0, op1, r, w):
        S.op("dve", lambda e: e.scalar_tensor_tensor(out=out, in0=in0, scalar=scalar, in1=in1, op0=op0, op1=op1),
             r=r, w=w)

    def CP(eng, out, in_, r, w):
        if eng == "act":
            S.op("act", lambda e: e.copy(out, in_), r=r, w=w)
        else:
            S.op(eng, lambda e: e.tensor_copy(out, in_), r=r, w=w)

    def DMA(eng, out, in_, key, r, w):
        S.op(eng, lambda e: e.dma_start(out=out, in_=in_), r=r, w=w, dma=key)

    def MEMSET(eng, ap, val, w):
        S.op(eng, lambda e: e.memset(ap, val), r=(), w=w)

    with S.scope():
        ident = S.sbuf("ident", [128, 128], BF16)
        ones_bf = S.sbuf("ones_bf", [128, 128], BF16)
        ones_f = S.sbuf("ones_f", [128, 128], F32)
        fvec = S.sbuf("fvec", [128, 208], F32)
        idx_scope = S.scope()
        idx_scope.__enter__()
        kiT2 = S.sbuf("kiT2", [128, 8192], BF16)
        qiT = S.sbuf("qiT", [128, 4, NOWN], BF16)
        absw = S.sbuf("absw", [128, 17, 8], F32)
        sgn = S.sbuf("sgn", [128, 17, 8], F32)

        DMA("pool", ident[:], ident_d, "c0", [], ["ident"])
        DMA("sp", fvec[:], fvec_d, "c1", [], ["fvec"])
        MEMSET("dve", ones_bf[:], 1.0, ["ones_bf"])
        MEMSET("dve", ones_f[:], 1.0 / 1024.0, ["ones_f"])
        MEMSET("dve", absw[:], 0.0, ["absw"])
        MEMSET("dve", sgn[:], 0.0, ["sgn"])

        bw_scope = S.scope()
        bw_scope.__enter__()
        wu = S.sbuf("wu", [128, 8, 1024], BF16)
        wvm = S.sbuf("wvm", [128, 8, 1024], BF16)
        wq = S.sbuf("wq", [128, 8, 1024], BF16)
        wqi = S.sbuf("wqi", [128, 8, 512], BF16)
        wwi = S.sbuf("wwi", [128, 8, 8], BF16)
        with S.scope():
            wk = S.sbuf("wk", [128, 8, 1024], BF16)
            wv = S.sbuf("wvv", [128, 8, 1024], BF16)
            wki = S.sbuf("wki", [128, 8, 64], BF16)
            kgb = S.sbuf("kgb", [128, 512], F32)
            xbA = [S.sbuf("xbA%d" % i, [128, 8, 512], BF16) for i in range(2)]
            ktA = [S.sbuf("ktA%d" % i, [128, 8, 512], BF16) for i in range(2)]
            vA = [S.sbuf("vA%d" % i, [128, 4, 8, 129], BF16) for i in range(2)]
            stA = S.sbuf("stA", [128, 4, 6], F32)
            mvA = S.sbuf("mvA", [128, 4, 2], F32)
            rsA = S.sbuf("rsA", [128, 4], F32)
            kinf = S.sbuf("kinf", [128, 4, 64], F32)
            kinb = S.sbuf("kinb", [128, 4, 128], BF16)
            psA = [S.psum("psA%d" % i, [128, 512]) for i in range(5)]
            psKi = S.psum("psKi", [128, 4, 64])
            psKt = S.psum("psKt", [128, 512], BF16)

            DMA("pool", xbA[0][:], wview(xT_all, 0, 512), "xA0", [], ["xbA0"])
            DMA("pool", wk[:], wview(w_in, C_K, 1024), "w0", [], ["wk"])
            DMA("pool", wki[:], wview(w_in, C_KI, 64), "w2", [], ["wki"])
            DMA("pool", wv[:], wview(w_in, C_VV, 1024), "w1", [], ["wvv"])
            DMA("sp", kgb[:], bvec_d[:, 2048:2560], "c2", [], ["kgb"])
            for i in range(2):
                MEMSET("dve", vA[i][:, :, :, 128:129], 1.0, ["vA%d" % i])
            prot = Rot([(psA[i], "psA%d" % i) for i in range(5)])
            evi = 0
            for T in range(16):
                xb, xid = xbA[T % 2], "xbA%d" % (T % 2)
                kt, ktid = ktA[T % 2], "ktA%d" % (T % 2)
                vt, vid = vA[T % 2], "vA%d" % (T % 2)
                if T > 0:
                    DMA("pool", xb[:], wview(xT_all, T * 512, 512), "xA%d" % (T % 2), [], [xid])
                if T == 2:
                    DMA("pool", wvm[:], wview(w_in, C_V, 1024), "w3", [], ["wvm"])
                    DMA("pool", wu[:], wview(w_in, C_U, 1024), "w4", [], ["wu"])
                if T == 4:
                    DMA("pool", wq[:], wview(w_in, C_Q, 1024), "w5", [], ["wq"])
                    DMA("pool", wqi[:], wview(w_in, C_QI, 512), "w6", [], ["wqi"])
                    DMA("pool", wwi[:], wview(w_in, C_WI, 8), "w7", [], ["wwi"])
                for h in range(8):
                    ps, pid = prot.next()
                    for kc in range(8):
                        MM(ps[:], wk[:, kc, h * 128:(h + 1) * 128], xb[:, kc, :], kc == 0, kc == 7,
                           ["wk", xid], [pid])
                    CP("act" if evi % 2 == 0 else "dve", kt[:, h, :], ps[:], [pid], [ktid])
                    evi += 1
                DMA("sp", KT_s[:, :, T * 512:(T + 1) * 512], kt[:], "sKT%d" % (T % 2), [ktid], ["KT_s"])
                for blk in range(4):
                    for half in range(2):
                        ps, pid = prot.next()
                        for kc in range(8):
                            MM(ps[:], xb[:, kc, blk * 128:(blk + 1) * 128], wv[:, kc, half * 512:(half + 1) * 512],
                               kc == 0, kc == 7, ["wvv", xid], [pid])
                        CP("act" if evi % 2 == 0 else "dve", vt[:, blk, half * 4:(half + 1) * 4, 0:128],
                           ps[:].rearrange("p (h d) -> p h d", h=4), [pid], [vid])
                        evi += 1
                DMA("sp", V_s[:, T * 4:(T + 1) * 4, :], vt[:].rearrange("p b h d -> p b (h d)"),
                    "sV%d" % (T % 2), [vid], ["V_s"])
                for blk in range(4):
                    for kc in range(8):
                        MM(psKi[:, blk, :], xb[:, kc, blk * 128:(blk + 1) * 128], wki[:, kc, :], kc == 0, kc == 7,
                           ["wki", xid], ["psKi"], skip_group_check=True)
                for blk in range(4):
                    S.op("dve", lambda e, blk=blk: e.bn_stats(stA[:, blk, :], psKi[:, blk, :]), r=["psKi"], w=["stA"])
                    S.op("dve", lambda e, blk=blk: e.bn_aggr(mvA[:, blk, :], stA[:, blk, :]), r=["stA"], w=["mvA"])
                TS("dve", rsA[:], mvA[:, :, 1], EPS, None, ALU.add, None, ["mvA"], ["rsA"])
                S.op("act", lambda e: e.sqrt(rsA[:], rsA[:]), r=["rsA"], w=["rsA"])
                S.op("dve", lambda e: e.reciprocal(rsA[:], rsA[:]), r=["rsA"], w=["rsA"])
                for blk in range(4):
                    TS("dve", kinf[:, blk, :], psKi[:, blk, :], mvA[:, blk, 0:1], rsA[:, blk:blk + 1],
                       ALU.subtract, ALU.mult, ["psKi", "mvA", "rsA"], ["kinf"])
                TT("dve", kinf[:].rearrange("p b f -> p (b f)"), kinf[:].rearrange("p b f -> p (b f)"),
                   kgb[:, 0:256], ALU.mult, ["kinf", "kgb"], ["kinf"])
                for dup in range(2):
                    TT("dve", kinb[:, :, dup * 64:(dup + 1) * 64], kinf[:],
                       kgb[:, 256:512].rearrange("p (b f) -> p b f", b=4), ALU.add, ["kinf", "kgb"], ["kinb"])
                for blk in range(4):
                    TR(psKt[:, blk * 128:(blk + 1) * 128], kinb[:, blk, :], ident[:], ["kinb", "ident"], ["psKt"])
                CP("act", kiT2[:, T * 512:(T + 1) * 512], psKt[:], ["psKt"], ["kiT2"])
            S.barrier()

        with S.scope():
            vgb = S.sbuf("vgb", [128, 2048], F32)
            wsf = S.sbuf("wsf", [128, 8, 128], F32)
            trif = S.sbuf("trif", [128, 128], F32)
            wsb = S.sbuf("wsb", [128, 8, 128], BF16)
            bsb = S.sbuf("bsb", [1, 1024], BF16)
            xbB = [S.sbuf("xbB%d" % i, [128, 8, 512], BF16) for i in range(2)]
            xpB = [S.sbuf("xpB%d" % i, [128, 8, 128], BF16) for i in range(2)]
            uT = [S.sbuf("uT%d" % i, [128, 8, 512], BF16) for i in range(2)]
            uTh = S.sbuf("uTh", [128, 8, 32], BF16)
            aTt = [S.sbuf("aTt%d" % i, [128, 8, 512], BF16) for i in range(2)]
            qTt = [S.sbuf("qTt%d" % i, [128, 8, 512], BF16) for i in range(2)]
            vg = [S.sbuf("vg%d" % i, [128, 1024], F32) for i in range(2)]
            vn = [S.sbuf("vn%d" % i, [128, 1024], BF16) for i in range(2)]
            stB = S.sbuf("stB", [128, 2, 6], F32)
            mvB = S.sbuf("mvB", [128, 2], F32)
            rsB = S.sbuf("rsB", [128, 1], F32)
            psB = [S.psum("psB%d" % i, [128, 512]) for i in range(4)]
            psSp = [S.psum("psSp%d" % i, [128, 4, 128]) for i in range(2)]
            psW = S.psum("psW", [128, 8])
            self_sel = S.sbuf("sel", [8, 4, 128], F32)
            absT = S.sbuf("absT", [8, 512], F32)
            wbc = S.sbuf("wbc", [128, 512], F32)
            DMA("sp", self_sel[:], sel_d, "c5", [], ["sel"])
            psSh = S.psum("psSh", [128, 8, 2])

            DMA("pool", bsb[:], bs_row, "w5", [], ["bsb"])
            DMA("sp", vgb[:], bvec_d[:, 0:2048], "c2", [], ["vgb"])
            DMA("sp", wsf[:], wsT_d, "c3", [], ["wsf"])
            DMA("sp", trif[:], trimask_d, "c4", [], ["trif"])
            for g in range(8):
                TT("dve", wsb[:, g, :], wsf[:, g, :], trif[:], ALU.mult, ["wsf", "trif"], ["wsb"])

            protB = Rot([(psB[i], "psB%d" % i) for i in range(4)])
            vcount = [0]

            def vpath(xsrc, xid):
                i = vcount[0] % 2
                vcount[0] += 1
                vgt, vgid, vnt, vnid = vg[i], "vg%d" % i, vn[i], "vn%d" % i
                for half in range(2):
                    ps, pid = protB.next()
                    for kc in range(8):
                        MM(ps[:], xsrc[:, kc, :], wvm[:, kc, half * 512:(half + 1) * 512], kc == 0, kc == 7,
                           ["wvm", xid], [pid])
                    ACT(vgt[:, half * 512:(half + 1) * 512], ps[:], AF.Gelu_apprx_tanh, [pid], [vgid])
                for half in range(2):
                    S.op("dve", lambda e, half=half: e.bn_stats(stB[:, half, :], vgt[:, half * 512:(half + 1) * 512]),
                         r=[vgid], w=["stB"])
                S.op("dve", lambda e: e.bn_aggr(mvB[:], stB[:].rearrange("p a b -> p (a b)")), r=["stB"], w=["mvB"])
                TS("dve", rsB[:], mvB[:, 1:2], EPS, None, ALU.add, None, ["mvB"], ["rsB"])
                S.op("act", lambda e: e.sqrt(rsB[:], rsB[:]), r=["rsB"], w=["rsB"])
                S.op("dve", lambda e: e.reciprocal(rsB[:], rsB[:]), r=["rsB"], w=["rsB"])
                TS("dve", vgt[:], vgt[:], mvB[:, 0:1], rsB[:, 0:1], ALU.subtract, ALU.mult,
                   [vgid, "mvB", "rsB"], [vgid])
                TT("pool", vgt[:], vgt[:], vgb[:, 0:1024], ALU.mult, [vgid, "vgb"], [vgid])
                TT("pool", vnt[:], vgt[:], vgb[:, 1024:2048], ALU.add, [vgid, "vgb"], [vnid])
                return vnt, vnid

            for T in range(5):
                n = 512 if T < 4 else 32
                tok0 = T * 512
                xb, xid = xbB[T % 2], "xbB%d" % (T % 2)
                DMA("pool", xb[:, :, 0:n], wview(xT_own, tok0, n), "xA%d" % (T % 2), [], [xid])
                if T < 4:
                    ut, uid = uT[T % 2], "uT%d" % (T % 2)
                else:
                    ut, uid = uTh, "uTh"
                at, aid = aTt[T % 2], "aTt%d" % (T % 2)
                qt, qid = qTt[T % 2], "qTt%d" % (T % 2)

                def do_u(cs):
                    for c in cs:
                        ps, pid = protB.next()
                        for kc in range(8):
                            MM(ps[:, 0:n], wu[:, kc, c * 128:(c + 1) * 128], xb[:, kc, 0:n], kc == 0, kc == 7,
                               ["wu", xid], [pid])
                        ACT(ut[:, c, 0:n], ps[:, 0:n], AF.Gelu_apprx_tanh, [pid], [uid])

                def do_q(hs):
                    for h in hs:
                        ps, pid = protB.next()
                        for kc in range(8):
                            MM(ps[:, 0:n], wq[:, kc, h * 128:(h + 1) * 128], xb[:, kc, 0:n], kc == 0, kc == 7,
                               ["wq", xid], [pid])
                        TS("dve", qt[:, h, 0:n], ps[:, 0:n], 128.0 ** -0.5, None, ALU.mult, None, [pid], [qid])
                    if hs[-1] == 7:
                        DMA("sp", qT_s[:, :, tok0:tok0 + n], qt[:, :, 0:n], "sq%d" % (T % 2), [qid], ["qT_s"])

                def do_qi():
                    pw, pwid = protB.next()
                    for kc in range(8):
                        MM(pw[0:8, 0:n], wwi[:, kc, :], xb[:, kc, 0:n], kc == 0, kc == 7, ["wwi", xid], [pwid])
                    ACT(absT[0:8, 0:n], pw[0:8, 0:n], AF.Abs, [pwid], ["absT"], scale=8.0 ** -0.5 * 64.0 ** -0.5)
                    for pr in range(4):
                        ps, pid = protB.next()
                        for kc in range(8):
                            MM(ps[:, 0:n], wqi[:, kc, pr * 128:(pr + 1) * 128], xb[:, kc, 0:n], kc == 0, kc == 7,
                               ["wqi", xid], [pid])
                        pbc, pbcid = protB.next()
                        MM(pbc[:, 0:n], self_sel[0:8, pr, :], absT[0:8, 0:n], True, True, ["sel", "absT"], [pbcid])
                        CP("act", wbc[:, 0:n], pbc[:, 0:n], [pbcid], ["wbc"])
                        TT("dve", qiT[:, pr, tok0:tok0 + n], ps[:, 0:n], wbc[:, 0:n], ALU.mult, [pid, "wbc"], ["qiT"])
                    nb = 4 if T < 4 else 1
                    for blk in range(nb):
                        nt = 128 if T < 4 else 32
                        slot = T * 4 + blk
                        for kc in range(8):
                            MM(psW[0:nt, :], xb[:, kc, blk * 128:blk * 128 + nt], wwi[:, kc, :], kc == 0, kc == 7,
                               ["wwi", xid], ["psW"])
                        ACT(sgn[0:nt, slot, :], psW[0:nt, :], AF.Sign, ["psW"], ["sgn"])

                def spatial(blk, vnt, vnid):
                    for cg in range(2):
                        sp_, spid = psSp[cg], "psSp%d" % cg
                        for cc in range(4):
                            c = cg * 4 + cc
                            MM(sp_[:, cc, :], vnt[:, c * 128:(c + 1) * 128], wsb[:, c, :], True, False,
                               [vnid, "wsb"], [spid], skip_group_check=True)
                            MM(sp_[:, cc, :], ones_bf[0:1, :], bsb[0:1, c * 128:(c + 1) * 128], False, True,
                               ["ones_bf", "bsb"], [spid], skip_group_check=True)
                        TT("dve", at[:, cg * 4:(cg + 1) * 4, blk * 128:(blk + 1) * 128], sp_[:],
                           ut[:, cg * 4:(cg + 1) * 4, blk * 128:(blk + 1) * 128], ALU.mult, [spid, uid], [aid])

                def spatial_h(g, vnt, vnid):
                    for c in range(8):
                        MM(psSh[:, c, :], vnt[:, c * 128:(c + 1) * 128], wsb[:, c, 126:128], True, False,
                           [vnid, "wsb"], ["psSh"], skip_group_check=True)
                        MM(psSh[:, c, :], ones_bf[0:1, :], bsb[0:1, c * 128 + 126:c * 128 + 128], False, True,
                           ["ones_bf", "bsb"], ["psSh"], skip_group_check=True)
                    TT("dve", at[:, :, 2 * g:2 * g + 2], psSh[:], ut[:, :, 2 * g:2 * g + 2], ALU.mult,
                       ["psSh", uid], [aid])

                if T < 4:
                    v0 = vpath(xb[:, :, 0:128], xid)
                    do_u(range(0, 8))
                    v1 = vpath(xb[:, :, 128:256], xid)
                    do_q(range(0, 8))
                    spatial(0, *v0)
                    v2 = vpath(xb[:, :, 256:384], xid)
                    do_qi()
                    spatial(1, *v1)
                    v3 = vpath(xb[:, :, 384:512], xid)
                    spatial(2, *v2)
                    spatial(3, *v3)
                else:
                    def load_prev(g):
                        DMA("pool", xpB[g % 2][:], wview(xT_prev, g * 128, 128), "xP%d" % (g % 2), [],
                            ["xpB%d" % (g % 2)])
                    load_prev(0)
                    load_prev(1)
                    vcur = vpath(xpB[0][:], "xpB0")
                    do_u(range(0, 8))
                    do_q(range(0, 8))
                    do_qi()
                    for g in range(16):
                        vnx = None
                        if g + 1 < 16:
                            vnx = vpath(xpB[(g + 1) % 2][:], "xpB%d" % ((g + 1) % 2))
                            if g + 2 < 16:
                                load_prev(g + 2)
                        spatial_h(g, *vcur)
                        vcur = vnx
                DMA("sp", aT_s[:, :, tok0:tok0 + n], at[:, :, 0:n], "sa%d" % (T % 2), [aid], ["aT_s"])
            S.barrier()
        bw_scope.__exit__(None, None, None)

        slot_nch = [g + 1 for g in range(16)] + [16]
        slot_nt = [128] * 16 + [32]
        slot_blk0 = []
        acc = 0
        for g in range(17):
            slot_blk0.append(acc)
            acc += 4 * slot_nch[g]
        assert acc == NBLK_MASK
        with S.scope():
            Irow = [S.sbuf("Irow%d" % i, [128, 8192], F32) for i in range(2)]
            maskb = [S.sbuf("maskb%d" % i, [128, 8192], BF16) for i in range(2)]
            Rb = [S.sbuf("Rb%d" % i, [128, 2, 512], BF16) for i in range(8)]
            Dm = [S.sbuf("Dm%d" % i, [128, 8, 128], BF16) for i in range(2)]
            identf = S.sbuf("identf", [128, 128], F32)
            penl = S.sbuf("penl", [128, 512], F32)
            penh = [S.sbuf("penh%d" % i, [32, 512], F32) for i in range(2)]
            pow2 = S.sbuf("pow2", [128, NBIS], F32)
            cmin = [S.sbuf("cmin%d" % i, [128, 16], F32) for i in range(2)]
            sm = S.sbuf("sm", [128, 16], F32)
            hwv = S.sbuf("hwv", [128, NBIS], F32)
            cand = [S.sbuf("cand%d" % i, [128, 1], F32) for i in range(2)]
            cnt = S.sbuf("cnt", [128, 1], F32)
            sv = S.sbuf("sv", [128, 1], F32)
            tau = S.sbuf("tau", [128, 1], F32)
            mTo = [S.sbuf("mTo%d" % i, [128, 8, 128], BF16) for i in range(2)]
            zps = [S.psum("zps%d" % i, [128, 2, 512]) for i in range(2)]
            Ips = [S.psum("Ips%d" % i, [128, 512]) for i in range(2)]
            trp = [S.psum("trp%d" % i, [128, 8, 128], BF16) for i in range(2)]

            DMA("sp", identf[:], ident_d, "c2", [], ["identf"])
            DMA("sp", penl[:], pen_last_d, "c3", [], ["penl"])
            DMA("sp", pow2[:], pow2_d, "c5", [], ["pow2"])
            junkA = S.sbuf("junkA", [128, 4096], BF16)
            sB = S.sbuf("sB", [128, 1], F32)
            tmpc = S.sbuf("tmpc", [128, 1], F32)
            rrot = Rot([(Rb[i], "Rb%d" % i) for i in range(8)])
            trc = [0]

            def stageI(g):
                nt, nch = slot_nt[g], slot_nch[g]
                tok0 = g * 128
                Ir, Iid = Irow[g % 2], "Irow%d" % (g % 2)
                dm, dmid = Dm[g % 2], "Dm%d" % (g % 2)
                cm, cmid = cmin[g % 2], "cmin%d" % (g % 2)
                for h in range(8):
                    TS("pool", dm[0:nt, h, 0:nt], identf[0:nt, 0:nt], sgn[0:nt, g, h:h + 1], 1.0, ALU.mult, ALU.mult,
                       ["identf", "sgn"], [dmid])

                def acc(c, rl):
                    ip, ipid = Ips[c % 2], "Ips%d" % (c % 2)
                    for h in range(8):
                        rb, rid = rl[h]
                        MM(ip[0:nt, :], dm[0:nt, h, 0:nt], rb[0:nt, :], h == 0, h == 7, [dmid, rid], [ipid])
                    TS("dve", Ir[0:nt, c * 512:(c + 1) * 512], ip[0:nt, :], 1.0, None, ALU.mult, ALU.min,
                       [ipid], [Iid, cmid], accum=cm[0:nt, c:c + 1])
                    if g == 16:
                        DMA("pool", penh[c % 2][:], pen_halo_d[:, c * 512:(c + 1) * 512], "ph%d" % (c % 2), [],
                            ["penh%d" % (c % 2)])
                        TT("pool", Ir[0:nt, c * 512:(c + 1) * 512], Ir[0:nt, c * 512:(c + 1) * 512],
                           penh[c % 2][0:nt, :], ALU.add, [Iid, "penh%d" % (c % 2)], [Iid])
                    elif c == nch - 1:
                        TT("pool", Ir[0:nt, c * 512:(c + 1) * 512], Ir[0:nt, c * 512:(c + 1) * 512], penl[0:nt, :],
                           ALU.add, [Iid, "penl"], [Iid])

                prev = None
                for c in range(nch):
                    rl = []
                    for pr in range(4):
                        zz, zid = zps[pr % 2], "zps%d" % (pr % 2)
                        MM(zz[0:nt, 0, :], qiT[0:64, pr, tok0:tok0 + nt], kiT2[0:64, c * 512:(c + 1) * 512], True, True,
                           ["qiT", "kiT2"], [zid], skip_group_check=True)
                        MM(zz[0:nt, 1, :], qiT[64:128, pr, tok0:tok0 + nt], kiT2[64:128, c * 512:(c + 1) * 512],
                           True, True, ["qiT", "kiT2"], [zid], tile_position=(64, 0), skip_group_check=True)
                        rb, rid = rrot.next()
                        ACT(rb[0:nt, :, :], zz[0:nt, :, :], AF.Relu, [zid], [rid])
                        rl.append((rb[:, 0, :], rid))
                        rl.append((rb[:, 1, :], rid))
                    if prev is not None:
                        acc(*prev)
                    prev = (c, rl)
                    yield
                acc(*prev)
                yield

            def advance(gen, k):
                if gen is None:
                    return
                for _ in range(k):
                    try:
                        next(gen)
                    except StopIteration:
                        return

            def stageII(g, nxt, nxt_n):
                nt, nch = slot_nt[g], slot_nch[g]
                n = nch * 512
                nA = 128 * max(1, int(round(0.47 * 4 * nch)))
                nD = n - nA
                Ir, Iid = Irow[g % 2], "Irow%d" % (g % 2)
                mk, mkid = maskb[g % 2], "maskb%d" % (g % 2)
                cm, cmid = cmin[g % 2], "cmin%d" % (g % 2)
                S.op("dve", lambda e: e.tensor_reduce(out=sm[0:nt, 0:1], in_=Ir[0:nt, 0:n], axis=AX.X, op=ALU.max),
                     r=[Iid], w=["sm"])
                S.op("dve", lambda e: e.tensor_reduce(out=sm[0:nt, 1:2], in_=cm[0:nt, 0:nch], axis=AX.X, op=ALU.min),
                     r=[cmid], w=["sm"])
                TT("dve", sm[0:nt, 2:3], sm[0:nt, 0:1], sm[0:nt, 1:2], ALU.subtract, ["sm"], ["sm"])
                TS("dve", sm[0:nt, 3:4], sm[0:nt, 2:3], 1.02, 2e-6, ALU.mult, ALU.add, ["sm"], ["sm"])
                TS("dve", hwv[0:nt, :], pow2[0:nt, :], sm[0:nt, 3:4], None, ALU.mult, None, ["pow2", "sm"], ["hwv"])
                TS("dve", sm[0:nt, 4:5], sm[0:nt, 2:3], -0.01, -1e-6, ALU.mult, ALU.add, ["sm"], ["sm"])
                TT("dve", sm[0:nt, 5:6], sm[0:nt, 4:5], sm[0:nt, 1:2], ALU.add, ["sm"], ["sm"])
                TT("dve", cand[0][0:nt, :], sm[0:nt, 5:6], hwv[0:nt, 0:1], ALU.add, ["sm", "hwv"], ["cand0"])
                done = 0
                for k in range(1, NBIS + 1):
                    ci, co = cand[(k - 1) % 2], cand[k % 2]
                    cid, coid = "cand%d" % ((k - 1) % 2), "cand%d" % (k % 2)
                    ACT(junkA[0:nt, 0:nA], Ir[0:nt, nD:n], AF.Sign, [Iid, cid], ["junkA", "sB"],
                        bias=ci[0:nt, 0:1], scale=-1.0, accum_out=sB[0:nt, 0:1])
                    TS("dve", mk[0:nt, 0:nD], Ir[0:nt, 0:nD], ci[0:nt, 0:1], None, ALU.is_ge, ALU.add,
                       [Iid, cid], [mkid, "cnt"], accum=cnt[0:nt, 0:1])
                    STT(tmpc[0:nt, :], cnt[0:nt, :], 2.0, sB[0:nt, :], ALU.mult, ALU.subtract, ["cnt", "sB"], ["tmpc"])
                    last = (k == NBIS)
                    TS("dve", sv[0:nt, :], tmpc[0:nt, :], 511.0 - nA, 1.0 if last else 0.5, ALU.is_ge, ALU.subtract,
                       ["tmpc"], ["sv"])
                    STT((tau if last else co)[0:nt, :], sv[0:nt, :], hwv[0:nt, k - 1:k], ci[0:nt, :], ALU.mult,
                        ALU.add, ["sv", "hwv", cid], ["tau" if last else coid])
                    want = (nxt_n * k) // NBIS
                    advance(nxt, want - done)
                    done = want
                TS("dve", mk[0:nt, 0:n], Ir[0:nt, 0:n], tau[0:nt, 0:1], None, ALU.is_ge, None, [Iid, "tau"], [mkid])
                nsb = n // 128
                for sb0 in range(0, nsb, 8):
                    tc_ = trc[0]
                    tp, tpid = trp[tc_ % 2], "trp%d" % (tc_ % 2)
                    mo, moid = mTo[tc_ % 2], "mTo%d" % (tc_ % 2)
                    nb8 = min(8, nsb - sb0)
                    for q in range(nb8):
                        sb = sb0 + q
                        TR(tp[:, q, 0:nt], mk[0:nt, sb * 128:(sb + 1) * 128], ident[0:nt, 0:nt], [mkid, "ident"], [tpid])
                    CP("act", mo[:, 0:nb8, 0:nt], tp[:, 0:nb8, 0:nt], [tpid], [moid])
                    b0 = slot_blk0[g] + sb0
                    DMA("sp", mT_s[:, b0:b0 + nb8, 0:nt], mo[:, 0:nb8, 0:nt], "sm%d" % (tc_ % 2), [moid], ["mT_s"])
                    trc[0] += 1
                advance(nxt, 1000)

            advance(stageI(0), 1000)
            for g in range(17):
                nxt = stageI(g + 1) if g + 1 < 17 else None
                stageII(g, nxt, (slot_nch[g + 1] + 1) if g + 1 < 17 else 0)
            S.barrier()
        idx_scope.__exit__(None, None, None)

        d1w_scope = S.scope()
        d1w_scope.__enter__()
        wa = S.sbuf("wa", [128, 8, 1024], BF16)
        wb = S.sbuf("wb", [128, 8, 1024], BF16)
        wo = S.sbuf("wo", [128, 8, 1024], BF16)
        for p in range(2):
            with S.scope():
                KT = S.sbuf("KT", [128, 4, 8192], BF16)
                Vv = S.sbuf("Vv", [128, 64, 4, 129], BF16)
                QT = [S.sbuf("QT%d" % i, [128, 4, 128], BF16) for i in range(2)]
                mTi = [S.sbuf("mTi%d" % i, [128, 8, 128], BF16) for i in range(3)]
                Eb = [S.sbuf("Eb%d" % i, [128, 4, 128], BF16) for i in range(3)]
                Pb = [S.sbuf("Pb%d" % i, [128, 4, 128], BF16) for i in range(3)]
                rinv = S.sbuf("rinv", [128, 4], F32)
                bo = [S.sbuf("bo%d" % i, [128, 4, 128], BF16) for i in range(2)]
                boTt = [S.sbuf("boTt%d" % i, [128, 4, 128], BF16) for i in range(2)]
                Sps = [S.psum("Sps%d" % i, [128, 4, 128]) for i in range(3)]
                Ops = [S.psum("Ops%d" % i, [128, 2, 129]) for i in range(4)]
                trb = S.psum("trb", [128, 4, 128], BF16)
                for c in range(16):
                    DMA("pool", KT[:, :, c * 512:(c + 1) * 512], KT_s[:, 4 * p:4 * p + 4, c * 512:(c + 1) * 512],
                        "lk%d" % (c % 4), [], ["KT%d" % c])
                    DMA("pool", Vv[:, 4 * c:4 * c + 4, :, :].rearrange("p b h d -> p b (h d)"),
                        V_s[:, 4 * c:4 * c + 4, 4 * p * 129:(4 * p + 4) * 129], "lv%d" % (c % 4), [], ["Vv%d" % c])
                iters = [(g, sb) for g in range(17) for sb in range(slot_nch[g] * 4)]
                groups = [(g, sb) for (g, sb) in iters if sb % 8 == 0]
                gidx = {gs: i for i, gs in enumerate(groups)}
                LOOK = 2
                deferred = []

                def load_q(g):
                    nt = slot_nt[g]
                    DMA("pool", QT[g % 2][:, :, 0:nt], qT_s[:, 4 * p:4 * p + 4, g * 128:g * 128 + nt],
                        "lq%d" % (g % 2), [], ["QT%d" % (g % 2)])

                def load_m(i):
                    g, sb = groups[i]
                    nt = slot_nt[g]
                    nb8 = min(8, slot_nch[g] * 4 - sb)
                    b0 = slot_blk0[g] + sb
                    DMA("pool", mTi[i % 3][:, 0:nb8, 0:nt], mT_s[:, b0:b0 + nb8, 0:nt], "lm%d" % (i % 3), [],
                        ["mTi%d" % (i % 3)])

                def stage1(it):
                    g, sb = iters[it]
                    nt = slot_nt[g]
                    c = sb // 4
                    if sb == 0:
                        if g == 0:
                            load_q(0)
                        if g + 1 < 17:
                            load_q(g + 1)
                        if p == 1 and g == 9:
                            DMA("pool", wa[:], wview(w_a, 0, 1024), "w0", [], ["wa"])
                            DMA("pool", wb[:], wview(w_b, 0, 1024), "w1", [], ["wb"])
                            DMA("pool", wo[:], wview(w_o, 0, 1024), "w4", [], ["wo"])
                    if sb % 8 == 0:
                        i = gidx[(g, sb)]
                        if i == 0:
                            load_m(0)
                        if i + 1 < len(groups):
                            load_m(i + 1)
                    i = gidx[(g, sb - sb % 8)]
                    mt, mtid = mTi[i % 3], "mTi%d" % (i % 3)
                    qt, qid = QT[g % 2], "QT%d" % (g % 2)
                    sp_, spid = Sps[it % 3], "Sps%d" % (it % 3)
                    eb, ebid = Eb[it % 3], "Eb%d" % (it % 3)
                    pb, pbid = Pb[it % 3], "Pb%d" % (it % 3)
                    for h in range(4):
                        MM(sp_[:, h, 0:nt], KT[:, h, sb * 128:(sb + 1) * 128], qt[:, h, 0:nt], True, True,
                           ["KT%d" % c, qid], [spid], skip_group_check=True)
                    ACT(eb[:, :, 0:nt], sp_[:, :, 0:nt], AF.Exp, [spid], [ebid])
                    TT("dve", pb[:, :, 0:nt], eb[:, :, 0:nt],
                       mt[:, sb % 8, 0:nt].unsqueeze(1).to_broadcast([128, 4, nt]), ALU.mult, [ebid, mtid], [pbid])

                def stage2(it):
                    g, sb = iters[it]
                    nt = slot_nt[g]
                    nsb = slot_nch[g] * 4
                    c = sb // 4
                    pb, pbid = Pb[it % 3], "Pb%d" % (it % 3)
                    o0, o0id = Ops[(g % 2) * 2], "Ops%d" % ((g % 2) * 2)
                    o1, o1id = Ops[(g % 2) * 2 + 1], "Ops%d" % ((g % 2) * 2 + 1)
                    for h in range(4):
                        ob, obid = (o0, o0id) if h < 2 else (o1, o1id)
                        MM(ob[0:nt, h % 2, :], pb[:, h, 0:nt], Vv[:, sb, h, :], sb == 0 and h % 2 == 0,
                           sb == nsb - 1, [pbid, "Vv%d" % c], [obid], skip_group_check=True)
                    if sb == nsb - 1:
                        bt, btid = bo[g % 2], "bo%d" % (g % 2)
                        btT, btTid = boTt[g % 2], "boTt%d" % (g % 2)
                        S.op("dve", lambda e: e.reciprocal(rinv[0:nt, 0:2], o0[0:nt, :, 128]), r=[o0id], w=["rinv"])
                        S.op("dve", lambda e: e.reciprocal(rinv[0:nt, 2:4], o1[0:nt, :, 128]), r=[o1id], w=["rinv"])
                        for h in range(4):
                            ob, obid = (o0, o0id) if h < 2 else (o1, o1id)
                            TS("dve", bt[0:nt, h, :], ob[0:nt, h % 2, 0:128], rinv[0:nt, h:h + 1], None, ALU.mult,
                               None, [obid, "rinv"], [btid])

                        def fin():
                            for h in range(4):
                                TR(trb[:, h, 0:nt], bt[0:nt, h, :], ident[0:nt, 0:nt], [btid, "ident"], ["trb"])
                            CP("act", btT[:, :, 0:nt], trb[:, :, 0:nt], ["trb"], [btTid])
                            DMA("sp", boT_s[:, 4 * p:4 * p + 4, g * 128:g * 128 + nt], btT[:, :, 0:nt],
                                "sb%d" % (g % 2), [btTid], ["boT_s"])
                        deferred.append((it + 6, fin))

                for k in range(len(iters) + LOOK):
                    if k < len(iters):
                        stage1(k)
                    if k - LOOK >= 0:
                        stage2(k - LOOK)
                    while deferred and deferred[0][0] <= k - LOOK:
                        deferred.pop(0)[1]()
                while deferred:
                    deferred.pop(0)[1]()
                S.barrier()

        def layernorm_fm(z, zid, n, gcol, bcol, outt, outid, sq, sqid, psm, psq, tmp):
            mean_sb, m2, rstd = tmp
            for kc in range(8):
                ACT(sq[:, kc, 0:n], z[:, kc, 0:n], AF.Square, [zid], [sqid])
            for kc in range(8):
                MM(psm[:, 0:n], ones_f[:], z[:, kc, 0:n], kc == 0, kc == 7, ["ones_f", zid], ["psm"])
            for kc in range(8):
                MM(psq[:, 0:n], ones_f[:], sq[:, kc, 0:n], kc == 0, kc == 7, ["ones_f", sqid], ["psq"])
            CP("act", mean_sb[:, 0:n], psm[:, 0:n], ["psm"], ["mean_sb"])
            TT("pool", m2[:, 0:n], mean_sb[:, 0:n], mean_sb[:, 0:n], ALU.mult, ["mean_sb"], ["m2"])
            TT("dve", rstd[:, 0:n], psq[:, 0:n], m2[:, 0:n], ALU.subtract, ["psq", "m2"], ["rstd"])
            TS("dve", rstd[:, 0:n], rstd[:, 0:n], 0.0, EPS, ALU.max, ALU.add, ["rstd"], ["rstd"])
            S.op("act", lambda e: e.sqrt(rstd[:, 0:n], rstd[:, 0:n]), r=["rstd"], w=["rstd"])
            S.op("dve", lambda e: e.reciprocal(rstd[:, 0:n], rstd[:, 0:n]), r=["rstd"], w=["rstd"])
            for kc in range(8):
                eng = "dve" if kc % 2 == 0 else "pool"
                TT(eng, sq[:, kc, 0:n], z[:, kc, 0:n], mean_sb[:, 0:n], ALU.subtract, [zid, "mean_sb"], [sqid])
                TT(eng, sq[:, kc, 0:n], sq[:, kc, 0:n], rstd[:, 0:n], ALU.mult, [sqid, "rstd"], [sqid])
                TS(eng, outt[:, kc, 0:n], sq[:, kc, 0:n], fvec[:, gcol + kc:gcol + kc + 1],
                   fvec[:, bcol + kc:bcol + kc + 1], ALU.mult, ALU.add, [sqid, "fvec"], [outid])

        with S.scope():
            wga = S.sbuf("wga", [128, 8, 1024], BF16)
            wgb = S.sbuf("wgb", [128, 8, 1024], BF16)
            aTi = [S.sbuf("aTi%d" % i, [128, 8, 512], BF16) for i in range(2)]
            bTi = [S.sbuf("bTi%d" % i, [128, 8, 512], BF16) for i in range(2)]
            xbD = [S.sbuf("xbD%d" % i, [128, 8, 512], BF16) for i in range(2)]
            xfD = [S.sbuf("xfD0", [128, 8, 512], F32)] * 2
            sga2 = [S.sbuf("sga%d" % i, [128, 512], F32) for i in range(2)]
            sgb2 = [S.sbuf("sgb%d" % i, [128, 512], F32) for i in range(2)]
            t12 = [S.sbuf("t1%d" % i, [128, 512], F32) for i in range(2)]
            t22 = [S.sbuf("t2%d" % i, [128, 512], F32) for i in range(2)]
            mixT = S.sbuf("mixT", [128, 8, 512], BF16)
            z1 = S.sbuf("z1", [128, 8, 512], F32)
            sq1 = S.sbuf("sq1", [128, 8, 512], F32)
            mean_sb = S.sbuf("mean_sb", [128, 512], F32)
            m2 = S.sbuf("m2", [128, 512], F32)
            rstd = S.sbuf("rstd", [128, 512], F32)
            psD = [S.psum("psD%d" % i, [128, 512]) for i in range(6)]
            psm = S.psum("psm", [128, 512])
            psq = S.psum("psq", [128, 512])
            DMA("pool", wga[:], wview(w_in, C_GA, 1024), "w2", [], ["wga"])
            DMA("pool", wgb[:], wview(w_in, C_GB, 1024), "w3", [], ["wgb"])
            protD = Rot([(psD[i], "psD%d" % i) for i in range(6)])
            pending1 = None
            for T in range(5):
                n = 512 if T < 4 else 32
                tok0 = T * 512
                i2 = T % 2
                DMA("pool", aTi[i2][:, :, 0:n], aT_s[:, :, tok0:tok0 + n], "la%d" % i2, [], ["aTi%d" % i2])
                DMA("pool", bTi[i2][:, :, 0:n], boT_s[:, :, tok0:tok0 + n], "lb%d" % i2, [], ["bTi%d" % i2])
                DMA("pool", xbD[i2][:, :, 0:n], wview(xT_own, tok0, n), "xA%d" % i2, [], ["xbD%d" % i2])
                DMA("sp", xfD[i2][:, :, 0:n], wview(xT_own, tok0, n), "xF0", [], ["xfD0"])
                for m in range(8):
                    pA, pAid = protD.next()
                    pBb, pBid = protD.next()
                    pGA, pGAid = protD.next()
                    pGB, pGBid = protD.next()
                    for kc in range(8):
                        MM(pA[:, 0:n], wa[:, kc, m * 128:(m + 1) * 128], aTi[i2][:, kc, 0:n], kc == 0, kc == 7,
                           ["wa", "aTi%d" % i2], [pAid])
                    for kc in range(8):
                        MM(pBb[:, 0:n], wb[:, kc, m * 128:(m + 1) * 128], bTi[i2][:, kc, 0:n], kc == 0, kc == 7,
                           ["wb", "bTi%d" % i2], [pBid])
                    for kc in range(8):
                        MM(pGA[:, 0:n], wga[:, kc, m * 128:(m + 1) * 128], xbD[i2][:, kc, 0:n], kc == 0, kc == 7,
                           ["wga", "xbD%d" % i2], [pGAid])
                    for kc in range(8):
                        MM(pGB[:, 0:n], wgb[:, kc, m * 128:(m + 1) * 128], xbD[i2][:, kc, 0:n], kc == 0, kc == 7,
                           ["wgb", "xbD%d" % i2], [pGBid])
                    sga, sgb, t1, t2 = sga2[m % 2], sgb2[m % 2], t12[m % 2], t22[m % 2]
                    sfx = str(m % 2)
                    ACT(sga[:, 0:n], pGA[:, 0:n], AF.Sigmoid, [pGAid], ["sga" + sfx])
                    ACT(sgb[:, 0:n], pGB[:, 0:n], AF.Sigmoid, [pGBid], ["sgb" + sfx])
                    TT("dve", t1[:, 0:n], pA[:, 0:n], sga[:, 0:n], ALU.mult, [pAid, "sga" + sfx], ["t1" + sfx])
                    TT("dve", t2[:, 0:n], pBb[:, 0:n], sgb[:, 0:n], ALU.mult, [pBid, "sgb" + sfx], ["t2" + sfx])
                    TT("pool", mixT[:, m, 0:n], t1[:, 0:n], t2[:, 0:n], ALU.add, ["t1" + sfx, "t2" + sfx], ["mixT"])
                    if m == 1 and pending1 is not None:
                        pending1()
                        pending1 = None
                for nn in range(8):
                    pY, pYid = protD.next()
                    for kc in range(8):
                        MM(pY[:, 0:n], wo[:, kc, nn * 128:(nn + 1) * 128], mixT[:, kc, 0:n], kc == 0, kc == 7,
                           ["wo", "mixT"], [pYid])
                    STT(z1[:, nn, 0:n], xfD[i2][:, nn, 0:n], ALPHA, pY[:, 0:n], ALU.mult, ALU.add,
                        ["xfD0", pYid], ["z1"])

                def fin1(n=n, tok0=tok0):
                    layernorm_fm(z1, "z1", n, 0, 8, sq1, "sq1", sq1, "sq1", psm, psq, (mean_sb, m2, rstd))
                    DMA("sp", h1T_s[:, :, tok0:tok0 + n], sq1[:, :, 0:n], "sh0", ["sq1"], ["h1T_s"])
                pending1 = fin1
            pending1()
            S.barrier()
        d1w_scope.__exit__(None, None, None)

        with S.scope():
            TW = 256
            wup = S.sbuf("wup", [128, 8, 5632], BF16)
            wdn = S.sbuf("wdn", [128, 22, 1024], BF16)
            upH = S.sbuf("upH", [128, 44, 32], F32)
            validH = S.sbuf("validH", [128, 32], F32)
            h1f2 = [S.sbuf("h1f%d" % i, [128, 8, TW], F32) for i in range(2)]
            h1b2 = [S.sbuf("h1b%d" % i, [128, 8, TW], BF16) for i in range(2)]
            h1f, h1b = h1f2[1], h1b2[1]
            ext = [S.sbuf("ext%d" % i, [128, 2, 130], F32) for i in range(4)]
            yg2 = [S.sbuf("yg%d" % i, [128, 2, 128], F32) for i in range(2)]
            yv2 = [S.sbuf("yv%d" % i, [128, 2, 128], F32) for i in range(2)]
            sgt2 = [S.sbuf("sgt%d" % i, [128, 2, 128], F32) for i in range(2)]
            aF = S.sbuf("aF", [128, 22, TW], BF16)
            sq2 = S.sbuf("sq2", [128, 8, TW], F32)
            mean_sb = S.sbuf("mean_sb2", [128, TW], F32)
            m2 = S.sbuf("m2b", [128, TW], F32)
            rstd = S.sbuf("rstd2", [128, TW], F32)
            psU = [S.psum("psU%d" % i, [128, 512]) for i in range(4)]
            psF = [S.psum("psF%d" % i, [128, 512]) for i in range(2)]
            psm = S.psum("psm2", [128, 512])
            psq = S.psum("psq2", [128, 512])
            for q4 in (0, 2, 1, 3):
                DMA("pool", wup[:, :, q4 * 1408:(q4 + 1) * 1408], wview(w_up, q4 * 1408, 1408), "w%d" % q4, [],
                    ["wup%d" % q4])

            def wupid(ch):
                return "wup%d" % (ch // 11)
            DMA("pool", wdn[:], w_dn.rearrange("(kc p) n -> p kc n", p=128), "w4", [], ["wdn"])
            DMA("sp", validH[:], validH_d, "c2", [], ["validH"])
            protU = Rot([(psU[i], "psU%d" % i) for i in range(4)])
            erot = Rot([(ext[i], "ext%d" % i) for i in range(4)])
            DMA("sp", h1f[:, :, 0:32], h1T_s[:, :, 2048:2080], "lh1", [], ["h1f1"])
            CP("dve", h1b[:, :, 0:32], h1f[:, :, 0:32], ["h1f1"], ["h1b1"])

            def halo_chunk(ch):
                ps, pid = protU.next()
                for kc in range(8):
                    MM(ps[:, 0:32], wup[:, kc, ch * 128:(ch + 1) * 128], h1b[:, kc, 0:32], kc == 0, kc == 7,
                       [wupid(ch), "h1b1"], [pid])
                TT("dve", upH[:, ch, :], ps[:, 0:32], validH[:], ALU.mult, [pid, "validH"], ["upH%d" % ch])
            halo_chunk(0)
            halo_chunk(22)
            NT2 = 2048 // TW

            def load_h1(T):
                DMA("sp", h1f2[T % 2][:], h1T_s[:, :, T * TW:(T + 1) * TW], "lh%d" % (T % 2), [], ["h1f%d" % (T % 2)])

            load_h1(0)
            pending = None
            for T in range(NT2):
                hf, hfid = h1f2[T % 2], "h1f%d" % (T % 2)
                hb, hbid = h1b2[T % 2], "h1b%d" % (T % 2)
                for kc in range(8):
                    CP("pool" if kc % 2 else "dve", hb[:, kc, :], hf[:, kc, :], [hfid], [hbid])
                for i in range(22):
                    yg, yv, sgt = yg2[i % 2], yv2[i % 2], sgt2[i % 2]
                    sfx = str(i % 2)
                    if T == 0 and i + 1 < 22:
                        halo_chunk(i + 1)
                        halo_chunk(22 + i + 1)
                    for which, ytile, yid in ((0, yg, "yg" + sfx), (1, yv, "yv" + sfx)):
                        ch = i + 22 * which
                        ps, pid = protU.next()
                        for kc in range(8):
                            MM(ps[:, 0:TW], wup[:, kc, ch * 128:(ch + 1) * 128], hb[:, kc, :], kc == 0, kc == 7,
                               [wupid(ch), hbid], [pid])
                        ex, exid = erot.next()
                        psv = ps[:, 0:TW].rearrange("p (b t) -> p b t", b=2)
                        CP("act", ex[:, :, 2:130], psv, [pid], [exid])
                        ACT(ytile[:], psv, AF.Identity, [pid, "fvec"], [yid], scale=fvec[:, 120 + ch:121 + ch],
                            bias=fvec[:, 164 + ch:165 + ch])
                        CP("pool", ex[:, :, 0:2], upH[:, ch, 4 * T:4 * T + 4].rearrange("p (b t) -> p b t", b=2),
                           ["upH%d" % ch], [exid])
                        STT(ytile[:], ex[:, :, 1:129], fvec[:, 76 + ch:77 + ch], ytile[:], ALU.mult, ALU.add,
                            [exid, "fvec", yid], [yid])
                        STT(ytile[:], ex[:, :, 0:128], fvec[:, 32 + ch:33 + ch], ytile[:], ALU.mult, ALU.add,
                            [exid, "fvec", yid], [yid])
                    ACT(sgt[:], yg[:], AF.Silu, ["yg" + sfx], ["sgt" + sfx])
                    TT("pool", aF[:, i, :].rearrange("p (b t) -> p b t", b=2), sgt[:], yv[:], ALU.mult,
                       ["sgt" + sfx, "yv" + sfx], ["aF"])
                    if i == 2 and pending is not None:
                        pending()
                        pending = None
                        if T + 1 < NT2:
                            load_h1(T + 1)
                if T == 0 and NT2 > 1:
                    load_h1(1)
                for nn in range(8):
                    pf, pfid = psF[nn % 2], "psF%d" % (nn % 2)
                    for kc in range(22):
                        MM(pf[:, 0:TW], wdn[:, kc, nn * 128:(nn + 1) * 128], aF[:, kc, :], kc == 0, kc == 21,
                           ["wdn", "aF"], [pfid])
                    STT(hf[:, nn, :], hf[:, nn, :], ALPHA, pf[:, 0:TW], ALU.mult, ALU.add, [hfid, pfid], [hfid])

                def fin(T=T, hf=hf, hfid=hfid):
                    layernorm_fm(hf, hfid, TW, 16, 24, sq2, "sq2", sq2, "sq2", psm, psq, (mean_sb, m2, rstd))
                    DMA("sp", outT[:, :, T * TW:(T + 1) * TW], sq2[:], "so0", ["sq2"], ["outT"])
                pending = fin
            pending()
            S.barrier()
        S.wait_all("sp")
        S.emit()
    return nc


_NC_CACHE = {}


def _host_inputs(x, w_in, ln_v_g, ln_v_b, w_spatial, b_spatial, ln_kidx_g, ln_kidx_b,
                 w_branch_a, w_branch_b, w_o, ln1_g, ln1_b, w_up, conv_w, conv_b, w_down, ln2_g, ln2_b):
    f32 = np.float32
    x = np.asarray(x, f32)

    def fm(v, kc):
        return np.asarray(v, f32).reshape(kc, 128).T

    fvec = np.zeros((128, 208), f32)
    fvec[:, 0:8] = fm(ln1_g[0], 8)
    fvec[:, 8:16] = fm(ln1_b[0], 8)
    fvec[:, 16:24] = fm(ln2_g[0], 8)
    fvec[:, 24:32] = fm(ln2_b[0], 8)
    for j in range(3):
        fvec[:, 32 + 44 * j:76 + 44 * j] = fm(conv_w[0, j], 44)
    fvec[:, 164:208] = fm(conv_b[0], 44)
    bvec = np.zeros((128, 2560), f32)
    bvec[:, 0:1024] = np.asarray(ln_v_g[0], f32)[None, :]
    bvec[:, 1024:2048] = np.asarray(ln_v_b[0], f32)[None, :]
    bvec[:, 2048:2304] = np.tile(np.asarray(ln_kidx_g[0], f32), 4)[None, :]
    bvec[:, 2304:2560] = np.tile(np.asarray(ln_kidx_b[0], f32), 4)[None, :]
    wsT = np.ascontiguousarray(np.transpose(np.asarray(w_spatial[0], f32), (2, 0, 1)))
    s_i = np.arange(128)
    trimask = (s_i[:, None] <= s_i[None, :]).astype(f32)
    common = {
        "w_in": np.ascontiguousarray(np.asarray(w_in[0], f32)),
        "w_a": np.ascontiguousarray(np.asarray(w_branch_a[0], f32)),
        "w_b": np.ascontiguousarray(np.asarray(w_branch_b[0], f32)),
        "w_o": np.ascontiguousarray(np.asarray(w_o[0], f32)),
        "w_up": np.ascontiguousarray(np.asarray(w_up[0], f32)),
        "w_dn": np.ascontiguousarray(np.asarray(w_down[0], f32)),
        "wsT": wsT,
        "bs_row": np.ascontiguousarray(np.asarray(b_spatial[0], f32).reshape(1, 1024)),
        "fvec": fvec, "bvec": bvec,
        "ident": np.eye(128, dtype=f32),
        "sel": np.stack([np.stack([(np.arange(128) // 64 + 2 * pr == h).astype(f32) for pr in range(4)])
                         for h in range(8)]),
        "trimask": trimask,
        "pow2": np.tile((2.0 ** -np.arange(1, NBIS + 1)).astype(f32)[None, :], (128, 1)),
    }
    xT = [np.ascontiguousarray(x[b].T) for b in range(2)]
    in_maps = []
    for c in range(8):
        b, j = c // 4, c % 4
        own_cols = []
        for g in range(16):
            t0 = (4 * g + j) * 128
            own_cols.append(np.arange(t0, t0 + 128))
        halo_pos = []
        for g in range(16):
            t0 = (4 * g + j) * 128
            halo_pos += [t0 - 2, t0 - 1]
        halo_pos = np.array(halo_pos)
        own_idx = np.concatenate(own_cols)
        xT_own = np.zeros((1024, NOWN), f32)
        xT_own[:, 0:2048] = xT[b][:, own_idx]
        hv = halo_pos >= 0
        xT_own[:, 2048:][:, hv] = xT[b][:, halo_pos[hv]]
        xT_prev = np.zeros((1024, 2048), f32)
        for g in range(16):
            pb = 4 * g + j - 1
            if pb >= 0:
                xT_prev[:, g * 128:(g + 1) * 128] = xT[b][:, pb * 128:(pb + 1) * 128]
        kk = np.arange(512)
        rblk, ks = kk // 128, kk % 128
        t = np.arange(128)
        allowed = (rblk[None, :] < j) | ((rblk[None, :] == j) & (ks[None, :] <= t[:, None]))
        pen_last = np.where(allowed, 0.0, NEG).astype(f32)
        sidx = np.arange(8192)
        hp = np.where(halo_pos >= 0, halo_pos, 0)
        pen_halo = np.where(sidx[None, :] <= hp[:, None], 0.0, NEG).astype(f32)
        validH = np.tile(hv.astype(f32)[None, :], (128, 1))
        m = dict(common)
        m.update({"xT_all": xT[b], "xT_own": xT_own, "xT_prev": xT_prev, "pen_last": pen_last,
                  "pen_halo": pen_halo, "validH": validH})
        in_maps.append(m)
    return in_maps


def kernel(**inputs):
    in_maps = _host_inputs(**inputs)
    if "nc" not in _NC_CACHE:
        _NC_CACHE["nc"] = build_program()
    nc = _NC_CACHE["nc"]
    res = run_bass_kernel_spmd(nc, in_maps, core_ids=list(range(8)))
    out = np.zeros((2, 8192, 1024), np.float32)
    for c in range(8):
        b, j = c // 4, c % 4
        o = np.asarray(res.results[c]["outT"], np.float32)
        o = o.transpose(2, 1, 0).reshape(2048, 1024)
        for g in range(16):
            t0 = (4 * g + j) * 128
            out[b, t0:t0 + 128, :] = o[g * 128:(g + 1) * 128, :]
    return out
```

```python
import contextlib
import numpy as np
import concourse.bass as bass
import concourse.mybir as mybir
from concourse.bass_utils import run_bass_kernel_spmd

F32 = mybir.dt.float32
BF16 = mybir.dt.bfloat16
AF = mybir.ActivationFunctionType
ALU = mybir.AluOpType
AX = mybir.AxisListType

NBIS = 14
ALPHA = 2.0 ** 0.25
EPS = 1e-5
NEG = -1.0e30
NOWN = 2080
C_U, C_V, C_Q, C_K, C_VV, C_QI, C_KI, C_WI, C_GA, C_GB = 0, 1024, 2048, 3072, 4096, 5120, 5632, 5696, 5704, 6728
NBLK_MASK = 4 * 136 + 64


class Sched:
    ENG = ("pe", "act", "dve", "pool", "sp")

    def __init__(self, nc):
        self.nc = nc
        self.stack = contextlib.ExitStack()
        self.ops = {e: [] for e in self.ENG}
        self.sems = {}
        self.ecnt = {e: 0 for e in self.ENG}
        self.dcnt = {}
        self.dlast = {}
        self.lastw = {}
        self.readers = {}
        self.waited = {e: {} for e in self.ENG}
        self.scopes = []
        for e in self.ENG:
            self.sems["E" + e] = self.stack.enter_context(nc.semaphore("sem_e_" + e))

    @contextlib.contextmanager
    def scope(self):
        st = contextlib.ExitStack()
        self.scopes.append(st)
        try:
            yield st
        finally:
            self.scopes.pop()
            st.close()

    def sbuf(self, name, shape, dt):
        self.uid = getattr(self, "uid", 0) + 1
        return self.scopes[-1].enter_context(self.nc.sbuf_tensor("s%d_%s" % (self.uid, name), list(shape), dt))

    def psum(self, name, shape, dt=F32):
        self.uid = getattr(self, "uid", 0) + 1
        return self.scopes[-1].enter_context(self.nc.psum_tensor("p%d_%s" % (self.uid, name), list(shape), dt))

    def _dsem(self, key):
        k = "D" + key
        if k not in self.sems:
            self.sems[k] = self.stack.enter_context(self.nc.semaphore("sem_d_" + key))
            self.dcnt[k] = 0
        return k

    def op(self, eng, fn, r=(), w=(), dma=None):
        deps = []
        for b in list(r) + list(w):
            t = self.lastw.get(b)
            if t is not None:
                deps.append(t)
        for b in w:
            deps.extend(self.readers.get(b, ()))
        if dma is not None:
            k = self._dsem(dma)
            prev = self.dlast.get(k)
            if prev is not None:
                deps.append(prev)
            self.dcnt[k] += 16
            tok = (k, self.dcnt[k], None)
            self.dlast[k] = tok
        else:
            self.ecnt[eng] += 1
            tok = ("E" + eng, self.ecnt[eng], eng)
        waits = {}
        for (s, v, e2) in deps:
            if e2 == eng and eng == "pe":
                continue
            if self.waited[eng].get(s, 0) >= v:
                continue
            waits[s] = max(waits.get(s, 0), v)
        for s, v in waits.items():
            self.waited[eng][s] = v
        self.ops[eng].append((sorted(waits.items()), fn, tok))
        for b in w:
            self.lastw[b] = tok
            self.readers[b] = []
        for b in r:
            self.readers.setdefault(b, []).append(tok)
        return tok

    def wait_all(self, eng):
        waits = {}
        for e in self.ENG:
            if self.ecnt[e] == 0 or (e == eng and e == "pe"):
                continue
            if self.waited[eng].get("E" + e, 0) < self.ecnt[e]:
                waits["E" + e] = self.ecnt[e]
        for k, v in self.dcnt.items():
            if v and self.waited[eng].get(k, 0) < v:
                waits[k] = v
        for s, v in waits.items():
            self.waited[eng][s] = v
        self.ops[eng].append((sorted(waits.items()), None, None))

    def flush(self):
        nc = self.nc
        sems = self.sems
        with nc.Block() as block:
            decos = {"pe": block.tensor, "act": block.scalar, "dve": block.vector,
                     "pool": block.gpsimd, "sp": block.sync}
            for e in self.ENG:
                ops = self.ops[e]

                def body(engobj, ops=ops):
                    for waits, fn, tok in ops:
                        for s, v in waits:
                            engobj.wait_ge(sems[s], v)
                        if fn is None:
                            continue
                        ins = fn(engobj)
                        ins.then_inc(sems[tok[0]], 16 if tok[2] is None else 1)

                decos[e](body)
        self.ops = {e: [] for e in self.ENG}

    def barrier(self):
        for e in self.ENG:
            self.wait_all(e)
        self.lastw.clear()
        self.readers.clear()
        self.flush()

    def emit(self):
        self.flush()
        self.stack.close()


class Rot:
    def __init__(self, items):
        self.items = items
        self.i = 0

    def next(self):
        it = self.items[self.i % len(self.items)]
        self.i += 1
        return it


def build_program():
    nc = bass.Bass("TRN2", target_bir_lowering=False)
    S = Sched(nc)

    def din(name, shape):
        return nc.dram_tensor(name, list(shape), F32, kind="ExternalInput").ap()

    xT_all = din("xT_all", [1024, 8192])
    xT_own = din("xT_own", [1024, NOWN])
    xT_prev = din("xT_prev", [1024, 2048])
    w_in = din("w_in", [1024, 7752])
    w_a = din("w_a", [1024, 1024])
    w_b = din("w_b", [1024, 1024])
    w_o = din("w_o", [1024, 1024])
    w_up = din("w_up", [1024, 5632])
    w_dn = din("w_dn", [2816, 1024])
    wsT_d = din("wsT", [128, 8, 128])
    bs_row = din("bs_row", [1, 1024])
    fvec_d = din("fvec", [128, 208])
    bvec_d = din("bvec", [128, 2560])
    ident_d = din("ident", [128, 128])
    trimask_d = din("trimask", [128, 128])
    pen_last_d = din("pen_last", [128, 512])
    pen_halo_d = din("pen_halo", [32, 8192])
    validH_d = din("validH", [128, 32])
    pow2_d = din("pow2", [128, NBIS])
    sel_d = din("sel", [8, 4, 128])
    outT = nc.dram_tensor("outT", [128, 8, 2048], F32, kind="ExternalOutput").ap()

    KT_s = nc.dram_tensor("KT_s", [128, 8, 8192], BF16).ap()
    V_s = nc.dram_tensor("V_s", [128, 64, 8 * 129], BF16).ap()
    qT_s = nc.dram_tensor("qT_s", [128, 8, NOWN], BF16).ap()
    mT_s = nc.dram_tensor("mT_s", [128, NBLK_MASK, 128], BF16).ap()
    aT_s = nc.dram_tensor("aT_s", [128, 8, NOWN], BF16).ap()
    boT_s = nc.dram_tensor("boT_s", [128, 8, NOWN], BF16).ap()
    h1T_s = nc.dram_tensor("h1T_s", [128, 8, NOWN], F32).ap()

    def wview(w, c0, n):
        return w[:, c0:c0 + n].rearrange("(kc p) n -> p kc n", p=128)

    def MM(out, lhsT, rhs, start, stop, r, w, **kw):
        S.op("pe", lambda e: e.matmul(out, lhsT=lhsT, rhs=rhs, start=start, stop=stop, **kw), r=r, w=w)

    def TR(out, in_, idn, r, w):
        S.op("pe", lambda e: e.transpose(out, in_, idn), r=r, w=w)

    def ACT(out, in_, func, r, w, **kw):
        S.op("act", lambda e: e.activation(out=out, in_=in_, func=func, **kw), r=r, w=w)

    def TS(eng, out, in0, s1, s2, op0, op1, r, w, accum=None):
        if accum is not None:
            S.op(eng, lambda e: e.tensor_scalar(out=out, in0=in0, scalar1=s1, scalar2=s2, op0=op0, op1=op1,
                                                accum_out=accum), r=r, w=w)
        elif op1 is None:
            S.op(eng, lambda e: e.tensor_scalar(out=out, in0=in0, scalar1=s1, scalar2=None, op0=op0), r=r, w=w)
        else:
            S.op(eng, lambda e: e.tensor_scalar(out=out, in0=in0, scalar1=s1, scalar2=s2, op0=op0, op1=op1),
                 r=r, w=w)

    def TT(eng, out, in0, in1, op, r, w):
        S.op(eng, lambda e: e.tensor_tensor(out=out, in0=in0, in1=in1, op=op), r=r, w=w)

    def STT(out, in0, scalar, in1, op0, op1, r, w):
        S.op("dve", lambda e: e.scalar_tensor_tensor(out=out, in0=in0, scalar=scalar, in1=in1, op0=op0, op1=op1),
             r=r, w=w)

    def CP(eng, out, in_, r, w):
        if eng == "act":
            S.op("act", lambda e: e.copy(out, in_), r=r, w=w)
        else:
            S.op(eng, lambda e: e.tensor_copy(out, in_), r=r, w=w)

    def DMA(eng, out, in_, key, r, w):
        S.op(eng, lambda e: e.dma_start(out=out, in_=in_), r=r, w=w, dma=key)

    def MEMSET(eng, ap, val, w):
        S.op(eng, lambda e: e.memset(ap, val), r=(), w=w)

    with S.scope():
        ident = S.sbuf("ident", [128, 128], BF16)
        ones_bf = S.sbuf("ones_bf", [128, 128], BF16)
        ones_f = S.sbuf("ones_f", [128, 128], F32)
        fvec = S.sbuf("fvec", [128, 208], F32)
        idx_scope = S.scope()
        idx_scope.__enter__()
        kiT2 = S.sbuf("kiT2", [128, 8192], BF16)
        qiT = S.sbuf("qiT", [128, 4, NOWN], BF16)
        absw = S.sbuf("absw", [128, 17, 8], F32)
        sgn = S.sbuf("sgn", [128, 17, 8], F32)

        DMA("pool", ident[:], ident_d, "c0", [], ["ident"])
        DMA("sp", fvec[:], fvec_d, "c1", [], ["fvec"])
        MEMSET("dve", ones_bf[:], 1.0, ["ones_bf"])
        MEMSET("dve", ones_f[:], 1.0 / 1024.0, ["ones_f"])
        MEMSET("dve", absw[:], 0.0, ["absw"])
        MEMSET("dve", sgn[:], 0.0, ["sgn"])

        bw_scope = S.scope()
        bw_scope.__enter__()
        wu = S.sbuf("wu", [128, 8, 1024], BF16)
        wvm = S.sbuf("wvm", [128, 8, 1024], BF16)
        wq = S.sbuf("wq", [128, 8, 1024], BF16)
        wqi = S.sbuf("wqi", [128, 8, 512], BF16)
        wwi = S.sbuf("wwi", [128, 8, 8], BF16)
        with S.scope():
            wk = S.sbuf("wk", [128, 8, 1024], BF16)
            wv = S.sbuf("wvv", [128, 8, 1024], BF16)
            wki = S.sbuf("wki", [128, 8, 64], BF16)
            kgb = S.sbuf("kgb", [128, 512], F32)
            xbA = [S.sbuf("xbA%d" % i, [128, 8, 512], BF16) for i in range(2)]
            ktA = [S.sbuf("ktA%d" % i, [128, 8, 512], BF16) for i in range(2)]
            vA = [S.sbuf("vA%d" % i, [128, 4, 8, 129], BF16) for i in range(2)]
            stA = S.sbuf("stA", [128, 4, 6], F32)
            mvA = S.sbuf("mvA", [128, 4, 2], F32)
            rsA = S.sbuf("rsA", [128, 4], F32)
            kinf = S.sbuf("kinf", [128, 4, 64], F32)
            kinb = S.sbuf("kinb", [128, 4, 128], BF16)
            psA = [S.psum("psA%d" % i, [128, 512]) for i in range(5)]
            psKi = S.psum("psKi", [128, 4, 64])
            psKt = S.psum("psKt", [128, 512], BF16)

            DMA("pool", xbA[0][:], wview(xT_all, 0, 512), "xA0", [], ["xbA0"])
            DMA("pool", wk[:], wview(w_in, C_K, 1024), "w0", [], ["wk"])
            DMA("pool", wki[:], wview(w_in, C_KI, 64), "w2", [], ["wki"])
            DMA("pool", wv[:], wview(w_in, C_VV, 1024), "w1", [], ["wvv"])
            DMA("sp", kgb[:], bvec_d[:, 2048:2560], "c2", [], ["kgb"])
            for i in range(2):
                MEMSET("dve", vA[i][:, :, :, 128:129], 1.0, ["vA%d" % i])
            prot = Rot([(psA[i], "psA%d" % i) for i in range(5)])
            evi = 0
            for T in range(16):
                xb, xid = xbA[T % 2], "xbA%d" % (T % 2)
                kt, ktid = ktA[T % 2], "ktA%d" % (T % 2)
                vt, vid = vA[T % 2], "vA%d" % (T % 2)
                if T > 0:
                    DMA("pool", xb[:], wview(xT_all, T * 512, 512), "xA%d" % (T % 2), [], [xid])
                if T == 2:
                    DMA("pool", wvm[:], wview(w_in, C_V, 1024), "w3", [], ["wvm"])
                    DMA("pool", wu[:], wview(w_in, C_U, 1024), "w4", [], ["wu"])
                if T == 4:
                    DMA("pool", wq[:], wview(w_in, C_Q, 1024), "w5", [], ["wq"])
                    DMA("pool", wqi[:], wview(w_in, C_QI, 512), "w6", [], ["wqi"])
                    DMA("pool", wwi[:], wview(w_in, C_WI, 8), "w7", [], ["wwi"])
                for h in range(8):
                    ps, pid = prot.next()
                    for kc in range(8):
                        MM(ps[:], wk[:, kc, h * 128:(h + 1) * 128], xb[:, kc, :], kc == 0, kc == 7,
                           ["wk", xid], [pid])
                    CP("act" if evi % 2 == 0 else "dve", kt[:, h, :], ps[:], [pid], [ktid])
                    evi += 1
                DMA("sp", KT_s[:, :, T * 512:(T + 1) * 512], kt[:], "sKT%d" % (T % 2), [ktid], ["KT_s"])
                for blk in range(4):
                    for half in range(2):
                        ps, pid = prot.next()
                        for kc in range(8):
                            MM(ps[:], xb[:, kc, blk * 128:(blk + 1) * 128], wv[:, kc, half * 512:(half + 1) * 512],
                               kc == 0, kc == 7, ["wvv", xid], [pid])
                        CP("act" if evi % 2 == 0 else "dve", vt[:, blk, half * 4:(half + 1) * 4, 0:128],
                           ps[:].rearrange("p (h d) -> p h d", h=4), [pid], [vid])
                        evi += 1
                DMA("sp", V_s[:, T * 4:(T + 1) * 4, :], vt[:].rearrange("p b h d -> p b (h d)"),
                    "sV%d" % (T % 2), [vid], ["V_s"])
                for blk in range(4):
                    for kc in range(8):
                        MM(psKi[:, blk, :], xb[:, kc, blk * 128:(blk + 1) * 128], wki[:, kc, :], kc == 0, kc == 7,
                           ["wki", xid], ["psKi"], skip_group_check=True)
                for blk in range(4):
                    S.op("dve", lambda e, blk=blk: e.bn_stats(stA[:, blk, :], psKi[:, blk, :]), r=["psKi"], w=["stA"])
                    S.op("dve", lambda e, blk=blk: e.bn_aggr(mvA[:, blk, :], stA[:, blk, :]), r=["stA"], w=["mvA"])
                TS("dve", rsA[:], mvA[:, :, 1], EPS, None, ALU.add, None, ["mvA"], ["rsA"])
                S.op("act", lambda e: e.sqrt(rsA[:], rsA[:]), r=["rsA"], w=["rsA"])
                S.op("dve", lambda e: e.reciprocal(rsA[:], rsA[:]), r=["rsA"], w=["rsA"])
                for blk in range(4):
                    TS("dve", kinf[:, blk, :], psKi[:, blk, :], mvA[:, blk, 0:1], rsA[:, blk:blk + 1],
                       ALU.subtract, ALU.mult, ["psKi", "mvA", "rsA"], ["kinf"])
                TT("dve", kinf[:].rearrange("p b f -> p (b f)"), kinf[:].rearrange("p b f -> p (b f)"),
                   kgb[:, 0:256], ALU.mult, ["kinf", "kgb"], ["kinf"])
                for dup in range(2):
                    TT("dve", kinb[:, :, dup * 64:(dup + 1) * 64], kinf[:],
                       kgb[:, 256:512].rearrange("p (b f) -> p b f", b=4), ALU.add, ["kinf", "kgb"], ["kinb"])
                for blk in range(4):
                    TR(psKt[:, blk * 128:(blk + 1) * 128], kinb[:, blk, :], ident[:], ["kinb", "ident"], ["psKt"])
                CP("act", kiT2[:, T * 512:(T + 1) * 512], psKt[:], ["psKt"], ["kiT2"])
            S.barrier()

        with S.scope():
            vgb = S.sbuf("vgb", [128, 2048], F32)
            wsf = S.sbuf("wsf", [128, 8, 128], F32)
            trif = S.sbuf("trif", [128, 128], F32)
            wsb = S.sbuf("wsb", [128, 8, 128], BF16)
            bsb = S.sbuf("bsb", [1, 1024], BF16)
            xbB = [S.sbuf("xbB%d" % i, [128, 8, 512], BF16) for i in range(2)]
            xpB = [S.sbuf("xpB%d" % i, [128, 8, 128], BF16) for i in range(2)]
            uT = [S.sbuf("uT%d" % i, [128, 8, 512], BF16) for i in range(2)]
            uTh = S.sbuf("uTh", [128, 8, 32], BF16)
            aTt = [S.sbuf("aTt%d" % i, [128, 8, 512], BF16) for i in range(2)]
            qTt = [S.sbuf("qTt%d" % i, [128, 8, 512], BF16) for i in range(2)]
            vg = [S.sbuf("vg%d" % i, [128, 1024], F32) for i in range(2)]
            vn = [S.sbuf("vn%d" % i, [128, 1024], BF16) for i in range(2)]
            stB = S.sbuf("stB", [128, 2, 6], F32)
            mvB = S.sbuf("mvB", [128, 2], F32)
            rsB = S.sbuf("rsB", [128, 1], F32)
            psB = [S.psum("psB%d" % i, [128, 512]) for i in range(4)]
            psSp = [S.psum("psSp%d" % i, [128, 4, 128]) for i in range(2)]
            psW = S.psum("psW", [128, 8])
            self_sel = S.sbuf("sel", [8, 4, 128], F32)
            absT = S.sbuf("absT", [8, 512], F32)
            wbc = S.sbuf("wbc", [128, 512], F32)
            DMA("sp", self_sel[:], sel_d, "c5", [], ["sel"])
            psSh = S.psum("psSh", [128, 8, 2])

            DMA("pool", bsb[:], bs_row, "w5", [], ["bsb"])
            DMA("sp", vgb[:], bvec_d[:, 0:2048], "c2", [], ["vgb"])
            DMA("sp", wsf[:], wsT_d, "c3", [], ["wsf"])
            DMA("sp", trif[:], trimask_d, "c4", [], ["trif"])
            for g in range(8):
                TT("dve", wsb[:, g, :], wsf[:, g, :], trif[:], ALU.mult, ["wsf", "trif"], ["wsb"])

            protB = Rot([(psB[i], "psB%d" % i) for i in range(4)])
            vcount = [0]

            def vpath(xsrc, xid):
                i = vcount[0] % 2
                vcount[0] += 1
                vgt, vgid, vnt, vnid = vg[i], "vg%d" % i, vn[i], "vn%d" % i
                for half in range(2):
                    ps, pid = protB.next()
                    for kc in range(8):
                        MM(ps[:], xsrc[:, kc, :], wvm[:, kc, half * 512:(half + 1) * 512], kc == 0, kc == 7,
                           ["wvm", xid], [pid])
                    ACT(vgt[:, half * 512:(half + 1) * 512], ps[:], AF.Gelu_apprx_tanh, [pid], [vgid])
                for half in range(2):
                    S.op("dve", lambda e, half=half: e.bn_stats(stB[:, half, :], vgt[:, half * 512:(half + 1) * 512]),
                         r=[vgid], w=["stB"])
                S.op("dve", lambda e: e.bn_aggr(mvB[:], stB[:].rearrange("p a b -> p (a b)")), r=["stB"], w=["mvB"])
                TS("dve", rsB[:], mvB[:, 1:2], EPS, None, ALU.add, None, ["mvB"], ["rsB"])
                S.op("act", lambda e: e.sqrt(rsB[:], rsB[:]), r=["rsB"], w=["rsB"])
                S.op("dve", lambda e: e.reciprocal(rsB[:], rsB[:]), r=["rsB"], w=["rsB"])
                TS("dve", vgt[:], vgt[:], mvB[:, 0:1], rsB[:, 0:1], ALU.subtract, ALU.mult,
                   [vgid, "mvB", "rsB"], [vgid])
                TT("pool", vgt[:], vgt[:], vgb[:, 0:1024], ALU.mult, [vgid, "vgb"], [vgid])
                TT("pool", vnt[:], vgt[:], vgb[:, 1024:2048], ALU.add, [vgid, "vgb"], [vnid])
                return vnt, vnid

            for T in range(5):
                n = 512 if T < 4 else 32
                tok0 = T * 512
                xb, xid = xbB[T % 2], "xbB%d" % (T % 2)
                DMA("pool", xb[:, :, 0:n], wview(xT_own, tok0, n), "xA%d" % (T % 2), [], [xid])
                if T < 4:
                    ut, uid = uT[T % 2], "uT%d" % (T % 2)
                else:
                    ut, uid = uTh, "uTh"
                at, aid = aTt[T % 2], "aTt%d" % (T % 2)
                qt, qid = qTt[T % 2], "qTt%d" % (T % 2)

                def do_u(cs):
                    for c in cs:
                        ps, pid = protB.next()
                        for kc in range(8):
                            MM(ps[:, 0:n], wu[:, kc, c * 128:(c + 1) * 128], xb[:, kc, 0:n], kc == 0, kc == 7,
                               ["wu", xid], [pid])
                        ACT(ut[:, c, 0:n], ps[:, 0:n], AF.Gelu_apprx_tanh, [pid], [uid])

                def do_q(hs):
                    for h in hs:
                        ps, pid = protB.next()
                        for kc in range(8):
                            MM(ps[:, 0:n], wq[:, kc, h * 128:(h + 1) * 128], xb[:, kc, 0:n], kc == 0, kc == 7,
                               ["wq", xid], [pid])
                        TS("dve", qt[:, h, 0:n], ps[:, 0:n], 128.0 ** -0.5, None, ALU.mult, None, [pid], [qid])
                    if hs[-1] == 7:
                        DMA("sp", qT_s[:, :, tok0:tok0 + n], qt[:, :, 0:n], "sq%d" % (T % 2), [qid], ["qT_s"])

                def do_qi():
                    pw, pwid = protB.next()
                    for kc in range(8):
                        MM(pw[0:8, 0:n], wwi[:, kc, :], xb[:, kc, 0:n], kc == 0, kc == 7, ["wwi", xid], [pwid])
                    ACT(absT[0:8, 0:n], pw[0:8, 0:n], AF.Abs, [pwid], ["absT"], scale=8.0 ** -0.5 * 64.0 ** -0.5)
                    for pr in range(4):
                        ps, pid = protB.next()
                        for kc in range(8):
                            MM(ps[:, 0:n], wqi[:, kc, pr * 128:(pr + 1) * 128], xb[:, kc, 0:n], kc == 0, kc == 7,
                               ["wqi", xid], [pid])
                        pbc, pbcid = protB.next()
                        MM(pbc[:, 0:n], self_sel[0:8, pr, :], absT[0:8, 0:n], True, True, ["sel", "absT"], [pbcid])
                        CP("act", wbc[:, 0:n], pbc[:, 0:n], [pbcid], ["wbc"])
                        TT("dve", qiT[:, pr, tok0:tok0 + n], ps[:, 0:n], wbc[:, 0:n], ALU.mult, [pid, "wbc"], ["qiT"])
                    nb = 4 if T < 4 else 1
                    for blk in range(nb):
                        nt = 128 if T < 4 else 32
                        slot = T * 4 + blk
                        for kc in range(8):
                            MM(psW[0:nt, :], xb[:, kc, blk * 128:blk * 128 + nt], wwi[:, kc, :], kc == 0, kc == 7,
                               ["wwi", xid], ["psW"])
                        ACT(sgn[0:nt, slot, :], psW[0:nt, :], AF.Sign, ["psW"], ["sgn"])

                def spatial(blk, vnt, vnid):
                    for cg in range(2):
                        sp_, spid = psSp[cg], "psSp%d" % cg
                        for cc in range(4):
                            c = cg * 4 + cc
                            MM(sp_[:, cc, :], vnt[:, c * 128:(c + 1) * 128], wsb[:, c, :], True, False,
                               [vnid, "wsb"], [spid], skip_group_check=True)
                            MM(sp_[:, cc, :], ones_bf[0:1, :], bsb[0:1, c * 128:(c + 1) * 128], False, True,
                               ["ones_bf", "bsb"], [spid], skip_group_check=True)
                        TT("dve", at[:, cg * 4:(cg + 1) * 4, blk * 128:(blk + 1) * 128], sp_[:],
                           ut[:, cg * 4:(cg + 1) * 4, blk * 128:(blk + 1) * 128], ALU.mult, [spid, uid], [aid])

                def spatial_h(g, vnt, vnid):
                    for c in range(8):
                        MM(psSh[:, c, :], vnt[:, c * 128:(c + 1) * 128], wsb[:, c, 126:128], True, False,
                           [vnid, "wsb"], ["psSh"], skip_group_check=True)
                        MM(psSh[:, c, :], ones_bf[0:1, :], bsb[0:1, c * 128 + 126:c * 128 + 128], False, True,
                           ["ones_bf", "bsb"], ["psSh"], skip_group_check=True)
                    TT("dve", at[:, :, 2 * g:2 * g + 2], psSh[:], ut[:, :, 2 * g:2 * g + 2], ALU.mult,
                       ["psSh", uid], [aid])

                if T < 4:
                    v0 = vpath(xb[:, :, 0:128], xid)
                    do_u(range(0, 8))
                    v1 = vpath(xb[:, :, 128:256], xid)
                    do_q(range(0, 8))
                    spatial(0, *v0)
                    v2 = vpath(xb[:, :, 256:384], xid)
                    do_qi()
                    spatial(1, *v1)
                    v3 = vpath(xb[:, :, 384:512], xid)
                    spatial(2, *v2)
                    spatial(3, *v3)
                else:
                    def load_prev(g):
                        DMA("pool", xpB[g % 2][:], wview(xT_prev, g * 128, 128), "xP%d" % (g % 2), [],
                            ["xpB%d" % (g % 2)])
                    load_prev(0)
                    load_prev(1)
                    vcur = vpath(xpB[0][:], "xpB0")
                    do_u(range(0, 8))
                    do_q(range(0, 8))
                    do_qi()
                    for g in range(16):
                        vnx = None
                        if g + 1 < 16:
                            vnx = vpath(xpB[(g + 1) % 2][:], "xpB%d" % ((g + 1) % 2))
                            if g + 2 < 16:
                                load_prev(g + 2)
                        spatial_h(g, *vcur)
                        vcur = vnx
                DMA("sp", aT_s[:, :, tok0:tok0 + n], at[:, :, 0:n], "sa%d" % (T % 2), [aid], ["aT_s"])
            S.barrier()
        bw_scope.__exit__(None, None, None)

        slot_nch = [g + 1 for g in range(16)] + [16]
        slot_nt = [128] * 16 + [32]
        slot_blk0 = []
        acc = 0
        for g in range(17):
            slot_blk0.append(acc)
            acc += 4 * slot_nch[g]
        assert acc == NBLK_MASK
        with S.scope():
            Irow = [S.sbuf("Irow%d" % i, [128, 8192], F32) for i in range(2)]
            maskb = [S.sbuf("maskb%d" % i, [128, 8192], BF16) for i in range(2)]
            Rb = [S.sbuf("Rb%d" % i, [128, 2, 512], BF16) for i in range(8)]
            Dm = [S.sbuf("Dm%d" % i, [128, 8, 128], BF16) for i in range(2)]
            identf = S.sbuf("identf", [128, 128], F32)
            penl = S.sbuf("penl", [128, 512], F32)
            penh = [S.sbuf("penh%d" % i, [32, 512], F32) for i in range(2)]
            pow2 = S.sbuf("pow2", [128, NBIS], F32)
            cmin = [S.sbuf("cmin%d" % i, [128, 16], F32) for i in range(2)]
            sm = S.sbuf("sm", [128, 16], F32)
            hwv = S.sbuf("hwv", [128, NBIS], F32)
            cand = [S.sbuf("cand%d" % i, [128, 1], F32) for i in range(2)]
            cnt = S.sbuf("cnt", [128, 1], F32)
            sv = S.sbuf("sv", [128, 1], F32)
            tau = S.sbuf("tau", [128, 1], F32)
            mTo = [S.sbuf("mTo%d" % i, [128, 8, 128], BF16) for i in range(2)]
            zps = [S.psum("zps%d" % i, [128, 2, 512]) for i in range(2)]
            Ips = [S.psum("Ips%d" % i, [128, 512]) for i in range(2)]
            trp = [S.psum("trp%d" % i, [128, 8, 128], BF16) for i in range(2)]

            DMA("sp", identf[:], ident_d, "c2", [], ["identf"])
            DMA("sp", penl[:], pen_last_d, "c3", [], ["penl"])
            DMA("sp", pow2[:], pow2_d, "c5", [], ["pow2"])
            junkA = S.sbuf("junkA", [128, 4096], BF16)
            sB = S.sbuf("sB", [128, 1], F32)
            tmpc = S.sbuf("tmpc", [128, 1], F32)
            rrot = Rot([(Rb[i], "Rb%d" % i) for i in range(8)])
            trc = [0]

            def stageI(g):
                nt, nch = slot_nt[g], slot_nch[g]
                tok0 = g * 128
                Ir, Iid = Irow[g % 2], "Irow%d" % (g % 2)
                dm, dmid = Dm[g % 2], "Dm%d" % (g % 2)
                cm, cmid = cmin[g % 2], "cmin%d" % (g % 2)
                for h in range(8):
                    TS("pool", dm[0:nt, h, 0:nt], identf[0:nt, 0:nt], sgn[0:nt, g, h:h + 1], 1.0, ALU.mult, ALU.mult,
                       ["identf", "sgn"], [dmid])

                def acc(c, rl):
                    ip, ipid = Ips[c % 2], "Ips%d" % (c % 2)
                    for h in range(8):
                        rb, rid = rl[h]
                        MM(ip[0:nt, :], dm[0:nt, h, 0:nt], rb[0:nt, :], h == 0, h == 7, [dmid, rid], [ipid])
                    TS("dve", Ir[0:nt, c * 512:(c + 1) * 512], ip[0:nt, :], 1.0, None, ALU.mult, ALU.min,
                       [ipid], [Iid, cmid], accum=cm[0:nt, c:c + 1])
                    if g == 16:
                        DMA("pool", penh[c % 2][:], pen_halo_d[:, c * 512:(c + 1) * 512], "ph%d" % (c % 2), [],
                            ["penh%d" % (c % 2)])
                        TT("pool", Ir[0:nt, c * 512:(c + 1) * 512], Ir[0:nt, c * 512:(c + 1) * 512],
                           penh[c % 2][0:nt, :], ALU.add, [Iid, "penh%d" % (c % 2)], [Iid])
                    elif c == nch - 1:
                        TT("pool", Ir[0:nt, c * 512:(c + 1) * 512], Ir[0:nt, c * 512:(c + 1) * 512], penl[0:nt, :],
                           ALU.add, [Iid, "penl"], [Iid])

                prev = None
                for c in range(nch):
                    if prev is not None:
                        acc(*prev)
                    rl = []
                    for pr in range(4):
                        zz, zid = zps[pr % 2], "zps%d" % (pr % 2)
                        MM(zz[0:nt, 0, :], qiT[0:64, pr, tok0:tok0 + nt], kiT2[0:64, c * 512:(c + 1) * 512], True, True,
                           ["qiT", "kiT2"], [zid], skip_group_check=True)
                        MM(zz[0:nt, 1, :], qiT[64:128, pr, tok0:tok0 + nt], kiT2[64:128, c * 512:(c + 1) * 512],
                           True, True, ["qiT", "kiT2"], [zid], tile_position=(64, 0), skip_group_check=True)
                        rb, rid = rrot.next()
                        ACT(rb[0:nt, :, :], zz[0:nt, :, :], AF.Relu, [zid], [rid])
                        rl.append((rb[:, 0, :], rid))
                        rl.append((rb[:, 1, :], rid))
                    prev = (c, rl)
                    yield
                acc(*prev)
                yield

            def advance(gen, k):
                if gen is None:
                    return
                for _ in range(k):
                    try:
                        next(gen)
                    except StopIteration:
                        return

            def stageII(g, nxt, nxt_n):
                nt, nch = slot_nt[g], slot_nch[g]
                n = nch * 512
                nA = 128 * max(1, int(round(0.42 * 4 * nch)))
                nD = n - nA
                Ir, Iid = Irow[g % 2], "Irow%d" % (g % 2)
                mk, mkid = maskb[g % 2], "maskb%d" % (g % 2)
                cm, cmid = cmin[g % 2], "cmin%d" % (g % 2)
                S.op("dve", lambda e: e.tensor_reduce(out=sm[0:nt, 0:1], in_=Ir[0:nt, 0:n], axis=AX.X, op=ALU.max),
                     r=[Iid], w=["sm"])
                S.op("dve", lambda e: e.tensor_reduce(out=sm[0:nt, 1:2], in_=cm[0:nt, 0:nch], axis=AX.X, op=ALU.min),
                     r=[cmid], w=["sm"])
                TT("dve", sm[0:nt, 2:3], sm[0:nt, 0:1], sm[0:nt, 1:2], ALU.subtract, ["sm"], ["sm"])
                TS("dve", sm[0:nt, 3:4], sm[0:nt, 2:3], 1.02, 2e-6, ALU.mult, ALU.add, ["sm"], ["sm"])
                TS("dve", hwv[0:nt, :], pow2[0:nt, :], sm[0:nt, 3:4], None, ALU.mult, None, ["pow2", "sm"], ["hwv"])
                TS("dve", sm[0:nt, 4:5], sm[0:nt, 2:3], -0.01, -1e-6, ALU.mult, ALU.add, ["sm"], ["sm"])
                TT("dve", sm[0:nt, 5:6], sm[0:nt, 4:5], sm[0:nt, 1:2], ALU.add, ["sm"], ["sm"])
                TT("dve", cand[0][0:nt, :], sm[0:nt, 5:6], hwv[0:nt, 0:1], ALU.add, ["sm", "hwv"], ["cand0"])
                done = 0
                for k in range(1, NBIS + 1):
                    ci, co = cand[(k - 1) % 2], cand[k % 2]
                    cid, coid = "cand%d" % ((k - 1) % 2), "cand%d" % (k % 2)
                    ACT(junkA[0:nt, 0:nA], Ir[0:nt, nD:n], AF.Sign, [Iid, cid], ["junkA", "sB"],
                        bias=ci[0:nt, 0:1], scale=-1.0, accum_out=sB[0:nt, 0:1])
                    TS("dve", mk[0:nt, 0:nD], Ir[0:nt, 0:nD], ci[0:nt, 0:1], None, ALU.is_ge, ALU.add,
                       [Iid, cid], [mkid, "cnt"], accum=cnt[0:nt, 0:1])
                    STT(tmpc[0:nt, :], cnt[0:nt, :], 2.0, sB[0:nt, :], ALU.mult, ALU.subtract, ["cnt", "sB"], ["tmpc"])
                    last = (k == NBIS)
                    TS("dve", sv[0:nt, :], tmpc[0:nt, :], 511.0 - nA, 1.0 if last else 0.5, ALU.is_ge, ALU.subtract,
                       ["tmpc"], ["sv"])
                    STT((tau if last else co)[0:nt, :], sv[0:nt, :], hwv[0:nt, k - 1:k], ci[0:nt, :], ALU.mult,
                        ALU.add, ["sv", "hwv", cid], ["tau" if last else coid])
                    want = (nxt_n * k) // NBIS
                    advance(nxt, want - done)
                    done = want
                TS("dve", mk[0:nt, 0:n], Ir[0:nt, 0:n], tau[0:nt, 0:1], None, ALU.is_ge, None, [Iid, "tau"], [mkid])
                nsb = n // 128
                for sb0 in range(0, nsb, 8):
                    tc_ = trc[0]
                    tp, tpid = trp[tc_ % 2], "trp%d" % (tc_ % 2)
                    mo, moid = mTo[tc_ % 2], "mTo%d" % (tc_ % 2)
                    nb8 = min(8, nsb - sb0)
                    for q in range(nb8):
                        sb = sb0 + q
                        TR(tp[:, q, 0:nt], mk[0:nt, sb * 128:(sb + 1) * 128], ident[0:nt, 0:nt], [mkid, "ident"], [tpid])
                    CP("act", mo[:, 0:nb8, 0:nt], tp[:, 0:nb8, 0:nt], [tpid], [moid])
                    b0 = slot_blk0[g] + sb0
                    DMA("sp", mT_s[:, b0:b0 + nb8, 0:nt], mo[:, 0:nb8, 0:nt], "sm%d" % (tc_ % 2), [moid], ["mT_s"])
                    trc[0] += 1
                advance(nxt, 1000)

            advance(stageI(0), 1000)
            for g in range(17):
                nxt = stageI(g + 1) if g + 1 < 17 else None
                stageII(g, nxt, (slot_nch[g + 1] + 1) if g + 1 < 17 else 0)
            S.barrier()
        idx_scope.__exit__(None, None, None)

        d1w_scope = S.scope()
        d1w_scope.__enter__()
        wa = S.sbuf("wa", [128, 8, 1024], BF16)
        wb = S.sbuf("wb", [128, 8, 1024], BF16)
        wo = S.sbuf("wo", [128, 8, 1024], BF16)
        for p in range(2):
            with S.scope():
                KT = S.sbuf("KT", [128, 4, 8192], BF16)
                Vv = S.sbuf("Vv", [128, 64, 4, 129], BF16)
                QT = [S.sbuf("QT%d" % i, [128, 4, 128], BF16) for i in range(2)]
                mTi = [S.sbuf("mTi%d" % i, [128, 8, 128], BF16) for i in range(3)]
                Eb = [S.sbuf("Eb%d" % i, [128, 4, 128], BF16) for i in range(3)]
                Pb = [S.sbuf("Pb%d" % i, [128, 4, 128], BF16) for i in range(3)]
                rinv = S.sbuf("rinv", [128, 4], F32)
                bo = [S.sbuf("bo%d" % i, [128, 4, 128], BF16) for i in range(2)]
                boTt = [S.sbuf("boTt%d" % i, [128, 4, 128], BF16) for i in range(2)]
                Sps = [S.psum("Sps%d" % i, [128, 4, 128]) for i in range(3)]
                Ops = [S.psum("Ops%d" % i, [128, 2, 129]) for i in range(4)]
                trb = S.psum("trb", [128, 4, 128], BF16)
                for c in range(16):
                    DMA("pool", KT[:, :, c * 512:(c + 1) * 512], KT_s[:, 4 * p:4 * p + 4, c * 512:(c + 1) * 512],
                        "lk%d" % (c % 4), [], ["KT%d" % c])
                    DMA("pool", Vv[:, 4 * c:4 * c + 4, :, :].rearrange("p b h d -> p b (h d)"),
                        V_s[:, 4 * c:4 * c + 4, 4 * p * 129:(4 * p + 4) * 129], "lv%d" % (c % 4), [], ["Vv%d" % c])
                iters = [(g, sb) for g in range(17) for sb in range(slot_nch[g] * 4)]
                groups = [(g, sb) for (g, sb) in iters if sb % 8 == 0]
                gidx = {gs: i for i, gs in enumerate(groups)}
                LOOK = 2
                deferred = []

                def load_q(g):
                    nt = slot_nt[g]
                    DMA("pool", QT[g % 2][:, :, 0:nt], qT_s[:, 4 * p:4 * p + 4, g * 128:g * 128 + nt],
                        "lq%d" % (g % 2), [], ["QT%d" % (g % 2)])

                def load_m(i):
                    g, sb = groups[i]
                    nt = slot_nt[g]
                    nb8 = min(8, slot_nch[g] * 4 - sb)
                    b0 = slot_blk0[g] + sb
                    DMA("pool", mTi[i % 3][:, 0:nb8, 0:nt], mT_s[:, b0:b0 + nb8, 0:nt], "lm%d" % (i % 3), [],
                        ["mTi%d" % (i % 3)])

                def stage1(it):
                    g, sb = iters[it]
                    nt = slot_nt[g]
                    c = sb // 4
                    if sb == 0:
                        if g == 0:
                            load_q(0)
                        if g + 1 < 17:
                            load_q(g + 1)
                        if p == 1 and g == 9:
                            DMA("pool", wa[:], wview(w_a, 0, 1024), "w0", [], ["wa"])
                            DMA("pool", wb[:], wview(w_b, 0, 1024), "w1", [], ["wb"])
                            DMA("pool", wo[:], wview(w_o, 0, 1024), "w4", [], ["wo"])
                    if sb % 8 == 0:
                        i = gidx[(g, sb)]
                        if i == 0:
                            load_m(0)
                        if i + 1 < len(groups):
                            load_m(i + 1)
                    i = gidx[(g, sb - sb % 8)]
                    mt, mtid = mTi[i % 3], "mTi%d" % (i % 3)
                    qt, qid = QT[g % 2], "QT%d" % (g % 2)
                    sp_, spid = Sps[it % 3], "Sps%d" % (it % 3)
                    eb, ebid = Eb[it % 3], "Eb%d" % (it % 3)
                    pb, pbid = Pb[it % 3], "Pb%d" % (it % 3)
                    for h in range(4):
                        MM(sp_[:, h, 0:nt], KT[:, h, sb * 128:(sb + 1) * 128], qt[:, h, 0:nt], True, True,
                           ["KT%d" % c, qid], [spid], skip_group_check=True)
                    ACT(eb[:, :, 0:nt], sp_[:, :, 0:nt], AF.Exp, [spid], [ebid])
                    TT("dve", pb[:, :, 0:nt], eb[:, :, 0:nt],
                       mt[:, sb % 8, 0:nt].unsqueeze(1).to_broadcast([128, 4, nt]), ALU.mult, [ebid, mtid], [pbid])

                def stage2(it):
                    g, sb = iters[it]
                    nt = slot_nt[g]
                    nsb = slot_nch[g] * 4
                    c = sb // 4
                    pb, pbid = Pb[it % 3], "Pb%d" % (it % 3)
                    o0, o0id = Ops[(g % 2) * 2], "Ops%d" % ((g % 2) * 2)
                    o1, o1id = Ops[(g % 2) * 2 + 1], "Ops%d" % ((g % 2) * 2 + 1)
                    for h in range(4):
                        ob, obid = (o0, o0id) if h < 2 else (o1, o1id)
                        MM(ob[0:nt, h % 2, :], pb[:, h, 0:nt], Vv[:, sb, h, :], sb == 0 and h % 2 == 0,
                           sb == nsb - 1, [pbid, "Vv%d" % c], [obid], skip_group_check=True)
                    if sb == nsb - 1:
                        bt, btid = bo[g % 2], "bo%d" % (g % 2)
                        btT, btTid = boTt[g % 2], "boTt%d" % (g % 2)
                        S.op("dve", lambda e: e.reciprocal(rinv[0:nt, 0:2], o0[0:nt, :, 128]), r=[o0id], w=["rinv"])
                        S.op("dve", lambda e: e.reciprocal(rinv[0:nt, 2:4], o1[0:nt, :, 128]), r=[o1id], w=["rinv"])
                        for h in range(4):
                            ob, obid = (o0, o0id) if h < 2 else (o1, o1id)
                            TS("dve", bt[0:nt, h, :], ob[0:nt, h % 2, 0:128], rinv[0:nt, h:h + 1], None, ALU.mult,
                               None, [obid, "rinv"], [btid])

                        def fin():
                            for h in range(4):
                                TR(trb[:, h, 0:nt], bt[0:nt, h, :], ident[0:nt, 0:nt], [btid, "ident"], ["trb"])
                            CP("act", btT[:, :, 0:nt], trb[:, :, 0:nt], ["trb"], [btTid])
                            DMA("sp", boT_s[:, 4 * p:4 * p + 4, g * 128:g * 128 + nt], btT[:, :, 0:nt],
                                "sb%d" % (g % 2), [btTid], ["boT_s"])
                        deferred.append((it + 6, fin))

                for k in range(len(iters) + LOOK):
                    if k < len(iters):
                        stage1(k)
                    if k - LOOK >= 0:
                        stage2(k - LOOK)
                    while deferred and deferred[0][0] <= k - LOOK:
                        deferred.pop(0)[1]()
                while deferred:
                    deferred.pop(0)[1]()
                S.barrier()

        def layernorm_fm(z, zid, n, gcol, bcol, outt, outid, sq, sqid, psm, psq, tmp):
            mean_sb, m2, rstd = tmp
            for kc in range(8):
                ACT(sq[:, kc, 0:n], z[:, kc, 0:n], AF.Square, [zid], [sqid])
            for kc in range(8):
                MM(psm[:, 0:n], ones_f[:], z[:, kc, 0:n], kc == 0, kc == 7, ["ones_f", zid], ["psm"])
            for kc in range(8):
                MM(psq[:, 0:n], ones_f[:], sq[:, kc, 0:n], kc == 0, kc == 7, ["ones_f", sqid], ["psq"])
            CP("act", mean_sb[:, 0:n], psm[:, 0:n], ["psm"], ["mean_sb"])
            TT("pool", m2[:, 0:n], mean_sb[:, 0:n], mean_sb[:, 0:n], ALU.mult, ["mean_sb"], ["m2"])
            TT("dve", rstd[:, 0:n], psq[:, 0:n], m2[:, 0:n], ALU.subtract, ["psq", "m2"], ["rstd"])
            TS("dve", rstd[:, 0:n], rstd[:, 0:n], 0.0, EPS, ALU.max, ALU.add, ["rstd"], ["rstd"])
            S.op("act", lambda e: e.sqrt(rstd[:, 0:n], rstd[:, 0:n]), r=["rstd"], w=["rstd"])
            S.op("dve", lambda e: e.reciprocal(rstd[:, 0:n], rstd[:, 0:n]), r=["rstd"], w=["rstd"])
            for kc in range(8):
                eng = "dve" if kc % 2 == 0 else "pool"
                TT(eng, sq[:, kc, 0:n], z[:, kc, 0:n], mean_sb[:, 0:n], ALU.subtract, [zid, "mean_sb"], [sqid])
                TT(eng, sq[:, kc, 0:n], sq[:, kc, 0:n], rstd[:, 0:n], ALU.mult, [sqid, "rstd"], [sqid])
                TS(eng, outt[:, kc, 0:n], sq[:, kc, 0:n], fvec[:, gcol + kc:gcol + kc + 1],
                   fvec[:, bcol + kc:bcol + kc + 1], ALU.mult, ALU.add, [sqid, "fvec"], [outid])

        with S.scope():
            wga = S.sbuf("wga", [128, 8, 1024], BF16)
            wgb = S.sbuf("wgb", [128, 8, 1024], BF16)
            aTi = [S.sbuf("aTi%d" % i, [128, 8, 512], BF16) for i in range(2)]
            bTi = [S.sbuf("bTi%d" % i, [128, 8, 512], BF16) for i in range(2)]
            xbD = [S.sbuf("xbD%d" % i, [128, 8, 512], BF16) for i in range(2)]
            xfD = [S.sbuf("xfD0", [128, 8, 512], F32)] * 2
            sga2 = [S.sbuf("sga%d" % i, [128, 512], F32) for i in range(2)]
            sgb2 = [S.sbuf("sgb%d" % i, [128, 512], F32) for i in range(2)]
            t12 = [S.sbuf("t1%d" % i, [128, 512], F32) for i in range(2)]
            t22 = [S.sbuf("t2%d" % i, [128, 512], F32) for i in range(2)]
            mixT = S.sbuf("mixT", [128, 8, 512], BF16)
            z1 = S.sbuf("z1", [128, 8, 512], F32)
            sq1 = S.sbuf("sq1", [128, 8, 512], F32)
            mean_sb = S.sbuf("mean_sb", [128, 512], F32)
            m2 = S.sbuf("m2", [128, 512], F32)
            rstd = S.sbuf("rstd", [128, 512], F32)
            psD = [S.psum("psD%d" % i, [128, 512]) for i in range(6)]
            psm = S.psum("psm", [128, 512])
            psq = S.psum("psq", [128, 512])
            DMA("pool", wga[:], wview(w_in, C_GA, 1024), "w2", [], ["wga"])
            DMA("pool", wgb[:], wview(w_in, C_GB, 1024), "w3", [], ["wgb"])
            protD = Rot([(psD[i], "psD%d" % i) for i in range(6)])
            pending1 = None
            for T in range(5):
                n = 512 if T < 4 else 32
                tok0 = T * 512
                i2 = T % 2
                DMA("pool", aTi[i2][:, :, 0:n], aT_s[:, :, tok0:tok0 + n], "la%d" % i2, [], ["aTi%d" % i2])
                DMA("pool", bTi[i2][:, :, 0:n], boT_s[:, :, tok0:tok0 + n], "lb%d" % i2, [], ["bTi%d" % i2])
                DMA("pool", xbD[i2][:, :, 0:n], wview(xT_own, tok0, n), "xA%d" % i2, [], ["xbD%d" % i2])
                DMA("sp", xfD[i2][:, :, 0:n], wview(xT_own, tok0, n), "xF0", [], ["xfD0"])
                for m in range(8):
                    pA, pAid = protD.next()
                    pBb, pBid = protD.next()
                    pGA, pGAid = protD.next()
                    pGB, pGBid = protD.next()
                    for kc in range(8):
                        MM(pA[:, 0:n], wa[:, kc, m * 128:(m + 1) * 128], aTi[i2][:, kc, 0:n], kc == 0, kc == 7,
                           ["wa", "aTi%d" % i2], [pAid])
                    for kc in range(8):
                        MM(pBb[:, 0:n], wb[:, kc, m * 128:(m + 1) * 128], bTi[i2][:, kc, 0:n], kc == 0, kc == 7,
                           ["wb", "bTi%d" % i2], [pBid])
                    for kc in range(8):
                        MM(pGA[:, 0:n], wga[:, kc, m * 128:(m + 1) * 128], xbD[i2][:, kc, 0:n], kc == 0, kc == 7,
                           ["wga", "xbD%d" % i2], [pGAid])
                    for kc in range(8):
                        MM(pGB[:, 0:n], wgb[:, kc, m * 128:(m + 1) * 128], xbD[i2][:, kc, 0:n], kc == 0, kc == 7,
                           ["wgb", "xbD%d" % i2], [pGBid])
                    sga, sgb, t1, t2 = sga2[m % 2], sgb2[m % 2], t12[m % 2], t22[m % 2]
                    sfx = str(m % 2)
                    ACT(sga[:, 0:n], pGA[:, 0:n], AF.Sigmoid, [pGAid], ["sga" + sfx])
                    ACT(sgb[:, 0:n], pGB[:, 0:n], AF.Sigmoid, [pGBid], ["sgb" + sfx])
                    TT("dve", t1[:, 0:n], pA[:, 0:n], sga[:, 0:n], ALU.mult, [pAid, "sga" + sfx], ["t1" + sfx])
                    TT("dve", t2[:, 0:n], pBb[:, 0:n], sgb[:, 0:n], ALU.mult, [pBid, "sgb" + sfx], ["t2" + sfx])
                    TT("pool", mixT[:, m, 0:n], t1[:, 0:n], t2[:, 0:n], ALU.add, ["t1" + sfx, "t2" + sfx], ["mixT"])
                    if m == 1 and pending1 is not None:
                        pending1()
                        pending1 = None
                for nn in range(8):
                    pY, pYid = protD.next()
                    for kc in range(8):
                        MM(pY[:, 0:n], wo[:, kc, nn * 128:(nn + 1) * 128], mixT[:, kc, 0:n], kc == 0, kc == 7,
                           ["wo", "mixT"], [pYid])
                    STT(z1[:, nn, 0:n], xfD[i2][:, nn, 0:n], ALPHA, pY[:, 0:n], ALU.mult, ALU.add,
                        ["xfD0", pYid], ["z1"])

                def fin1(n=n, tok0=tok0):
                    layernorm_fm(z1, "z1", n, 0, 8, sq1, "sq1", sq1, "sq1", psm, psq, (mean_sb, m2, rstd))
                    DMA("sp", h1T_s[:, :, tok0:tok0 + n], sq1[:, :, 0:n], "sh0", ["sq1"], ["h1T_s"])
                pending1 = fin1
            pending1()
            S.barrier()
        d1w_scope.__exit__(None, None, None)

        with S.scope():
            TW = 256
            wup = S.sbuf("wup", [128, 8, 5632], BF16)
            wdn = S.sbuf("wdn", [128, 22, 1024], BF16)
            upH = S.sbuf("upH", [128, 44, 32], F32)
            validH = S.sbuf("validH", [128, 32], F32)
            h1f2 = [S.sbuf("h1f%d" % i, [128, 8, TW], F32) for i in range(2)]
            h1b2 = [S.sbuf("h1b%d" % i, [128, 8, TW], BF16) for i in range(2)]
            h1f, h1b = h1f2[1], h1b2[1]
            ext = [S.sbuf("ext%d" % i, [128, 2, 130], F32) for i in range(4)]
            yg2 = [S.sbuf("yg%d" % i, [128, 2, 128], F32) for i in range(2)]
            yv2 = [S.sbuf("yv%d" % i, [128, 2, 128], F32) for i in range(2)]
            sgt2 = [S.sbuf("sgt%d" % i, [128, 2, 128], F32) for i in range(2)]
            aF = S.sbuf("aF", [128, 22, TW], BF16)
            sq2 = S.sbuf("sq2", [128, 8, TW], F32)
            mean_sb = S.sbuf("mean_sb2", [128, TW], F32)
            m2 = S.sbuf("m2b", [128, TW], F32)
            rstd = S.sbuf("rstd2", [128, TW], F32)
            psU = [S.psum("psU%d" % i, [128, 512]) for i in range(4)]
            psF = [S.psum("psF%d" % i, [128, 512]) for i in range(2)]
            psm = S.psum("psm2", [128, 512])
            psq = S.psum("psq2", [128, 512])
            for q4 in (0, 2, 1, 3):
                DMA("pool", wup[:, :, q4 * 1408:(q4 + 1) * 1408], wview(w_up, q4 * 1408, 1408), "w%d" % q4, [],
                    ["wup%d" % q4])

            def wupid(ch):
                return "wup%d" % (ch // 11)
            DMA("pool", wdn[:], w_dn.rearrange("(kc p) n -> p kc n", p=128), "w4", [], ["wdn"])
            DMA("sp", validH[:], validH_d, "c2", [], ["validH"])
            protU = Rot([(psU[i], "psU%d" % i) for i in range(4)])
            erot = Rot([(ext[i], "ext%d" % i) for i in range(4)])
            DMA("sp", h1f[:, :, 0:32], h1T_s[:, :, 2048:2080], "lh1", [], ["h1f1"])
            CP("dve", h1b[:, :, 0:32], h1f[:, :, 0:32], ["h1f1"], ["h1b1"])

            def halo_chunk(ch):
                ps, pid = protU.next()
                for kc in range(8):
                    MM(ps[:, 0:32], wup[:, kc, ch * 128:(ch + 1) * 128], h1b[:, kc, 0:32], kc == 0, kc == 7,
                       [wupid(ch), "h1b1"], [pid])
                TT("dve", upH[:, ch, :], ps[:, 0:32], validH[:], ALU.mult, [pid, "validH"], ["upH%d" % ch])
            halo_chunk(0)
            halo_chunk(22)
            NT2 = 2048 // TW

            def load_h1(T):
                DMA("sp", h1f2[T % 2][:], h1T_s[:, :, T * TW:(T + 1) * TW], "lh%d" % (T % 2), [], ["h1f%d" % (T % 2)])

            load_h1(0)
            pending = None
            for T in range(NT2):
                hf, hfid = h1f2[T % 2], "h1f%d" % (T % 2)
                hb, hbid = h1b2[T % 2], "h1b%d" % (T % 2)
                for kc in range(8):
                    CP("pool" if kc % 2 else "dve", hb[:, kc, :], hf[:, kc, :], [hfid], [hbid])
                for i in range(22):
                    yg, yv, sgt = yg2[i % 2], yv2[i % 2], sgt2[i % 2]
                    sfx = str(i % 2)
                    if T == 0 and i + 1 < 22:
                        halo_chunk(i + 1)
                        halo_chunk(22 + i + 1)
                    for which, ytile, yid in ((0, yg, "yg" + sfx), (1, yv, "yv" + sfx)):
                        ch = i + 22 * which
                        ps, pid = protU.next()
                        for kc in range(8):
                            MM(ps[:, 0:TW], wup[:, kc, ch * 128:(ch + 1) * 128], hb[:, kc, :], kc == 0, kc == 7,
                               [wupid(ch), hbid], [pid])
                        ex, exid = erot.next()
                        psv = ps[:, 0:TW].rearrange("p (b t) -> p b t", b=2)
                        CP("act", ex[:, :, 2:130], psv, [pid], [exid])
                        ACT(ytile[:], psv, AF.Identity, [pid, "fvec"], [yid], scale=fvec[:, 120 + ch:121 + ch],
                            bias=fvec[:, 164 + ch:165 + ch])
                        CP("pool", ex[:, :, 0:2], upH[:, ch, 4 * T:4 * T + 4].rearrange("p (b t) -> p b t", b=2),
                           ["upH%d" % ch], [exid])
                        STT(ytile[:], ex[:, :, 1:129], fvec[:, 76 + ch:77 + ch], ytile[:], ALU.mult, ALU.add,
                            [exid, "fvec", yid], [yid])
                        STT(ytile[:], ex[:, :, 0:128], fvec[:, 32 + ch:33 + ch], ytile[:], ALU.mult, ALU.add,
                            [exid, "fvec", yid], [yid])
                    ACT(sgt[:], yg[:], AF.Silu, ["yg" + sfx], ["sgt" + sfx])
                    TT("pool", aF[:, i, :].rearrange("p (b t) -> p b t", b=2), sgt[:], yv[:], ALU.mult,
                       ["sgt" + sfx, "yv" + sfx], ["aF"])
                    if i == 2 and pending is not None:
                        pending()
                        pending = None
                        if T + 1 < NT2:
                            load_h1(T + 1)
                if T == 0 and NT2 > 1:
                    load_h1(1)
                for nn in range(8):
                    pf, pfid = psF[nn % 2], "psF%d" % (nn % 2)
                    for kc in range(22):
                        MM(pf[:, 0:TW], wdn[:, kc, nn * 128:(nn + 1) * 128], aF[:, kc, :], kc == 0, kc == 21,
                           ["wdn", "aF"], [pfid])
                    STT(hf[:, nn, :], hf[:, nn, :], ALPHA, pf[:, 0:TW], ALU.mult, ALU.add, [hfid, pfid], [hfid])

                def fin(T=T, hf=hf, hfid=hfid):
                    layernorm_fm(hf, hfid, TW, 16, 24, sq2, "sq2", sq2, "sq2", psm, psq, (mean_sb, m2, rstd))
                    DMA("sp", outT[:, :, T * TW:(T + 1) * TW], sq2[:], "so0", ["sq2"], ["outT"])
                pending = fin
            pending()
            S.barrier()
        S.wait_all("sp")
        S.emit()
    return nc


_NC_CACHE = {}


def _host_inputs(x, w_in, ln_v_g, ln_v_b, w_spatial, b_spatial, ln_kidx_g, ln_kidx_b,
                 w_branch_a, w_branch_b, w_o, ln1_g, ln1_b, w_up, conv_w, conv_b, w_down, ln2_g, ln2_b):
    f32 = np.float32
    x = np.asarray(x, f32)

    def fm(v, kc):
        return np.asarray(v, f32).reshape(kc, 128).T

    fvec = np.zeros((128, 208), f32)
    fvec[:, 0:8] = fm(ln1_g[0], 8)
    fvec[:, 8:16] = fm(ln1_b[0], 8)
    fvec[:, 16:24] = fm(ln2_g[0], 8)
    fvec[:, 24:32] = fm(ln2_b[0], 8)
    for j in range(3):
        fvec[:, 32 + 44 * j:76 + 44 * j] = fm(conv_w[0, j], 44)
    fvec[:, 164:208] = fm(conv_b[0], 44)
    bvec = np.zeros((128, 2560), f32)
    bvec[:, 0:1024] = np.asarray(ln_v_g[0], f32)[None, :]
    bvec[:, 1024:2048] = np.asarray(ln_v_b[0], f32)[None, :]
    bvec[:, 2048:2304] = np.tile(np.asarray(ln_kidx_g[0], f32), 4)[None, :]
    bvec[:, 2304:2560] = np.tile(np.asarray(ln_kidx_b[0], f32), 4)[None, :]
    wsT = np.ascontiguousarray(np.transpose(np.asarray(w_spatial[0], f32), (2, 0, 1)))
    s_i = np.arange(128)
    trimask = (s_i[:, None] <= s_i[None, :]).astype(f32)
    common = {
        "w_in": np.ascontiguousarray(np.asarray(w_in[0], f32)),
        "w_a": np.ascontiguousarray(np.asarray(w_branch_a[0], f32)),
        "w_b": np.ascontiguousarray(np.asarray(w_branch_b[0], f32)),
        "w_o": np.ascontiguousarray(np.asarray(w_o[0], f32)),
        "w_up": np.ascontiguousarray(np.asarray(w_up[0], f32)),
        "w_dn": np.ascontiguousarray(np.asarray(w_down[0], f32)),
        "wsT": wsT,
        "bs_row": np.ascontiguousarray(np.asarray(b_spatial[0], f32).reshape(1, 1024)),
        "fvec": fvec, "bvec": bvec,
        "ident": np.eye(128, dtype=f32),
        "sel": np.stack([np.stack([(np.arange(128) // 64 + 2 * pr == h).astype(f32) for pr in range(4)])
                         for h in range(8)]),
        "trimask": trimask,
        "pow2": np.tile((2.0 ** -np.arange(1, NBIS + 1)).astype(f32)[None, :], (128, 1)),
    }
    xT = [np.ascontiguousarray(x[b].T) for b in range(2)]
    in_maps = []
    for c in range(8):
        b, j = c // 4, c % 4
        own_cols = []
        for g in range(16):
            t0 = (4 * g + j) * 128
            own_cols.append(np.arange(t0, t0 + 128))
        halo_pos = []
        for g in range(16):
            t0 = (4 * g + j) * 128
            halo_pos += [t0 - 2, t0 - 1]
        halo_pos = np.array(halo_pos)
        own_idx = np.concatenate(own_cols)
        xT_own = np.zeros((1024, NOWN), f32)
        xT_own[:, 0:2048] = xT[b][:, own_idx]
        hv = halo_pos >= 0
        xT_own[:, 2048:][:, hv] = xT[b][:, halo_pos[hv]]
        xT_prev = np.zeros((1024, 2048), f32)
        for g in range(16):
            pb = 4 * g + j - 1
            if pb >= 0:
                xT_prev[:, g * 128:(g + 1) * 128] = xT[b][:, pb * 128:(pb + 1) * 128]
        kk = np.arange(512)
        rblk, ks = kk // 128, kk % 128
        t = np.arange(128)
        allowed = (rblk[None, :] < j) | ((rblk[None, :] == j) & (ks[None, :] <= t[:, None]))
        pen_last = np.where(allowed, 0.0, NEG).astype(f32)
        sidx = np.arange(8192)
        hp = np.where(halo_pos >= 0, halo_pos, 0)
        pen_halo = np.where(sidx[None, :] <= hp[:, None], 0.0, NEG).astype(f32)
        validH = np.tile(hv.astype(f32)[None, :], (128, 1))
        m = dict(common)
        m.update({"xT_all": xT[b], "xT_own": xT_own, "xT_prev": xT_prev, "pen_last": pen_last,
                  "pen_halo": pen_halo, "validH": validH})
        in_maps.append(m)
    return in_maps


def kernel(**inputs):
    in_maps = _host_inputs(**inputs)
    if "nc" not in _NC_CACHE:
        _NC_CACHE["nc"] = build_program()
    nc = _NC_CACHE["nc"]
    res = run_bass_kernel_spmd(nc, in_maps, core_ids=list(range(8)))
    out = np.zeros((2, 8192, 1024), np.float32)
    for c in range(8):
        b, j = c // 4, c % 4
        o = np.asarray(res.results[c]["outT"], np.float32)
        o = o.transpose(2, 1, 0).reshape(2048, 1024)
        for g in range(16):
            t0 = (4 * g + j) * 128
            out[b, t0:t0 + 128, :] = o[g * 128:(g + 1) * 128, :]
    return out
```

```python
import contextlib
import numpy as np
import concourse.bass as bass
import concourse.mybir as mybir
from concourse.bass_utils import run_bass_kernel_spmd

F32 = mybir.dt.float32
BF16 = mybir.dt.bfloat16
AF = mybir.ActivationFunctionType
ALU = mybir.AluOpType
AX = mybir.AxisListType

NBIS = 14
ALPHA = 2.0 ** 0.25
EPS = 1e-5
NEG = -1.0e30
NOWN = 2080
C_U, C_V, C_Q, C_K, C_VV, C_QI, C_KI, C_WI, C_GA, C_GB = 0, 1024, 2048, 3072, 4096, 5120, 5632, 5696, 5704, 6728
NBLK_MASK = 4 * 136 + 64


class Sched:
    ENG = ("pe", "act", "dve", "pool", "sp")

    def __init__(self, nc):
        self.nc = nc
        self.stack = contextlib.ExitStack()
        self.ops = {e: [] for e in self.ENG}
        self.sems = {}
        self.ecnt = {e: 0 for e in self.ENG}
        self.dcnt = {}
        self.dlast = {}
        self.lastw = {}
        self.readers = {}
        self.waited = {e: {} for e in self.ENG}
        self.scopes = []
        self.needed = {e: set() for e in self.ENG}
        self.rawmap = {e: {} for e in self.ENG}
        self.sig = {e: 0 for e in self.ENG}
        for e in self.ENG:
            self.sems["E" + e] = self.stack.enter_context(nc.semaphore("sem_e_" + e))

    @contextlib.contextmanager
    def scope(self):
        st = contextlib.ExitStack()
        self.scopes.append(st)
        try:
            yield st
        finally:
            self.scopes.pop()
            st.close()

    def sbuf(self, name, shape, dt):
        self.uid = getattr(self, "uid", 0) + 1
        return self.scopes[-1].enter_context(self.nc.sbuf_tensor("s%d_%s" % (self.uid, name), list(shape), dt))

    def psum(self, name, shape, dt=F32):
        self.uid = getattr(self, "uid", 0) + 1
        return self.scopes[-1].enter_context(self.nc.psum_tensor("p%d_%s" % (self.uid, name), list(shape), dt))

    def _dsem(self, key):
        k = "D" + key
        if k not in self.sems:
            self.sems[k] = self.stack.enter_context(self.nc.semaphore("sem_d_" + key))
            self.dcnt[k] = 0
        return k

    def op(self, eng, fn, r=(), w=(), dma=None):
        deps = []
        for b in list(r) + list(w):
            t = self.lastw.get(b)
            if t is not None:
                deps.append(t)
        for b in w:
            deps.extend(self.readers.get(b, ()))
        if dma is not None:
            k = self._dsem(dma)
            prev = self.dlast.get(k)
            if prev is not None:
                deps.append(prev)
            self.dcnt[k] += 16
            tok = (k, self.dcnt[k], None)
            self.dlast[k] = tok
        else:
            self.ecnt[eng] += 1
            tok = ("E" + eng, self.ecnt[eng], eng)
        waits = {}
        for (s, v, e2) in deps:
            if e2 == eng and eng == "pe":
                continue
            if self.waited[eng].get(s, 0) >= v:
                continue
            waits[s] = max(waits.get(s, 0), v)
        for s, v in waits.items():
            self.waited[eng][s] = v
            if s[0] == "E":
                self.needed[s[1:]].add(v)
        self.ops[eng].append((sorted(waits.items()), fn, tok))
        for b in w:
            self.lastw[b] = tok
            self.readers[b] = []
        for b in r:
            self.readers.setdefault(b, []).append(tok)
        return tok

    def wait_all(self, eng):
        waits = {}
        for e in self.ENG:
            if self.ecnt[e] == 0 or (e == eng and e == "pe"):
                continue
            if self.waited[eng].get("E" + e, 0) < self.ecnt[e]:
                waits["E" + e] = self.ecnt[e]
        for k, v in self.dcnt.items():
            if v and self.waited[eng].get(k, 0) < v:
                waits[k] = v
        for s, v in waits.items():
            self.waited[eng][s] = v
            if s[0] == "E":
                self.needed[s[1:]].add(v)
        self.ops[eng].append((sorted(waits.items()), None, None))

    def flush(self):
        nc = self.nc
        sems = self.sems
        rawmap = self.rawmap
        for e in self.ENG:
            for waits, fn, tok in self.ops[e]:
                if fn is not None and tok[2] is not None and tok[1] in self.needed[e]:
                    self.sig[e] += 1
                    rawmap[e][tok[1]] = self.sig[e]
        with nc.Block() as block:
            decos = {"pe": block.tensor, "act": block.scalar, "dve": block.vector,
                     "pool": block.gpsimd, "sp": block.sync}
            for e in self.ENG:
                ops = self.ops[e]

                def body(engobj, ops=ops, e=e):
                    for waits, fn, tok in ops:
                        for s, v in waits:
                            if s[0] == "E":
                                v = rawmap[s[1:]][v]
                            engobj.wait_ge(sems[s], v)
                        if fn is None:
                            continue
                        ins = fn(engobj)
                        if tok[2] is None:
                            ins.then_inc(sems[tok[0]], 16)
                        elif tok[1] in rawmap[e]:
                            ins.then_inc(sems[tok[0]], 1)

                decos[e](body)
        self.ops = {e: [] for e in self.ENG}

    def barrier(self):
        for e in self.ENG:
            self.wait_all(e)
        self.lastw.clear()
        self.readers.clear()
        self.flush()

    def emit(self):
        self.flush()
        self.stack.close()


class Rot:
    def __init__(self, items):
        self.items = items
        self.i = 0

    def next(self):
        it = self.items[self.i % len(self.items)]
        self.i += 1
        return it


def build_program():
    nc = bass.Bass("TRN2", target_bir_lowering=False)
    S = Sched(nc)

    def din(name, shape):
        return nc.dram_tensor(name, list(shape), F32, kind="ExternalInput").ap()

    xT_all = din("xT_all", [1024, 8192])
    xT_own = din("xT_own", [1024, NOWN])
    xT_prev = din("xT_prev", [1024, 2048])
    w_in = din("w_in", [1024, 7752])
    w_a = din("w_a", [1024, 1024])
    w_b = din("w_b", [1024, 1024])
    w_o = din("w_o", [1024, 1024])
    w_up = din("w_up", [1024, 5632])
    w_dn = din("w_dn", [2816, 1024])
    wsT_d = din("wsT", [128, 8, 128])
    bs_row = din("bs_row", [1, 1024])
    fvec_d = din("fvec", [128, 208])
    bvec_d = din("bvec", [128, 2560])
    ident_d = din("ident", [128, 128])
    trimask_d = din("trimask", [128, 128])
    pen_last_d = din("pen_last", [128, 512])
    pen_halo_d = din("pen_halo", [32, 8192])
    validH_d = din("validH", [128, 32])
    pow2_d = din("pow2", [128, NBIS])
    sel_d = din("sel", [8, 4, 128])
    outT = nc.dram_tensor("outT", [128, 8, 2048], F32, kind="ExternalOutput").ap()

    KT_s = nc.dram_tensor("KT_s", [128, 8, 8192], BF16).ap()
    V_s = nc.dram_tensor("V_s", [128, 64, 8 * 129], BF16).ap()
    qT_s = nc.dram_tensor("qT_s", [128, 8, NOWN], BF16).ap()
    mT_s = nc.dram_tensor("mT_s", [128, NBLK_MASK, 128], BF16).ap()
    aT_s = nc.dram_tensor("aT_s", [128, 8, NOWN], BF16).ap()
    boT_s = nc.dram_tensor("boT_s", [128, 8, NOWN], BF16).ap()
    h1T_s = nc.dram_tensor("h1T_s", [128, 8, NOWN], F32).ap()

    def wview(w, c0, n):
        return w[:, c0:c0 + n].rearrange("(kc p) n -> p kc n", p=128)

    def MM(out, lhsT, rhs, start, stop, r, w, **kw):
        S.op("pe", lambda e: e.matmul(out, lhsT=lhsT, rhs=rhs, start=start, stop=stop, **kw), r=r, w=w)

    def TR(out, in_, idn, r, w):
        S.op("pe", lambda e: e.transpose(out, in_, idn), r=r, w=w)

    def ACT(out, in_, func, r, w, **kw):
        S.op("act", lambda e: e.activation(out=out, in_=in_, func=func, **kw), r=r, w=w)

    def TS(eng, out, in0, s1, s2, op0, op1, r, w, accum=None):
        if accum is not None:
            S.op(eng, lambda e: e.tensor_scalar(out=out, in0=in0, scalar1=s1, scalar2=s2, op0=op0, op1=op1,
                                                accum_out=accum), r=r, w=w)
        elif op1 is None:
            S.op(eng, lambda e: e.tensor_scalar(out=out, in0=in0, scalar1=s1, scalar2=None, op0=op0), r=r, w=w)
        else:
            S.op(eng, lambda e: e.tensor_scalar(out=out, in0=in0, scalar1=s1, scalar2=s2, op0=op0, op1=op1),
                 r=r, w=w)

    def TT(eng, out, in0, in1, op, r, w):
        S.op(eng, lambda e: e.tensor_tensor(out=out, in0=in0, in1=in1, op=op), r=r, w=w)

    def STT(out, in0, scalar, in1, op0, op1, r, w):
        S.op("dve", lambda e: e.scalar_tensor_tensor(out=out, in0=in0, scalar=scalar, in1=in1, op0=op0, op1=op1),
             r=r, w=w)

    def CP(eng, out, in_, r, w):
        if eng == "act":
            S.op("act", lambda e: e.copy(out, in_), r=r, w=w)
        else:
            S.op(eng, lambda e: e.tensor_copy(out, in_), r=r, w=w)

    def DMA(eng, out, in_, key, r, w):
        S.op(eng, lambda e: e.dma_start(out=out, in_=in_), r=r, w=w, dma=key)

    def MEMSET(eng, ap, val, w):
        S.op(eng, lambda e: e.memset(ap, val), r=(), w=w)

    with S.scope():
        ident = S.sbuf("ident", [128, 128], BF16)
        ones_bf = S.sbuf("ones_bf", [128, 128], BF16)
        ones_f = S.sbuf("ones_f", [128, 128], F32)
        fvec = S.sbuf("fvec", [128, 208], F32)
        idx_scope = S.scope()
        idx_scope.__enter__()
        kiT2 = S.sbuf("kiT2", [128, 8192], BF16)
        qiT = S.sbuf("qiT", [128, 4, NOWN], BF16)
        absw = S.sbuf("absw", [128, 17, 8], F32)
        sgn = S.sbuf("sgn", [128, 17, 8], F32)

        DMA("pool", ident[:], ident_d, "c0", [], ["ident"])
        DMA("sp", fvec[:], fvec_d, "c1", [], ["fvec"])
        MEMSET("dve", ones_bf[:], 1.0, ["ones_bf"])
        MEMSET("dve", ones_f[:], 1.0 / 1024.0, ["ones_f"])
        MEMSET("dve", absw[:], 0.0, ["absw"])
        MEMSET("dve", sgn[:], 0.0, ["sgn"])

        bw_scope = S.scope()
        bw_scope.__enter__()
        wu = S.sbuf("wu", [128, 8, 1024], BF16)
        wvm = S.sbuf("wvm", [128, 8, 1024], BF16)
        wq = S.sbuf("wq", [128, 8, 1024], BF16)
        wqi = S.sbuf("wqi", [128, 8, 512], BF16)
        wwi = S.sbuf("wwi", [128, 8, 8], BF16)
        with S.scope():
            wk = S.sbuf("wk", [128, 8, 1024], BF16)
            wv = S.sbuf("wvv", [128, 8, 1024], BF16)
            wki = S.sbuf("wki", [128, 8, 64], BF16)
            kgb = S.sbuf("kgb", [128, 512], F32)
            xbA = [S.sbuf("xbA%d" % i, [128, 8, 512], BF16) for i in range(2)]
            ktA = [S.sbuf("ktA%d" % i, [128, 8, 512], BF16) for i in range(2)]
            vA = [S.sbuf("vA%d" % i, [128, 4, 8, 129], BF16) for i in range(2)]
            stA = S.sbuf("stA", [128, 4, 6], F32)
            mvA = S.sbuf("mvA", [128, 4, 2], F32)
            rsA = S.sbuf("rsA", [128, 4], F32)
            kinf = S.sbuf("kinf", [128, 4, 64], F32)
            kinb = S.sbuf("kinb", [128, 4, 128], BF16)
            psA = [S.psum("psA%d" % i, [128, 512]) for i in range(5)]
            psKi = S.psum("psKi", [128, 4, 64])
            psKt = S.psum("psKt", [128, 512], BF16)

            DMA("pool", xbA[0][:], wview(xT_all, 0, 512), "xA0", [], ["xbA0"])
            DMA("pool", wk[:], wview(w_in, C_K, 1024), "w0", [], ["wk"])
            DMA("pool", wki[:], wview(w_in, C_KI, 64), "w2", [], ["wki"])
            DMA("pool", wv[:], wview(w_in, C_VV, 1024), "w1", [], ["wvv"])
            DMA("sp", kgb[:], bvec_d[:, 2048:2560], "c2", [], ["kgb"])
            for i in range(2):
                MEMSET("dve", vA[i][:, :, :, 128:129], 1.0, ["vA%d" % i])
            prot = Rot([(psA[i], "psA%d" % i) for i in range(5)])
            evi = 0
            for T in range(16):
                xb, xid = xbA[T % 2], "xbA%d" % (T % 2)
                kt, ktid = ktA[T % 2], "ktA%d" % (T % 2)
                vt, vid = vA[T % 2], "vA%d" % (T % 2)
                if T > 0:
                    DMA("pool", xb[:], wview(xT_all, T * 512, 512), "xA%d" % (T % 2), [], [xid])
                if T == 2:
                    DMA("pool", wvm[:], wview(w_in, C_V, 1024), "w3", [], ["wvm"])
                    DMA("pool", wu[:], wview(w_in, C_U, 1024), "w4", [], ["wu"])
                if T == 4:
                    DMA("pool", wq[:], wview(w_in, C_Q, 1024), "w5", [], ["wq"])
                    DMA("pool", wqi[:], wview(w_in, C_QI, 512), "w6", [], ["wqi"])
                    DMA("pool", wwi[:], wview(w_in, C_WI, 8), "w7", [], ["wwi"])
                for h in range(8):
                    ps, pid = prot.next()
                    for kc in range(8):
                        MM(ps[:], wk[:, kc, h * 128:(h + 1) * 128], xb[:, kc, :], kc == 0, kc == 7,
                           ["wk", xid], [pid])
                    CP("act" if evi % 2 == 0 else "dve", kt[:, h, :], ps[:], [pid], [ktid])
                    evi += 1
                DMA("sp", KT_s[:, :, T * 512:(T + 1) * 512], kt[:], "sKT%d" % (T % 2), [ktid], ["KT_s"])
                for blk in range(4):
                    for half in range(2):
                        ps, pid = prot.next()
                        for kc in range(8):
                            MM(ps[:], xb[:, kc, blk * 128:(blk + 1) * 128], wv[:, kc, half * 512:(half + 1) * 512],
                               kc == 0, kc == 7, ["wvv", xid], [pid])
                        CP("act" if evi % 2 == 0 else "dve", vt[:, blk, half * 4:(half + 1) * 4, 0:128],
                           ps[:].rearrange("p (h d) -> p h d", h=4), [pid], [vid])
                        evi += 1
                DMA("sp", V_s[:, T * 4:(T + 1) * 4, :], vt[:].rearrange("p b h d -> p b (h d)"),
                    "sV%d" % (T % 2), [vid], ["V_s"])
                for blk in range(4):
                    for kc in range(8):
                        MM(psKi[:, blk, :], xb[:, kc, blk * 128:(blk + 1) * 128], wki[:, kc, :], kc == 0, kc == 7,
                           ["wki", xid], ["psKi"], skip_group_check=True)
                for blk in range(4):
                    S.op("dve", lambda e, blk=blk: e.bn_stats(stA[:, blk, :], psKi[:, blk, :]), r=["psKi"], w=["stA"])
                    S.op("dve", lambda e, blk=blk: e.bn_aggr(mvA[:, blk, :], stA[:, blk, :]), r=["stA"], w=["mvA"])
                TS("dve", rsA[:], mvA[:, :, 1], EPS, None, ALU.add, None, ["mvA"], ["rsA"])
                S.op("act", lambda e: e.sqrt(rsA[:], rsA[:]), r=["rsA"], w=["rsA"])
                S.op("dve", lambda e: e.reciprocal(rsA[:], rsA[:]), r=["rsA"], w=["rsA"])
                for blk in range(4):
                    TS("dve", kinf[:, blk, :], psKi[:, blk, :], mvA[:, blk, 0:1], rsA[:, blk:blk + 1],
                       ALU.subtract, ALU.mult, ["psKi", "mvA", "rsA"], ["kinf"])
                TT("dve", kinf[:].rearrange("p b f -> p (b f)"), kinf[:].rearrange("p b f -> p (b f)"),
                   kgb[:, 0:256], ALU.mult, ["kinf", "kgb"], ["kinf"])
                for dup in range(2):
                    TT("dve", kinb[:, :, dup * 64:(dup + 1) * 64], kinf[:],
                       kgb[:, 256:512].rearrange("p (b f) -> p b f", b=4), ALU.add, ["kinf", "kgb"], ["kinb"])
                for blk in range(4):
                    TR(psKt[:, blk * 128:(blk + 1) * 128], kinb[:, blk, :], ident[:], ["kinb", "ident"], ["psKt"])
                CP("act", kiT2[:, T * 512:(T + 1) * 512], psKt[:], ["psKt"], ["kiT2"])
            S.barrier()

        with S.scope():
            vgb = S.sbuf("vgb", [128, 2048], F32)
            wsf = S.sbuf("wsf", [128, 8, 128], F32)
            trif = S.sbuf("trif", [128, 128], F32)
            wsb = S.sbuf("wsb", [128, 8, 128], BF16)
            bsb = S.sbuf("bsb", [1, 1024], BF16)
            xbB = [S.sbuf("xbB%d" % i, [128, 8, 512], BF16) for i in range(2)]
            xpB = [S.sbuf("xpB%d" % i, [128, 8, 128], BF16) for i in range(2)]
            uT = [S.sbuf("uT%d" % i, [128, 8, 512], BF16) for i in range(2)]
            uTh = S.sbuf("uTh", [128, 8, 32], BF16)
            aTt = [S.sbuf("aTt%d" % i, [128, 8, 512], BF16) for i in range(2)]
            qTt = [S.sbuf("qTt%d" % i, [128, 8, 512], BF16) for i in range(2)]
            vg = [S.sbuf("vg%d" % i, [128, 1024], F32) for i in range(2)]
            vn = [S.sbuf("vn%d" % i, [128, 1024], BF16) for i in range(2)]
            stB = S.sbuf("stB", [128, 2, 6], F32)
            mvB = S.sbuf("mvB", [128, 2], F32)
            rsB = S.sbuf("rsB", [128, 1], F32)
            psB = [S.psum("psB%d" % i, [128, 512]) for i in range(4)]
            psSp = [S.psum("psSp%d" % i, [128, 4, 128]) for i in range(2)]
            psW = S.psum("psW", [128, 8])
            self_sel = S.sbuf("sel", [8, 4, 128], F32)
            absT = S.sbuf("absT", [8, 512], F32)
            wbc = S.sbuf("wbc", [128, 512], F32)
            DMA("sp", self_sel[:], sel_d, "c5", [], ["sel"])
            psSh = S.psum("psSh", [128, 8, 2])

            DMA("pool", bsb[:], bs_row, "w5", [], ["bsb"])
            DMA("sp", vgb[:], bvec_d[:, 0:2048], "c2", [], ["vgb"])
            DMA("sp", wsf[:], wsT_d, "c3", [], ["wsf"])
            DMA("sp", trif[:], trimask_d, "c4", [], ["trif"])
            for g in range(8):
                TT("dve", wsb[:, g, :], wsf[:, g, :], trif[:], ALU.mult, ["wsf", "trif"], ["wsb"])

            protB = Rot([(psB[i], "psB%d" % i) for i in range(4)])
            vcount = [0]

            def vpath(xsrc, xid):
                i = vcount[0] % 2
                vcount[0] += 1
                vgt, vgid, vnt, vnid = vg[i], "vg%d" % i, vn[i], "vn%d" % i
                for half in range(2):
                    ps, pid = protB.next()
                    for kc in range(8):
                        MM(ps[:], xsrc[:, kc, :], wvm[:, kc, half * 512:(half + 1) * 512], kc == 0, kc == 7,
                           ["wvm", xid], [pid])
                    ACT(vgt[:, half * 512:(half + 1) * 512], ps[:], AF.Gelu_apprx_tanh, [pid], [vgid])
                for half in range(2):
                    S.op("dve", lambda e, half=half: e.bn_stats(stB[:, half, :], vgt[:, half * 512:(half + 1) * 512]),
                         r=[vgid], w=["stB"])
                S.op("dve", lambda e: e.bn_aggr(mvB[:], stB[:].rearrange("p a b -> p (a b)")), r=["stB"], w=["mvB"])
                TS("dve", rsB[:], mvB[:, 1:2], EPS, None, ALU.add, None, ["mvB"], ["rsB"])
                S.op("act", lambda e: e.sqrt(rsB[:], rsB[:]), r=["rsB"], w=["rsB"])
                S.op("dve", lambda e: e.reciprocal(rsB[:], rsB[:]), r=["rsB"], w=["rsB"])
                TS("dve", vgt[:], vgt[:], mvB[:, 0:1], rsB[:, 0:1], ALU.subtract, ALU.mult,
                   [vgid, "mvB", "rsB"], [vgid])
                TT("pool", vgt[:], vgt[:], vgb[:, 0:1024], ALU.mult, [vgid, "vgb"], [vgid])
                TT("pool", vnt[:], vgt[:], vgb[:, 1024:2048], ALU.add, [vgid, "vgb"], [vnid])
                return vnt, vnid

            for T in range(5):
                n = 512 if T < 4 else 32
                tok0 = T * 512
                xb, xid = xbB[T % 2], "xbB%d" % (T % 2)
                DMA("pool", xb[:, :, 0:n], wview(xT_own, tok0, n), "xA%d" % (T % 2), [], [xid])
                if T < 4:
                    ut, uid = uT[T % 2], "uT%d" % (T % 2)
                else:
                    ut, uid = uTh, "uTh"
                at, aid = aTt[T % 2], "aTt%d" % (T % 2)
                qt, qid = qTt[T % 2], "qTt%d" % (T % 2)

                def do_u(cs):
                    for c in cs:
                        ps, pid = protB.next()
                        for kc in range(8):
                            MM(ps[:, 0:n], wu[:, kc, c * 128:(c + 1) * 128], xb[:, kc, 0:n], kc == 0, kc == 7,
                               ["wu", xid], [pid])
                        ACT(ut[:, c, 0:n], ps[:, 0:n], AF.Gelu_apprx_tanh, [pid], [uid])

                def do_q(hs):
                    for h in hs:
                        ps, pid = protB.next()
                        for kc in range(8):
                            MM(ps[:, 0:n], wq[:, kc, h * 128:(h + 1) * 128], xb[:, kc, 0:n], kc == 0, kc == 7,
                               ["wq", xid], [pid])
                        TS("dve", qt[:, h, 0:n], ps[:, 0:n], 128.0 ** -0.5, None, ALU.mult, None, [pid], [qid])
                    if hs[-1] == 7:
                        DMA("sp", qT_s[:, :, tok0:tok0 + n], qt[:, :, 0:n], "sq%d" % (T % 2), [qid], ["qT_s"])

                def do_qi():
                    pw, pwid = protB.next()
                    for kc in range(8):
                        MM(pw[0:8, 0:n], wwi[:, kc, :], xb[:, kc, 0:n], kc == 0, kc == 7, ["wwi", xid], [pwid])
                    ACT(absT[0:8, 0:n], pw[0:8, 0:n], AF.Abs, [pwid], ["absT"], scale=8.0 ** -0.5 * 64.0 ** -0.5)
                    for pr in range(4):
                        ps, pid = protB.next()
                        for kc in range(8):
                            MM(ps[:, 0:n], wqi[:, kc, pr * 128:(pr + 1) * 128], xb[:, kc, 0:n], kc == 0, kc == 7,
                               ["wqi", xid], [pid])
                        pbc, pbcid = protB.next()
                        MM(pbc[:, 0:n], self_sel[0:8, pr, :], absT[0:8, 0:n], True, True, ["sel", "absT"], [pbcid])
                        CP("act", wbc[:, 0:n], pbc[:, 0:n], [pbcid], ["wbc"])
                        TT("dve", qiT[:, pr, tok0:tok0 + n], ps[:, 0:n], wbc[:, 0:n], ALU.mult, [pid, "wbc"], ["qiT"])
                    nb = 4 if T < 4 else 1
                    for blk in range(nb):
                        nt = 128 if T < 4 else 32
                        slot = T * 4 + blk
                        for kc in range(8):
                            MM(psW[0:nt, :], xb[:, kc, blk * 128:blk * 128 + nt], wwi[:, kc, :], kc == 0, kc == 7,
                               ["wwi", xid], ["psW"])
                        ACT(sgn[0:nt, slot, :], psW[0:nt, :], AF.Sign, ["psW"], ["sgn"])

                def spatial(blk, vnt, vnid):
                    for cg in range(2):
                        sp_, spid = psSp[cg], "psSp%d" % cg
                        for cc in range(4):
                            c = cg * 4 + cc
                            MM(sp_[:, cc, :], vnt[:, c * 128:(c + 1) * 128], wsb[:, c, :], True, False,
                               [vnid, "wsb"], [spid], skip_group_check=True)
                            MM(sp_[:, cc, :], ones_bf[0:1, :], bsb[0:1, c * 128:(c + 1) * 128], False, True,
                               ["ones_bf", "bsb"], [spid], skip_group_check=True)
                        TT("dve", at[:, cg * 4:(cg + 1) * 4, blk * 128:(blk + 1) * 128], sp_[:],
                           ut[:, cg * 4:(cg + 1) * 4, blk * 128:(blk + 1) * 128], ALU.mult, [spid, uid], [aid])

                def spatial_h(g, vnt, vnid):
                    for c in range(8):
                        MM(psSh[:, c, :], vnt[:, c * 128:(c + 1) * 128], wsb[:, c, 126:128], True, False,
                           [vnid, "wsb"], ["psSh"], skip_group_check=True)
                        MM(psSh[:, c, :], ones_bf[0:1, :], bsb[0:1, c * 128 + 126:c * 128 + 128], False, True,
                           ["ones_bf", "bsb"], ["psSh"], skip_group_check=True)
                    TT("dve", at[:, :, 2 * g:2 * g + 2], psSh[:], ut[:, :, 2 * g:2 * g + 2], ALU.mult,
                       ["psSh", uid], [aid])

                if T < 4:
                    v0 = vpath(xb[:, :, 0:128], xid)
                    do_u(range(0, 8))
                    v1 = vpath(xb[:, :, 128:256], xid)
                    do_q(range(0, 8))
                    spatial(0, *v0)
                    v2 = vpath(xb[:, :, 256:384], xid)
                    do_qi()
                    spatial(1, *v1)
                    v3 = vpath(xb[:, :, 384:512], xid)
                    spatial(2, *v2)
                    spatial(3, *v3)
                else:
                    def load_prev(g):
                        DMA("pool", xpB[g % 2][:], wview(xT_prev, g * 128, 128), "xP%d" % (g % 2), [],
                            ["xpB%d" % (g % 2)])
                    load_prev(0)
                    load_prev(1)
                    vcur = vpath(xpB[0][:], "xpB0")
                    do_u(range(0, 8))
                    do_q(range(0, 8))
                    do_qi()
                    for g in range(16):
                        vnx = None
                        if g + 1 < 16:
                            vnx = vpath(xpB[(g + 1) % 2][:], "xpB%d" % ((g + 1) % 2))
                            if g + 2 < 16:
                                load_prev(g + 2)
                        spatial_h(g, *vcur)
                        vcur = vnx
                DMA("sp", aT_s[:, :, tok0:tok0 + n], at[:, :, 0:n], "sa%d" % (T % 2), [aid], ["aT_s"])
            S.barrier()
        bw_scope.__exit__(None, None, None)

        slot_nch = [g + 1 for g in range(16)] + [16]
        slot_nt = [128] * 16 + [32]
        slot_blk0 = []
        acc = 0
        for g in range(17):
            slot_blk0.append(acc)
            acc += 4 * slot_nch[g]
        assert acc == NBLK_MASK
        with S.scope():
            Irow = [S.sbuf("Irow%d" % i, [128, 8192], F32) for i in range(2)]
            maskb = [S.sbuf("maskb%d" % i, [128, 8192], BF16) for i in range(2)]
            Rb = [S.sbuf("Rb%d" % i, [128, 2, 512], BF16) for i in range(8)]
            Dm = [S.sbuf("Dm%d" % i, [128, 8, 128], BF16) for i in range(2)]
            identf = S.sbuf("identf", [128, 128], F32)
            penl = S.sbuf("penl", [128, 512], F32)
            penh = [S.sbuf("penh%d" % i, [32, 512], F32) for i in range(2)]
            pow2 = S.sbuf("pow2", [128, NBIS], F32)
            cmin = [S.sbuf("cmin%d" % i, [128, 16], F32) for i in range(2)]
            sm = S.sbuf("sm", [128, 16], F32)
            hwv = S.sbuf("hwv", [128, NBIS], F32)
            cand = [S.sbuf("cand%d" % i, [128, 1], F32) for i in range(2)]
            cnt = S.sbuf("cnt", [128, 1], F32)
            sv = S.sbuf("sv", [128, 1], F32)
            tau = S.sbuf("tau", [128, 1], F32)
            mTo = [S.sbuf("mTo%d" % i, [128, 8, 128], BF16) for i in range(2)]
            zps = [S.psum("zps%d" % i, [128, 2, 512]) for i in range(2)]
            Ips = [S.psum("Ips%d" % i, [128, 512]) for i in range(2)]
            trp = [S.psum("trp%d" % i, [128, 8, 128], BF16) for i in range(2)]

            DMA("sp", identf[:], ident_d, "c2", [], ["identf"])
            DMA("sp", penl[:], pen_last_d, "c3", [], ["penl"])
            DMA("sp", pow2[:], pow2_d, "c5", [], ["pow2"])
            junkA = S.sbuf("junkA", [128, 4096], BF16)
            sB = S.sbuf("sB", [128, 1], F32)
            tmpc = S.sbuf("tmpc", [128, 1], F32)
            rrot = Rot([(Rb[i], "Rb%d" % i) for i in range(8)])
            trc = [0]

            def stageI(g):
                nt, nch = slot_nt[g], slot_nch[g]
                tok0 = g * 128
                Ir, Iid = Irow[g % 2], "Irow%d" % (g % 2)
                dm, dmid = Dm[g % 2], "Dm%d" % (g % 2)
                cm, cmid = cmin[g % 2], "cmin%d" % (g % 2)
                for h in range(8):
                    TS("pool", dm[0:nt, h, 0:nt], identf[0:nt, 0:nt], sgn[0:nt, g, h:h + 1], 1.0, ALU.mult, ALU.mult,
                       ["identf", "sgn"], [dmid])

                def acc(c, rl):
                    ip, ipid = Ips[c % 2], "Ips%d" % (c % 2)
                    for h in range(8):
                        rb, rid = rl[h]
                        MM(ip[0:nt, :], dm[0:nt, h, 0:nt], rb[0:nt, :], h == 0, h == 7, [dmid, rid], [ipid])
                    TS("dve", Ir[0:nt, c * 512:(c + 1) * 512], ip[0:nt, :], 1.0, None, ALU.mult, ALU.min,
                       [ipid], [Iid, cmid], accum=cm[0:nt, c:c + 1])
                    if g == 16:
                        DMA("pool", penh[c % 2][:], pen_halo_d[:, c * 512:(c + 1) * 512], "ph%d" % (c % 2), [],
                            ["penh%d" % (c % 2)])
                        TT("pool", Ir[0:nt, c * 512:(c + 1) * 512], Ir[0:nt, c * 512:(c + 1) * 512],
                           penh[c % 2][0:nt, :], ALU.add, [Iid, "penh%d" % (c % 2)], [Iid])
                    elif c == nch - 1:
                        TT("pool", Ir[0:nt, c * 512:(c + 1) * 512], Ir[0:nt, c * 512:(c + 1) * 512], penl[0:nt, :],
                           ALU.add, [Iid, "penl"], [Iid])

                prev = None
                for c in range(nch):
                    if prev is not None:
                        acc(*prev)
                    rl = []
                    for pr in range(4):
                        zz, zid = zps[pr % 2], "zps%d" % (pr % 2)
                        MM(zz[0:nt, 0, :], qiT[0:64, pr, tok0:tok0 + nt], kiT2[0:64, c * 512:(c + 1) * 512], True, True,
                           ["qiT", "kiT2"], [zid], skip_group_check=True)
                        MM(zz[0:nt, 1, :], qiT[64:128, pr, tok0:tok0 + nt], kiT2[64:128, c * 512:(c + 1) * 512],
                           True, True, ["qiT", "kiT2"], [zid], tile_position=(64, 0), skip_group_check=True)
                        rb, rid = rrot.next()
                        ACT(rb[0:nt, :, :], zz[0:nt, :, :], AF.Relu, [zid], [rid])
                        rl.append((rb[:, 0, :], rid))
                        rl.append((rb[:, 1, :], rid))
                    prev = (c, rl)
                    yield
                acc(*prev)
                yield

            def advance(gen, k):
                if gen is None:
                    return
                for _ in range(k):
                    try:
                        next(gen)
                    except StopIteration:
                        return

            def stageII(g, nxt, nxt_n):
                nt, nch = slot_nt[g], slot_nch[g]
                n = nch * 512
                nA = 128 * max(1, int(round(0.42 * 4 * nch)))
                nD = n - nA
                Ir, Iid = Irow[g % 2], "Irow%d" % (g % 2)
                mk, mkid = maskb[g % 2], "maskb%d" % (g % 2)
                cm, cmid = cmin[g % 2], "cmin%d" % (g % 2)
                S.op("dve", lambda e: e.tensor_reduce(out=sm[0:nt, 0:1], in_=Ir[0:nt, 0:n], axis=AX.X, op=ALU.max),
                     r=[Iid], w=["sm"])
                S.op("dve", lambda e: e.tensor_reduce(out=sm[0:nt, 1:2], in_=cm[0:nt, 0:nch], axis=AX.X, op=ALU.min),
                     r=[cmid], w=["sm"])
                TT("dve", sm[0:nt, 2:3], sm[0:nt, 0:1], sm[0:nt, 1:2], ALU.subtract, ["sm"], ["sm"])
                TS("dve", sm[0:nt, 3:4], sm[0:nt, 2:3], 1.02, 2e-6, ALU.mult, ALU.add, ["sm"], ["sm"])
                TS("dve", hwv[0:nt, :], pow2[0:nt, :], sm[0:nt, 3:4], None, ALU.mult, None, ["pow2", "sm"], ["hwv"])
                TS("dve", sm[0:nt, 4:5], sm[0:nt, 2:3], -0.01, -1e-6, ALU.mult, ALU.add, ["sm"], ["sm"])
                TT("dve", sm[0:nt, 5:6], sm[0:nt, 4:5], sm[0:nt, 1:2], ALU.add, ["sm"], ["sm"])
                TT("dve", cand[0][0:nt, :], sm[0:nt, 5:6], hwv[0:nt, 0:1], ALU.add, ["sm", "hwv"], ["cand0"])
                done = 0
                for k in range(1, NBIS + 1):
                    ci, co = cand[(k - 1) % 2], cand[k % 2]
                    cid, coid = "cand%d" % ((k - 1) % 2), "cand%d" % (k % 2)
                    ACT(junkA[0:nt, 0:nA], Ir[0:nt, nD:n], AF.Sign, [Iid, cid], ["junkA", "sB"],
                        bias=ci[0:nt, 0:1], scale=-1.0, accum_out=sB[0:nt, 0:1])
                    TS("dve", mk[0:nt, 0:nD], Ir[0:nt, 0:nD], ci[0:nt, 0:1], None, ALU.is_ge, ALU.add,
                       [Iid, cid], [mkid, "cnt"], accum=cnt[0:nt, 0:1])
                    STT(tmpc[0:nt, :], cnt[0:nt, :], 2.0, sB[0:nt, :], ALU.mult, ALU.subtract, ["cnt", "sB"], ["tmpc"])
                    last = (k == NBIS)
                    TS("dve", sv[0:nt, :], tmpc[0:nt, :], 511.0 - nA, 1.0 if last else 0.5, ALU.is_ge, ALU.subtract,
                       ["tmpc"], ["sv"])
                    STT((tau if last else co)[0:nt, :], sv[0:nt, :], hwv[0:nt, k - 1:k], ci[0:nt, :], ALU.mult,
                        ALU.add, ["sv", "hwv", cid], ["tau" if last else coid])
                    want = (nxt_n * k) // NBIS
                    advance(nxt, want - done)
                    done = want
                TS("dve", mk[0:nt, 0:n], Ir[0:nt, 0:n], tau[0:nt, 0:1], None, ALU.is_ge, None, [Iid, "tau"], [mkid])
                nsb = n // 128
                for sb0 in range(0, nsb, 8):
                    tc_ = trc[0]
                    tp, tpid = trp[tc_ % 2], "trp%d" % (tc_ % 2)
                    mo, moid = mTo[tc_ % 2], "mTo%d" % (tc_ % 2)
                    nb8 = min(8, nsb - sb0)
                    for q in range(nb8):
                        sb = sb0 + q
                        TR(tp[:, q, 0:nt], mk[0:nt, sb * 128:(sb + 1) * 128], ident[0:nt, 0:nt], [mkid, "ident"], [tpid])
                    CP("act", mo[:, 0:nb8, 0:nt], tp[:, 0:nb8, 0:nt], [tpid], [moid])
                    b0 = slot_blk0[g] + sb0
                    DMA("sp", mT_s[:, b0:b0 + nb8, 0:nt], mo[:, 0:nb8, 0:nt], "sm%d" % (tc_ % 2), [moid], ["mT_s"])
                    trc[0] += 1
                advance(nxt, 1000)

            advance(stageI(0), 1000)
            for g in range(17):
                nxt = stageI(g + 1) if g + 1 < 17 else None
                stageII(g, nxt, (slot_nch[g + 1] + 1) if g + 1 < 17 else 0)
            S.barrier()
        idx_scope.__exit__(None, None, None)

        d1w_scope = S.scope()
        d1w_scope.__enter__()
        wa = S.sbuf("wa", [128, 8, 1024], BF16)
        wb = S.sbuf("wb", [128, 8, 1024], BF16)
        wo = S.sbuf("wo", [128, 8, 1024], BF16)
        for p in range(2):
            with S.scope():
                KT = S.sbuf("KT", [128, 4, 8192], BF16)
                Vv = S.sbuf("Vv", [128, 64, 4, 129], BF16)
                QT = [S.sbuf("QT%d" % i, [128, 4, 128], BF16) for i in range(2)]
                mTi = [S.sbuf("mTi%d" % i, [128, 8, 128], BF16) for i in range(3)]
                Eb = [S.sbuf("Eb%d" % i, [128, 4, 128], BF16) for i in range(3)]
                Pb = [S.sbuf("Pb%d" % i, [128, 4, 128], BF16) for i in range(3)]
                rinv = S.sbuf("rinv", [128, 4], F32)
                bo = [S.sbuf("bo%d" % i, [128, 4, 128], BF16) for i in range(2)]
                boTt = [S.sbuf("boTt%d" % i, [128, 4, 128], BF16) for i in range(2)]
                Sps = [S.psum("Sps%d" % i, [128, 4, 128]) for i in range(3)]
                Ops = [S.psum("Ops%d" % i, [128, 2, 129]) for i in range(4)]
                trb = S.psum("trb", [128, 4, 128], BF16)
                for c in range(16):
                    DMA("pool", KT[:, :, c * 512:(c + 1) * 512], KT_s[:, 4 * p:4 * p + 4, c * 512:(c + 1) * 512],
                        "lk%d" % (c % 4), [], ["KT%d" % c])
                    DMA("pool", Vv[:, 4 * c:4 * c + 4, :, :].rearrange("p b h d -> p b (h d)"),
                        V_s[:, 4 * c:4 * c + 4, 4 * p * 129:(4 * p + 4) * 129], "lv%d" % (c % 4), [], ["Vv%d" % c])
                iters = [(g, sb) for g in range(17) for sb in range(slot_nch[g] * 4)]
                groups = [(g, sb) for (g, sb) in iters if sb % 8 == 0]
                gidx = {gs: i for i, gs in enumerate(groups)}
                LOOK = 2
                deferred = []

                def load_q(g):
                    nt = slot_nt[g]
                    DMA("pool", QT[g % 2][:, :, 0:nt], qT_s[:, 4 * p:4 * p + 4, g * 128:g * 128 + nt],
                        "lq%d" % (g % 2), [], ["QT%d" % (g % 2)])

                def load_m(i):
                    g, sb = groups[i]
                    nt = slot_nt[g]
                    nb8 = min(8, slot_nch[g] * 4 - sb)
                    b0 = slot_blk0[g] + sb
                    DMA("pool", mTi[i % 3][:, 0:nb8, 0:nt], mT_s[:, b0:b0 + nb8, 0:nt], "lm%d" % (i % 3), [],
                        ["mTi%d" % (i % 3)])

                def stage1(it):
                    g, sb = iters[it]
                    nt = slot_nt[g]
                    c = sb // 4
                    if sb == 0:
                        if g == 0:
                            load_q(0)
                        if g + 1 < 17:
                            load_q(g + 1)
                        if p == 1 and g == 9:
                            DMA("pool", wa[:], wview(w_a, 0, 1024), "w0", [], ["wa"])
                            DMA("pool", wb[:], wview(w_b, 0, 1024), "w1", [], ["wb"])
                            DMA("pool", wo[:], wview(w_o, 0, 1024), "w4", [], ["wo"])
                    if sb % 8 == 0:
                        i = gidx[(g, sb)]
                        if i == 0:
                            load_m(0)
                        if i + 1 < len(groups):
                            load_m(i + 1)
                    i = gidx[(g, sb - sb % 8)]
                    mt, mtid = mTi[i % 3], "mTi%d" % (i % 3)
                    qt, qid = QT[g % 2], "QT%d" % (g % 2)
                    sp_, spid = Sps[it % 3], "Sps%d" % (it % 3)
                    eb, ebid = Eb[it % 3], "Eb%d" % (it % 3)
                    pb, pbid = Pb[it % 3], "Pb%d" % (it % 3)
                    for h in range(4):
                        MM(sp_[:, h, 0:nt], KT[:, h, sb * 128:(sb + 1) * 128], qt[:, h, 0:nt], True, True,
                           ["KT%d" % c, qid], [spid], skip_group_check=True)
                    ACT(eb[:, :, 0:nt], sp_[:, :, 0:nt], AF.Exp, [spid], [ebid])
                    TT("dve", pb[:, :, 0:nt], eb[:, :, 0:nt],
                       mt[:, sb % 8, 0:nt].unsqueeze(1).to_broadcast([128, 4, nt]), ALU.mult, [ebid, mtid], [pbid])

                def stage2(it):
                    g, sb = iters[it]
                    nt = slot_nt[g]
                    nsb = slot_nch[g] * 4
                    c = sb // 4
                    pb, pbid = Pb[it % 3], "Pb%d" % (it % 3)
                    o0, o0id = Ops[(g % 2) * 2], "Ops%d" % ((g % 2) * 2)
                    o1, o1id = Ops[(g % 2) * 2 + 1], "Ops%d" % ((g % 2) * 2 + 1)
                    for h in range(4):
                        ob, obid = (o0, o0id) if h < 2 else (o1, o1id)
                        MM(ob[0:nt, h % 2, :], pb[:, h, 0:nt], Vv[:, sb, h, :], sb == 0 and h % 2 == 0,
                           sb == nsb - 1, [pbid, "Vv%d" % c], [obid], skip_group_check=True)
                    if sb == nsb - 1:
                        bt, btid = bo[g % 2], "bo%d" % (g % 2)
                        btT, btTid = boTt[g % 2], "boTt%d" % (g % 2)
                        S.op("dve", lambda e: e.reciprocal(rinv[0:nt, 0:2], o0[0:nt, :, 128]), r=[o0id], w=["rinv"])
                        S.op("dve", lambda e: e.reciprocal(rinv[0:nt, 2:4], o1[0:nt, :, 128]), r=[o1id], w=["rinv"])
                        for h in range(4):
                            ob, obid = (o0, o0id) if h < 2 else (o1, o1id)
                            TS("dve", bt[0:nt, h, :], ob[0:nt, h % 2, 0:128], rinv[0:nt, h:h + 1], None, ALU.mult,
                               None, [obid, "rinv"], [btid])

                        def fin():
                            for h in range(4):
                                TR(trb[:, h, 0:nt], bt[0:nt, h, :], ident[0:nt, 0:nt], [btid, "ident"], ["trb"])
                            CP("act", btT[:, :, 0:nt], trb[:, :, 0:nt], ["trb"], [btTid])
                            DMA("sp", boT_s[:, 4 * p:4 * p + 4, g * 128:g * 128 + nt], btT[:, :, 0:nt],
                                "sb%d" % (g % 2), [btTid], ["boT_s"])
                        deferred.append((it + 6, fin))

                for k in range(len(iters) + LOOK):
                    if k < len(iters):
                        stage1(k)
                    if k - LOOK >= 0:
                        stage2(k - LOOK)
                    while deferred and deferred[0][0] <= k - LOOK:
                        deferred.pop(0)[1]()
                while deferred:
                    deferred.pop(0)[1]()
                S.barrier()

        def layernorm_fm(z, zid, n, gcol, bcol, outt, outid, sq, sqid, psm, psq, tmp):
            mean_sb, m2, rstd = tmp
            for kc in range(8):
                ACT(sq[:, kc, 0:n], z[:, kc, 0:n], AF.Square, [zid], [sqid])
            for kc in range(8):
                MM(psm[:, 0:n], ones_f[:], z[:, kc, 0:n], kc == 0, kc == 7, ["ones_f", zid], ["psm"])
            for kc in range(8):
                MM(psq[:, 0:n], ones_f[:], sq[:, kc, 0:n], kc == 0, kc == 7, ["ones_f", sqid], ["psq"])
            CP("act", mean_sb[:, 0:n], psm[:, 0:n], ["psm"], ["mean_sb"])
            TT("pool", m2[:, 0:n], mean_sb[:, 0:n], mean_sb[:, 0:n], ALU.mult, ["mean_sb"], ["m2"])
            TT("dve", rstd[:, 0:n], psq[:, 0:n], m2[:, 0:n], ALU.subtract, ["psq", "m2"], ["rstd"])
            TS("dve", rstd[:, 0:n], rstd[:, 0:n], 0.0, EPS, ALU.max, ALU.add, ["rstd"], ["rstd"])
            S.op("act", lambda e: e.sqrt(rstd[:, 0:n], rstd[:, 0:n]), r=["rstd"], w=["rstd"])
            S.op("dve", lambda e: e.reciprocal(rstd[:, 0:n], rstd[:, 0:n]), r=["rstd"], w=["rstd"])
            for kc in range(8):
                eng = "dve" if kc % 2 == 0 else "pool"
                TT(eng, sq[:, kc, 0:n], z[:, kc, 0:n], mean_sb[:, 0:n], ALU.subtract, [zid, "mean_sb"], [sqid])
                TT(eng, sq[:, kc, 0:n], sq[:, kc, 0:n], rstd[:, 0:n], ALU.mult, [sqid, "rstd"], [sqid])
                TS(eng, outt[:, kc, 0:n], sq[:, kc, 0:n], fvec[:, gcol + kc:gcol + kc + 1],
                   fvec[:, bcol + kc:bcol + kc + 1], ALU.mult, ALU.add, [sqid, "fvec"], [outid])

        with S.scope():
            wga = S.sbuf("wga", [128, 8, 1024], BF16)
            wgb = S.sbuf("wgb", [128, 8, 1024], BF16)
            aTi = [S.sbuf("aTi%d" % i, [128, 8, 512], BF16) for i in range(2)]
            bTi = [S.sbuf("bTi%d" % i, [128, 8, 512], BF16) for i in range(2)]
            xbD = [S.sbuf("xbD%d" % i, [128, 8, 512], BF16) for i in range(2)]
            xfD = [S.sbuf("xfD0", [128, 8, 512], F32)] * 2
            sga2 = [S.sbuf("sga%d" % i, [128, 512], F32) for i in range(2)]
            sgb2 = [S.sbuf("sgb%d" % i, [128, 512], F32) for i in range(2)]
            t12 = [S.sbuf("t1%d" % i, [128, 512], F32) for i in range(2)]
            t22 = [S.sbuf("t2%d" % i, [128, 512], F32) for i in range(2)]
            mixT = S.sbuf("mixT", [128, 8, 512], BF16)
            z1 = S.sbuf("z1", [128, 8, 512], F32)
            sq1 = S.sbuf("sq1", [128, 8, 512], F32)
            mean_sb = S.sbuf("mean_sb", [128, 512], F32)
            m2 = S.sbuf("m2", [128, 512], F32)
            rstd = S.sbuf("rstd", [128, 512], F32)
            psD = [S.psum("psD%d" % i, [128, 512]) for i in range(6)]
            psm = S.psum("psm", [128, 512])
            psq = S.psum("psq", [128, 512])
            DMA("pool", wga[:], wview(w_in, C_GA, 1024), "w2", [], ["wga"])
            DMA("pool", wgb[:], wview(w_in, C_GB, 1024), "w3", [], ["wgb"])
            protD = Rot([(psD[i], "psD%d" % i) for i in range(6)])
            pending1 = None
            for T in range(5):
                n = 512 if T < 4 else 32
                tok0 = T * 512
                i2 = T % 2
                DMA("pool", aTi[i2][:, :, 0:n], aT_s[:, :, tok0:tok0 + n], "la%d" % i2, [], ["aTi%d" % i2])
                DMA("pool", bTi[i2][:, :, 0:n], boT_s[:, :, tok0:tok0 + n], "lb%d" % i2, [], ["bTi%d" % i2])
                DMA("pool", xbD[i2][:, :, 0:n], wview(xT_own, tok0, n), "xA%d" % i2, [], ["xbD%d" % i2])
                DMA("sp", xfD[i2][:, :, 0:n], wview(xT_own, tok0, n), "xF0", [], ["xfD0"])
                for m in range(8):
                    pA, pAid = protD.next()
                    pBb, pBid = protD.next()
                    pGA, pGAid = protD.next()
                    pGB, pGBid = protD.next()
                    for kc in range(8):
                        MM(pA[:, 0:n], wa[:, kc, m * 128:(m + 1) * 128], aTi[i2][:, kc, 0:n], kc == 0, kc == 7,
                           ["wa", "aTi%d" % i2], [pAid])
                    for kc in range(8):
                        MM(pBb[:, 0:n], wb[:, kc, m * 128:(m + 1) * 128], bTi[i2][:, kc, 0:n], kc == 0, kc == 7,
                           ["wb", "bTi%d" % i2], [pBid])
                    for kc in range(8):
                        MM(pGA[:, 0:n], wga[:, kc, m * 128:(m + 1) * 128], xbD[i2][:, kc, 0:n], kc == 0, kc == 7,
                           ["wga", "xbD%d" % i2], [pGAid])
                    for kc in range(8):
                        MM(pGB[:, 0:n], wgb[:, kc, m * 128:(m + 1) * 128], xbD[i2][:, kc, 0:n], kc == 0, kc == 7,
                           ["wgb", "xbD%d" % i2], [pGBid])
                    sga, sgb, t1, t2 = sga2[m % 2], sgb2[m % 2], t12[m % 2], t22[m % 2]
                    sfx = str(m % 2)
                    ACT(sga[:, 0:n], pGA[:, 0:n], AF.Sigmoid, [pGAid], ["sga" + sfx])
                    ACT(sgb[:, 0:n], pGB[:, 0:n], AF.Sigmoid, [pGBid], ["sgb" + sfx])
                    TT("dve", t1[:, 0:n], pA[:, 0:n], sga[:, 0:n], ALU.mult, [pAid, "sga" + sfx], ["t1" + sfx])
                    TT("dve", t2[:, 0:n], pBb[:, 0:n], sgb[:, 0:n], ALU.mult, [pBid, "sgb" + sfx], ["t2" + sfx])
                    TT("pool", mixT[:, m, 0:n], t1[:, 0:n], t2[:, 0:n], ALU.add, ["t1" + sfx, "t2" + sfx], ["mixT"])
                    if m == 1 and pending1 is not None:
                        pending1()
                        pending1 = None
                for nn in range(8):
                    pY, pYid = protD.next()
                    for kc in range(8):
                        MM(pY[:, 0:n], wo[:, kc, nn * 128:(nn + 1) * 128], mixT[:, kc, 0:n], kc == 0, kc == 7,
                           ["wo", "mixT"], [pYid])
                    STT(z1[:, nn, 0:n], xfD[i2][:, nn, 0:n], ALPHA, pY[:, 0:n], ALU.mult, ALU.add,
                        ["xfD0", pYid], ["z1"])

                def fin1(n=n, tok0=tok0):
                    layernorm_fm(z1, "z1", n, 0, 8, sq1, "sq1", sq1, "sq1", psm, psq, (mean_sb, m2, rstd))
                    DMA("sp", h1T_s[:, :, tok0:tok0 + n], sq1[:, :, 0:n], "sh0", ["sq1"], ["h1T_s"])
                pending1 = fin1
            pending1()
            S.barrier()
        d1w_scope.__exit__(None, None, None)

        with S.scope():
            TW = 256
            wup = S.sbuf("wup", [128, 8, 5632], BF16)
            wdn = S.sbuf("wdn", [128, 22, 1024], BF16)
            upH = S.sbuf("upH", [128, 44, 32], F32)
            validH = S.sbuf("validH", [128, 32], F32)
            h1f2 = [S.sbuf("h1f%d" % i, [128, 8, TW], F32) for i in range(2)]
            h1b2 = [S.sbuf("h1b%d" % i, [128, 8, TW], BF16) for i in range(2)]
            h1f, h1b = h1f2[1], h1b2[1]
            ext = [S.sbuf("ext%d" % i, [128, 2, 130], F32) for i in range(4)]
            yg2 = [S.sbuf("yg%d" % i, [128, 2, 128], F32) for i in range(2)]
            yv2 = [S.sbuf("yv%d" % i, [128, 2, 128], F32) for i in range(2)]
            sgt2 = [S.sbuf("sgt%d" % i, [128, 2, 128], F32) for i in range(2)]
            aF = S.sbuf("aF", [128, 22, TW], BF16)
            sq2 = S.sbuf("sq2", [128, 8, TW], F32)
            mean_sb = S.sbuf("mean_sb2", [128, TW], F32)
            m2 = S.sbuf("m2b", [128, TW], F32)
            rstd = S.sbuf("rstd2", [128, TW], F32)
            psU = [S.psum("psU%d" % i, [128, 512]) for i in range(4)]
            psF = [S.psum("psF%d" % i, [128, 512]) for i in range(2)]
            psm = S.psum("psm2", [128, 512])
            psq = S.psum("psq2", [128, 512])
            for q4 in (0, 2, 1, 3):
                DMA("pool", wup[:, :, q4 * 1408:(q4 + 1) * 1408], wview(w_up, q4 * 1408, 1408), "w%d" % q4, [],
                    ["wup%d" % q4])

            def wupid(ch):
                return "wup%d" % (ch // 11)
            DMA("pool", wdn[:], w_dn.rearrange("(kc p) n -> p kc n", p=128), "w4", [], ["wdn"])
            DMA("sp", validH[:], validH_d, "c2", [], ["validH"])
            protU = Rot([(psU[i], "psU%d" % i) for i in range(4)])
            erot = Rot([(ext[i], "ext%d" % i) for i in range(4)])
            DMA("sp", h1f[:, :, 0:32], h1T_s[:, :, 2048:2080], "lh1", [], ["h1f1"])
            CP("dve", h1b[:, :, 0:32], h1f[:, :, 0:32], ["h1f1"], ["h1b1"])

            def halo_chunk(ch):
                ps, pid = protU.next()
                for kc in range(8):
                    MM(ps[:, 0:32], wup[:, kc, ch * 128:(ch + 1) * 128], h1b[:, kc, 0:32], kc == 0, kc == 7,
                       [wupid(ch), "h1b1"], [pid])
                TT("dve", upH[:, ch, :], ps[:, 0:32], validH[:], ALU.mult, [pid, "validH"], ["upH%d" % ch])
            halo_chunk(0)
            halo_chunk(22)
            NT2 = 2048 // TW

            def load_h1(T):
                DMA("sp", h1f2[T % 2][:], h1T_s[:, :, T * TW:(T + 1) * TW], "lh%d" % (T % 2), [], ["h1f%d" % (T % 2)])

            load_h1(0)
            pending = None
            for T in range(NT2):
                hf, hfid = h1f2[T % 2], "h1f%d" % (T % 2)
                hb, hbid = h1b2[T % 2], "h1b%d" % (T % 2)
                for kc in range(8):
                    CP("pool" if kc % 2 else "dve", hb[:, kc, :], hf[:, kc, :], [hfid], [hbid])
                for i in range(22):
                    yg, yv, sgt = yg2[i % 2], yv2[i % 2], sgt2[i % 2]
                    sfx = str(i % 2)
                    if T == 0 and i + 1 < 22:
                        halo_chunk(i + 1)
                        halo_chunk(22 + i + 1)
                    for which, ytile, yid in ((0, yg, "yg" + sfx), (1, yv, "yv" + sfx)):
                        ch = i + 22 * which
                        ps, pid = protU.next()
                        for kc in range(8):
                            MM(ps[:, 0:TW], wup[:, kc, ch * 128:(ch + 1) * 128], hb[:, kc, :], kc == 0, kc == 7,
                               [wupid(ch), hbid], [pid])
                        ex, exid = erot.next()
                        psv = ps[:, 0:TW].rearrange("p (b t) -> p b t", b=2)
                        CP("act", ex[:, :, 2:130], psv, [pid], [exid])
                        ACT(ytile[:], psv, AF.Identity, [pid, "fvec"], [yid], scale=fvec[:, 120 + ch:121 + ch],
                            bias=fvec[:, 164 + ch:165 + ch])
                        CP("pool", ex[:, :, 0:2], upH[:, ch, 4 * T:4 * T + 4].rearrange("p (b t) -> p b t", b=2),
                           ["upH%d" % ch], [exid])
                        STT(ytile[:], ex[:, :, 1:129], fvec[:, 76 + ch:77 + ch], ytile[:], ALU.mult, ALU.add,
                            [exid, "fvec", yid], [yid])
                        STT(ytile[:], ex[:, :, 0:128], fvec[:, 32 + ch:33 + ch], ytile[:], ALU.mult, ALU.add,
                            [exid, "fvec", yid], [yid])
                    ACT(sgt[:], yg[:], AF.Silu, ["yg" + sfx], ["sgt" + sfx])
                    TT("pool", aF[:, i, :].rearrange("p (b t) -> p b t", b=2), sgt[:], yv[:], ALU.mult,
                       ["sgt" + sfx, "yv" + sfx], ["aF"])
                    if i == 2 and pending is not None:
                        pending()
                        pending = None
                        if T + 1 < NT2:
                            load_h1(T + 1)
                if T == 0 and NT2 > 1:
                    load_h1(1)
                for nn in range(8):
                    pf, pfid = psF[nn % 2], "psF%d" % (nn % 2)
                    for kc in range(22):
                        MM(pf[:, 0:TW], wdn[:, kc, nn * 128:(nn + 1) * 128], aF[:, kc, :], kc == 0, kc == 21,
                           ["wdn", "aF"], [pfid])
                    STT(hf[:, nn, :], hf[:, nn, :], ALPHA, pf[:, 0:TW], ALU.mult, ALU.add, [hfid, pfid], [hfid])

                def fin(T=T, hf=hf, hfid=hfid):
                    layernorm_fm(hf, hfid, TW, 16, 24, sq2, "sq2", sq2, "sq2", psm, psq, (mean_sb, m2, rstd))
                    DMA("sp", outT[:, :, T * TW:(T + 1) * TW], sq2[:], "so0", ["sq2"], ["outT"])
                pending = fin
            pending()
            S.barrier()
        S.wait_all("sp")
        S.emit()
    return nc


_NC_CACHE = {}


def _host_inputs(x, w_in, ln_v_g, ln_v_b, w_spatial, b_spatial, ln_kidx_g, ln_kidx_b,
                 w_branch_a, w_branch_b, w_o, ln1_g, ln1_b, w_up, conv_w, conv_b, w_down, ln2_g, ln2_b):
    f32 = np.float32
    x = np.asarray(x, f32)

    def fm(v, kc):
        return np.asarray(v, f32).reshape(kc, 128).T

    fvec = np.zeros((128, 208), f32)
    fvec[:, 0:8] = fm(ln1_g[0], 8)
    fvec[:, 8:16] = fm(ln1_b[0], 8)
    fvec[:, 16:24] = fm(ln2_g[0], 8)
    fvec[:, 24:32] = fm(ln2_b[0], 8)
    for j in range(3):
        fvec[:, 32 + 44 * j:76 + 44 * j] = fm(conv_w[0, j], 44)
    fvec[:, 164:208] = fm(conv_b[0], 44)
    bvec = np.zeros((128, 2560), f32)
    bvec[:, 0:1024] = np.asarray(ln_v_g[0], f32)[None, :]
    bvec[:, 1024:2048] = np.asarray(ln_v_b[0], f32)[None, :]
    bvec[:, 2048:2304] = np.tile(np.asarray(ln_kidx_g[0], f32), 4)[None, :]
    bvec[:, 2304:2560] = np.tile(np.asarray(ln_kidx_b[0], f32), 4)[None, :]
    wsT = np.ascontiguousarray(np.transpose(np.asarray(w_spatial[0], f32), (2, 0, 1)))
    s_i = np.arange(128)
    trimask = (s_i[:, None] <= s_i[None, :]).astype(f32)
    common = {
        "w_in": np.ascontiguousarray(np.asarray(w_in[0], f32)),
        "w_a": np.ascontiguousarray(np.asarray(w_branch_a[0], f32)),
        "w_b": np.ascontiguousarray(np.asarray(w_branch_b[0], f32)),
        "w_o": np.ascontiguousarray(np.asarray(w_o[0], f32)),
        "w_up": np.ascontiguousarray(np.asarray(w_up[0], f32)),
        "w_dn": np.ascontiguousarray(np.asarray(w_down[0], f32)),
        "wsT": wsT,
        "bs_row": np.ascontiguousarray(np.asarray(b_spatial[0], f32).reshape(1, 1024)),
        "fvec": fvec, "bvec": bvec,
        "ident": np.eye(128, dtype=f32),
        "sel": np.stack([np.stack([(np.arange(128) // 64 + 2 * pr == h).astype(f32) for pr in range(4)])
                         for h in range(8)]),
        "trimask": trimask,
        "pow2": np.tile((2.0 ** -np.arange(1, NBIS + 1)).astype(f32)[None, :], (128, 1)),
    }
    xT = [np.ascontiguousarray(x[b].T) for b in range(2)]
    in_maps = []
    for c in range(8):
        b, j = c // 4, c % 4
        own_cols = []
        for g in range(16):
            t0 = (4 * g + j) * 128
            own_cols.append(np.arange(t0, t0 + 128))
        halo_pos = []
        for g in range(16):
            t0 = (4 * g + j) * 128
            halo_pos += [t0 - 2, t0 - 1]
        halo_pos = np.array(halo_pos)
        own_idx = np.concatenate(own_cols)
        xT_own = np.zeros((1024, NOWN), f32)
        xT_own[:, 0:2048] = xT[b][:, own_idx]
        hv = halo_pos >= 0
        xT_own[:, 2048:][:, hv] = xT[b][:, halo_pos[hv]]
        xT_prev = np.zeros((1024, 2048), f32)
        for g in range(16):
            pb = 4 * g + j - 1
            if pb >= 0:
                xT_prev[:, g * 128:(g + 1) * 128] = xT[b][:, pb * 128:(pb + 1) * 128]
        kk = np.arange(512)
        rblk, ks = kk // 128, kk % 128
        t = np.arange(128)
        allowed = (rblk[None, :] < j) | ((rblk[None, :] == j) & (ks[None, :] <= t[:, None]))
        pen_last = np.where(allowed, 0.0, NEG).astype(f32)
        sidx = np.arange(8192)
        hp = np.where(halo_pos >= 0, halo_pos, 0)
        pen_halo = np.where(sidx[None, :] <= hp[:, None], 0.0, NEG).astype(f32)
        validH = np.tile(hv.astype(f32)[None, :], (128, 1))
        m = dict(common)
        m.update({"xT_all": xT[b], "xT_own": xT_own, "xT_prev": xT_prev, "pen_last": pen_last,
                  "pen_halo": pen_halo, "validH": validH})
        in_maps.append(m)
    return in_maps


def kernel(**inputs):
    in_maps = _host_inputs(**inputs)
    if "nc" not in _NC_CACHE:
        _NC_CACHE["nc"] = build_program()
    nc = _NC_CACHE["nc"]
    res = run_bass_kernel_spmd(nc, in_maps, core_ids=list(range(8)))
    out = np.zeros((2, 8192, 1024), np.float32)
    for c in range(8):
        b, j = c // 4, c % 4
        o = np.asarray(res.results[c]["outT"], np.float32)
        o = o.transpose(2, 1, 0).reshape(2048, 1024)
        for g in range(16):
            t0 = (4 * g + j) * 128
            out[b, t0:t0 + 128, :] = o[g * 128:(g + 1) * 128, :]
    return out
```

```python
import contextlib
import numpy as np
import concourse.bass as bass
import concourse.mybir as mybir
from concourse.bass_utils import run_bass_kernel_spmd

F32 = mybir.dt.float32
BF16 = mybir.dt.bfloat16
AF = mybir.ActivationFunctionType
ALU = mybir.AluOpType
AX = mybir.AxisListType

NBIS = 14
ALPHA = 2.0 ** 0.25
EPS = 1e-5
NEG = -1.0e30
NOWN = 2080
C_U, C_V, C_Q, C_K, C_VV, C_QI, C_KI, C_WI, C_GA, C_GB = 0, 1024, 2048, 3072, 4096, 5120, 5632, 5696, 5704, 6728
NBLK_MASK = 4 * 136 + 64


class Sched:
    ENG = ("pe", "act", "dve", "pool", "sp")

    def __init__(self, nc):
        self.nc = nc
        self.stack = contextlib.ExitStack()
        self.ops = {e: [] for e in self.ENG}
        self.sems = {}
        self.ecnt = {e: 0 for e in self.ENG}
        self.dcnt = {}
        self.dlast = {}
        self.lastw = {}
        self.readers = {}
        self.waited = {e: {} for e in self.ENG}
        self.scopes = []
        self.needed = {e: set() for e in self.ENG}
        self.rawmap = {e: {} for e in self.ENG}
        self.sig = {e: 0 for e in self.ENG}
        for e in self.ENG:
            self.sems["E" + e] = self.stack.enter_context(nc.semaphore("sem_e_" + e))

    @contextlib.contextmanager
    def scope(self):
        st = contextlib.ExitStack()
        self.scopes.append(st)
        try:
            yield st
        finally:
            self.scopes.pop()
            st.close()

    def sbuf(self, name, shape, dt):
        self.uid = getattr(self, "uid", 0) + 1
        return self.scopes[-1].enter_context(self.nc.sbuf_tensor("s%d_%s" % (self.uid, name), list(shape), dt))

    def psum(self, name, shape, dt=F32):
        self.uid = getattr(self, "uid", 0) + 1
        return self.scopes[-1].enter_context(self.nc.psum_tensor("p%d_%s" % (self.uid, name), list(shape), dt))

    def _dsem(self, key):
        k = "D" + key
        if k not in self.sems:
            self.sems[k] = self.stack.enter_context(self.nc.semaphore("sem_d_" + key))
            self.dcnt[k] = 0
        return k

    def op(self, eng, fn, r=(), w=(), dma=None):
        deps = []
        for b in list(r) + list(w):
            t = self.lastw.get(b)
            if t is not None:
                deps.append(t)
        for b in w:
            deps.extend(self.readers.get(b, ()))
        if dma is not None:
            k = self._dsem(dma)
            prev = self.dlast.get(k)
            if prev is not None:
                deps.append(prev)
            self.dcnt[k] += 16
            tok = (k, self.dcnt[k], None)
            self.dlast[k] = tok
        else:
            self.ecnt[eng] += 1
            tok = ("E" + eng, self.ecnt[eng], eng)
        waits = {}
        for (s, v, e2) in deps:
            if e2 == eng and eng == "pe":
                continue
            if self.waited[eng].get(s, 0) >= v:
                continue
            waits[s] = max(waits.get(s, 0), v)
        for s, v in waits.items():
            self.waited[eng][s] = v
            if s[0] == "E":
                self.needed[s[1:]].add(v)
        self.ops[eng].append((sorted(waits.items()), fn, tok))
        for b in w:
            self.lastw[b] = tok
            self.readers[b] = []
        for b in r:
            self.readers.setdefault(b, []).append(tok)
        return tok

    def wait_all(self, eng):
        waits = {}
        for e in self.ENG:
            if self.ecnt[e] == 0 or (e == eng and e == "pe"):
                continue
            if self.waited[eng].get("E" + e, 0) < self.ecnt[e]:
                waits["E" + e] = self.ecnt[e]
        for k, v in self.dcnt.items():
            if v and self.waited[eng].get(k, 0) < v:
                waits[k] = v
        for s, v in waits.items():
            self.waited[eng][s] = v
            if s[0] == "E":
                self.needed[s[1:]].add(v)
        self.ops[eng].append((sorted(waits.items()), None, None))

    def flush(self):
        nc = self.nc
        sems = self.sems
        rawmap = self.rawmap
        for e in self.ENG:
            for waits, fn, tok in self.ops[e]:
                if fn is not None and tok[2] is not None and tok[1] in self.needed[e]:
                    self.sig[e] += 1
                    rawmap[e][tok[1]] = self.sig[e]
        with nc.Block() as block:
            decos = {"pe": block.tensor, "act": block.scalar, "dve": block.vector,
                     "pool": block.gpsimd, "sp": block.sync}
            for e in self.ENG:
                ops = self.ops[e]

                def body(engobj, ops=ops, e=e):
                    for waits, fn, tok in ops:
                        for s, v in waits:
                            if s[0] == "E":
                                v = rawmap[s[1:]][v]
                            engobj.wait_ge(sems[s], v)
                        if fn is None:
                            continue
                        ins = fn(engobj)
                        if tok[2] is None:
                            ins.then_inc(sems[tok[0]], 16)
                        elif tok[1] in rawmap[e]:
                            ins.then_inc(sems[tok[0]], 1)

                decos[e](body)
        self.ops = {e: [] for e in self.ENG}

    def barrier(self):
        for e in self.ENG:
            self.wait_all(e)
        self.lastw.clear()
        self.readers.clear()
        self.flush()

    def emit(self):
        self.flush()
        self.stack.close()


class Rot:
    def __init__(self, items):
        self.items = items
        self.i = 0

    def next(self):
        it = self.items[self.i % len(self.items)]
        self.i += 1
        return it


def build_program():
    nc = bass.Bass("TRN2", target_bir_lowering=False)
    S = Sched(nc)

    def din(name, shape):
        return nc.dram_tensor(name, list(shape), F32, kind="ExternalInput").ap()

    xT_all = din("xT_all", [1024, 8192])
    xT_own = din("xT_own", [1024, NOWN])
    xT_prev = din("xT_prev", [1024, 2048])
    w_in = din("w_in", [1024, 7752])
    w_a = din("w_a", [1024, 1024])
    w_b = din("w_b", [1024, 1024])
    w_o = din("w_o", [1024, 1024])
    w_up = din("w_up", [1024, 5632])
    w_dn = din("w_dn", [2816, 1024])
    wsT_d = din("wsT", [128, 8, 128])
    bs_row = din("bs_row", [1, 1024])
    fvec_d = din("fvec", [128, 208])
    bvec_d = din("bvec", [128, 2560])
    ident_d = din("ident", [128, 128])
    trimask_d = din("trimask", [128, 128])
    pen_last_d = din("pen_last", [128, 512])
    pen_halo_d = din("pen_halo", [32, 8192])
    validH_d = din("validH", [128, 32])
    pow2_d = din("pow2", [128, NBIS])
    sel_d = din("sel", [8, 4, 128])
    outT = nc.dram_tensor("outT", [128, 8, 2048], F32, kind="ExternalOutput").ap()

    KT_s = nc.dram_tensor("KT_s", [128, 8, 8192], BF16).ap()
    V_s = nc.dram_tensor("V_s", [128, 64, 8 * 129], BF16).ap()
    qT_s = nc.dram_tensor("qT_s", [128, 8, NOWN], BF16).ap()
    mT_s = nc.dram_tensor("mT_s", [128, NBLK_MASK, 128], BF16).ap()
    aT_s = nc.dram_tensor("aT_s", [128, 8, NOWN], BF16).ap()
    boT_s = nc.dram_tensor("boT_s", [128, 8, NOWN], BF16).ap()
    h1T_s = nc.dram_tensor("h1T_s", [128, 8, NOWN], F32).ap()

    def wview(w, c0, n):
        return w[:, c0:c0 + n].rearrange("(kc p) n -> p kc n", p=128)

    def MM(out, lhsT, rhs, start, stop, r, w, **kw):
        S.op("pe", lambda e: e.matmul(out, lhsT=lhsT, rhs=rhs, start=start, stop=stop, **kw), r=r, w=w)

    def TR(out, in_, idn, r, w):
        S.op("pe", lambda e: e.transpose(out, in_, idn), r=r, w=w)

    def ACT(out, in_, func, r, w, **kw):
        S.op("act", lambda e: e.activation(out=out, in_=in_, func=func, **kw), r=r, w=w)

    def TS(eng, out, in0, s1, s2, op0, op1, r, w, accum=None):
        if accum is not None:
            S.op(eng, lambda e: e.tensor_scalar(out=out, in0=in0, scalar1=s1, scalar2=s2, op0=op0, op1=op1,
                                                accum_out=accum), r=r, w=w)
        elif op1 is None:
            S.op(eng, lambda e: e.tensor_scalar(out=out, in0=in0, scalar1=s1, scalar2=None, op0=op0), r=r, w=w)
        else:
            S.op(eng, lambda e: e.tensor_scalar(out=out, in0=in0, scalar1=s1, scalar2=s2, op0=op0, op1=op1),
                 r=r, w=w)

    def TT(eng, out, in0, in1, op, r, w):
        S.op(eng, lambda e: e.tensor_tensor(out=out, in0=in0, in1=in1, op=op), r=r, w=w)

    def STT(out, in0, scalar, in1, op0, op1, r, w):
        S.op("dve", lambda e: e.scalar_tensor_tensor(out=out, in0=in0, scalar=scalar, in1=in1, op0=op0, op1=op1),
             r=r, w=w)

    def CP(eng, out, in_, r, w):
        if eng == "act":
            S.op("act", lambda e: e.copy(out, in_), r=r, w=w)
        else:
            S.op(eng, lambda e: e.tensor_copy(out, in_), r=r, w=w)

    def DMA(eng, out, in_, key, r, w):
        S.op(eng, lambda e: e.dma_start(out=out, in_=in_), r=r, w=w, dma=key)

    def MEMSET(eng, ap, val, w):
        S.op(eng, lambda e: e.memset(ap, val), r=(), w=w)

    with S.scope():
        ident = S.sbuf("ident", [128, 128], BF16)
        ones_bf = S.sbuf("ones_bf", [128, 128], BF16)
        ones_f = S.sbuf("ones_f", [128, 128], F32)
        fvec = S.sbuf("fvec", [128, 208], F32)
        idx_scope = S.scope()
        idx_scope.__enter__()
        kiT2 = S.sbuf("kiT2", [128, 8192], BF16)
        qiT = S.sbuf("qiT", [128, 4, NOWN], BF16)
        absw = S.sbuf("absw", [128, 17, 8], F32)
        sgn = S.sbuf("sgn", [128, 17, 8], F32)

        DMA("pool", ident[:], ident_d, "c0", [], ["ident"])
        DMA("sp", fvec[:], fvec_d, "c1", [], ["fvec"])
        MEMSET("dve", ones_bf[:], 1.0, ["ones_bf"])
        MEMSET("dve", ones_f[:], 1.0 / 1024.0, ["ones_f"])
        MEMSET("dve", absw[:], 0.0, ["absw"])
        MEMSET("dve", sgn[:], 0.0, ["sgn"])

        bw_scope = S.scope()
        bw_scope.__enter__()
        wu = S.sbuf("wu", [128, 8, 1024], BF16)
        wvm = S.sbuf("wvm", [128, 8, 1024], BF16)
        wq = S.sbuf("wq", [128, 8, 1024], BF16)
        wqi = S.sbuf("wqi", [128, 8, 512], BF16)
        wwi = S.sbuf("wwi", [128, 8, 8], BF16)
        with S.scope():
            wk = S.sbuf("wk", [128, 8, 1024], BF16)
            wv = S.sbuf("wvv", [128, 8, 1024], BF16)
            wki = S.sbuf("wki", [128, 8, 64], BF16)
            kgb = S.sbuf("kgb", [128, 512], F32)
            xbA = [S.sbuf("xbA%d" % i, [128, 8, 512], BF16) for i in range(2)]
            ktA = [S.sbuf("ktA%d" % i, [128, 8, 512], BF16) for i in range(2)]
            vA = [S.sbuf("vA%d" % i, [128, 4, 8, 129], BF16) for i in range(2)]
            stA = S.sbuf("stA", [128, 4, 6], F32)
            mvA = S.sbuf("mvA", [128, 4, 2], F32)
            rsA = S.sbuf("rsA", [128, 4], F32)
            kinf = S.sbuf("kinf", [128, 4, 64], F32)
            kinb = S.sbuf("kinb", [128, 4, 128], BF16)
            psA = [S.psum("psA%d" % i, [128, 512]) for i in range(5)]
            psKi = S.psum("psKi", [128, 4, 64])
            psKt = S.psum("psKt", [128, 512], BF16)

            DMA("pool", xbA[0][:], wview(xT_all, 0, 512), "xA0", [], ["xbA0"])
            DMA("pool", wk[:], wview(w_in, C_K, 1024), "w0", [], ["wk"])
            DMA("pool", wki[:], wview(w_in, C_KI, 64), "w2", [], ["wki"])
            DMA("pool", wv[:], wview(w_in, C_VV, 1024), "w1", [], ["wvv"])
            DMA("sp", kgb[:], bvec_d[:, 2048:2560], "c2", [], ["kgb"])
            for i in range(2):
                MEMSET("dve", vA[i][:, :, :, 128:129], 1.0, ["vA%d" % i])
            prot = Rot([(psA[i], "psA%d" % i) for i in range(5)])
            evi = 0
            for T in range(16):
                xb, xid = xbA[T % 2], "xbA%d" % (T % 2)
                kt, ktid = ktA[T % 2], "ktA%d" % (T % 2)
                vt, vid = vA[T % 2], "vA%d" % (T % 2)
                if T > 0:
                    DMA("pool", xb[:], wview(xT_all, T * 512, 512), "xA%d" % (T % 2), [], [xid])
                if T == 2:
                    DMA("pool", wvm[:], wview(w_in, C_V, 1024), "w3", [], ["wvm"])
                    DMA("pool", wu[:], wview(w_in, C_U, 1024), "w4", [], ["wu"])
                if T == 4:
                    DMA("pool", wq[:], wview(w_in, C_Q, 1024), "w5", [], ["wq"])
                    DMA("pool", wqi[:], wview(w_in, C_QI, 512), "w6", [], ["wqi"])
                    DMA("pool", wwi[:], wview(w_in, C_WI, 8), "w7", [], ["wwi"])
                for h in range(8):
                    ps, pid = prot.next()
                    for kc in range(8):
                        MM(ps[:], wk[:, kc, h * 128:(h + 1) * 128], xb[:, kc, :], kc == 0, kc == 7,
                           ["wk", xid], [pid])
                    CP("act" if evi % 2 == 0 else "dve", kt[:, h, :], ps[:], [pid], [ktid])
                    evi += 1
                DMA("sp", KT_s[:, :, T * 512:(T + 1) * 512], kt[:], "sKT%d" % (T % 2), [ktid], ["KT_s"])
                for blk in range(4):
                    for half in range(2):
                        ps, pid = prot.next()
                        for kc in range(8):
                            MM(ps[:], xb[:, kc, blk * 128:(blk + 1) * 128], wv[:, kc, half * 512:(half + 1) * 512],
                               kc == 0, kc == 7, ["wvv", xid], [pid])
                        CP("act" if evi % 2 == 0 else "dve", vt[:, blk, half * 4:(half + 1) * 4, 0:128],
                           ps[:].rearrange("p (h d) -> p h d", h=4), [pid], [vid])
                        evi += 1
                DMA("sp", V_s[:, T * 4:(T + 1) * 4, :], vt[:].rearrange("p b h d -> p b (h d)"),
                    "sV%d" % (T % 2), [vid], ["V_s"])
                for blk in range(4):
                    for kc in range(8):
                        MM(psKi[:, blk, :], xb[:, kc, blk * 128:(blk + 1) * 128], wki[:, kc, :], kc == 0, kc == 7,
                           ["wki", xid], ["psKi"], skip_group_check=True)
                for blk in range(4):
                    S.op("dve", lambda e, blk=blk: e.bn_stats(stA[:, blk, :], psKi[:, blk, :]), r=["psKi"], w=["stA"])
                    S.op("dve", lambda e, blk=blk: e.bn_aggr(mvA[:, blk, :], stA[:, blk, :]), r=["stA"], w=["mvA"])
                TS("dve", rsA[:], mvA[:, :, 1], EPS, None, ALU.add, None, ["mvA"], ["rsA"])
                S.op("act", lambda e: e.sqrt(rsA[:], rsA[:]), r=["rsA"], w=["rsA"])
                S.op("dve", lambda e: e.reciprocal(rsA[:], rsA[:]), r=["rsA"], w=["rsA"])
                for blk in range(4):
                    TS("dve", kinf[:, blk, :], psKi[:, blk, :], mvA[:, blk, 0:1], rsA[:, blk:blk + 1],
                       ALU.subtract, ALU.mult, ["psKi", "mvA", "rsA"], ["kinf"])
                TT("dve", kinf[:].rearrange("p b f -> p (b f)"), kinf[:].rearrange("p b f -> p (b f)"),
                   kgb[:, 0:256], ALU.mult, ["kinf", "kgb"], ["kinf"])
                for dup in range(2):
                    TT("dve", kinb[:, :, dup * 64:(dup + 1) * 64], kinf[:],
                       kgb[:, 256:512].rearrange("p (b f) -> p b f", b=4), ALU.add, ["kinf", "kgb"], ["kinb"])
                for blk in range(4):
                    TR(psKt[:, blk * 128:(blk + 1) * 128], kinb[:, blk, :], ident[:], ["kinb", "ident"], ["psKt"])
                CP("act", kiT2[:, T * 512:(T + 1) * 512], psKt[:], ["psKt"], ["kiT2"])
            S.barrier()

        with S.scope():
            vgb = S.sbuf("vgb", [128, 2048], F32)
            wsf = S.sbuf("wsf", [128, 8, 128], F32)
            trif = S.sbuf("trif", [128, 128], F32)
            wsb = S.sbuf("wsb", [128, 8, 128], BF16)
            bsb = S.sbuf("bsb", [1, 1024], BF16)
            xbB = [S.sbuf("xbB%d" % i, [128, 8, 512], BF16) for i in range(2)]
            xpB = [S.sbuf("xpB%d" % i, [128, 8, 128], BF16) for i in range(2)]
            uT = [S.sbuf("uT%d" % i, [128, 8, 512], BF16) for i in range(2)]
            uTh = S.sbuf("uTh", [128, 8, 32], BF16)
            aTt = [S.sbuf("aTt%d" % i, [128, 8, 512], BF16) for i in range(2)]
            qTt = [S.sbuf("qTt%d" % i, [128, 8, 512], BF16) for i in range(2)]
            vg = [S.sbuf("vg%d" % i, [128, 1024], F32) for i in range(2)]
            vn = [S.sbuf("vn%d" % i, [128, 1024], BF16) for i in range(2)]
            stB = S.sbuf("stB", [128, 2, 6], F32)
            mvB = S.sbuf("mvB", [128, 2], F32)
            rsB = S.sbuf("rsB", [128, 1], F32)
            psB = [S.psum("psB%d" % i, [128, 512]) for i in range(4)]
            psSp = [S.psum("psSp%d" % i, [128, 4, 128]) for i in range(2)]
            psW = S.psum("psW", [128, 8])
            self_sel = S.sbuf("sel", [8, 4, 128], F32)
            absT = S.sbuf("absT", [8, 512], F32)
            wbc = S.sbuf("wbc", [128, 512], F32)
            DMA("sp", self_sel[:], sel_d, "c5", [], ["sel"])
            psSh = S.psum("psSh", [128, 8, 2])

            DMA("pool", bsb[:], bs_row, "w5", [], ["bsb"])
            DMA("sp", vgb[:], bvec_d[:, 0:2048], "c2", [], ["vgb"])
            DMA("sp", wsf[:], wsT_d, "c3", [], ["wsf"])
            DMA("sp", trif[:], trimask_d, "c4", [], ["trif"])
            for g in range(8):
                TT("dve", wsb[:, g, :], wsf[:, g, :], trif[:], ALU.mult, ["wsf", "trif"], ["wsb"])

            protB = Rot([(psB[i], "psB%d" % i) for i in range(4)])
            vcount = [0]

            def vpath(xsrc, xid):
                i = vcount[0] % 2
                vcount[0] += 1
                vgt, vgid, vnt, vnid = vg[i], "vg%d" % i, vn[i], "vn%d" % i
                for half in range(2):
                    ps, pid = protB.next()
                    for kc in range(8):
                        MM(ps[:], xsrc[:, kc, :], wvm[:, kc, half * 512:(half + 1) * 512], kc == 0, kc == 7,
                           ["wvm", xid], [pid])
                    ACT(vgt[:, half * 512:(half + 1) * 512], ps[:], AF.Gelu_apprx_tanh, [pid], [vgid])
                for half in range(2):
                    S.op("dve", lambda e, half=half: e.bn_stats(stB[:, half, :], vgt[:, half * 512:(half + 1) * 512]),
                         r=[vgid], w=["stB"])
                S.op("dve", lambda e: e.bn_aggr(mvB[:], stB[:].rearrange("p a b -> p (a b)")), r=["stB"], w=["mvB"])
                TS("dve", rsB[:], mvB[:, 1:2], EPS, None, ALU.add, None, ["mvB"], ["rsB"])
                S.op("act", lambda e: e.sqrt(rsB[:], rsB[:]), r=["rsB"], w=["rsB"])
                S.op("dve", lambda e: e.reciprocal(rsB[:], rsB[:]), r=["rsB"], w=["rsB"])
                TS("dve", vgt[:], vgt[:], mvB[:, 0:1], rsB[:, 0:1], ALU.subtract, ALU.mult,
                   [vgid, "mvB", "rsB"], [vgid])
                TT("pool", vgt[:], vgt[:], vgb[:, 0:1024], ALU.mult, [vgid, "vgb"], [vgid])
                TT("pool", vnt[:], vgt[:], vgb[:, 1024:2048], ALU.add, [vgid, "vgb"], [vnid])
                return vnt, vnid

            for T in range(5):
                n = 512 if T < 4 else 32
                tok0 = T * 512
                xb, xid = xbB[T % 2], "xbB%d" % (T % 2)
                DMA("pool", xb[:, :, 0:n], wview(xT_own, tok0, n), "xA%d" % (T % 2), [], [xid])
                if T < 4:
                    ut, uid = uT[T % 2], "uT%d" % (T % 2)
                else:
                    ut, uid = uTh, "uTh"
                at, aid = aTt[T % 2], "aTt%d" % (T % 2)
                qt, qid = qTt[T % 2], "qTt%d" % (T % 2)

                def do_u(cs):
                    for c in cs:
                        ps, pid = protB.next()
                        for kc in range(8):
                            MM(ps[:, 0:n], wu[:, kc, c * 128:(c + 1) * 128], xb[:, kc, 0:n], kc == 0, kc == 7,
                               ["wu", xid], [pid])
                        ACT(ut[:, c, 0:n], ps[:, 0:n], AF.Gelu_apprx_tanh, [pid], [uid])

                def do_q(hs):
                    for h in hs:
                        ps, pid = protB.next()
                        for kc in range(8):
                            MM(ps[:, 0:n], wq[:, kc, h * 128:(h + 1) * 128], xb[:, kc, 0:n], kc == 0, kc == 7,
                               ["wq", xid], [pid])
                        TS("dve", qt[:, h, 0:n], ps[:, 0:n], 128.0 ** -0.5, None, ALU.mult, None, [pid], [qid])
                    if hs[-1] == 7:
                        DMA("sp", qT_s[:, :, tok0:tok0 + n], qt[:, :, 0:n], "sq%d" % (T % 2), [qid], ["qT_s"])

                def do_qi():
                    pw, pwid = protB.next()
                    for kc in range(8):
                        MM(pw[0:8, 0:n], wwi[:, kc, :], xb[:, kc, 0:n], kc == 0, kc == 7, ["wwi", xid], [pwid])
                    ACT(absT[0:8, 0:n], pw[0:8, 0:n], AF.Abs, [pwid], ["absT"], scale=8.0 ** -0.5 * 64.0 ** -0.5)
                    for pr in range(4):
                        ps, pid = protB.next()
                        for kc in range(8):
                            MM(ps[:, 0:n], wqi[:, kc, pr * 128:(pr + 1) * 128], xb[:, kc, 0:n], kc == 0, kc == 7,
                               ["wqi", xid], [pid])
                        pbc, pbcid = protB.next()
                        MM(pbc[:, 0:n], self_sel[0:8, pr, :], absT[0:8, 0:n], True, True, ["sel", "absT"], [pbcid])
                        CP("act", wbc[:, 0:n], pbc[:, 0:n], [pbcid], ["wbc"])
                        TT("dve", qiT[:, pr, tok0:tok0 + n], ps[:, 0:n], wbc[:, 0:n], ALU.mult, [pid, "wbc"], ["qiT"])
                    nb = 4 if T < 4 else 1
                    for blk in range(nb):
                        nt = 128 if T < 4 else 32
                        slot = T * 4 + blk
                        for kc in range(8):
                            MM(psW[0:nt, :], xb[:, kc, blk * 128:blk * 128 + nt], wwi[:, kc, :], kc == 0, kc == 7,
                               ["wwi", xid], ["psW"])
                        ACT(sgn[0:nt, slot, :], psW[0:nt, :], AF.Sign, ["psW"], ["sgn"])

                def spatial(blk, vnt, vnid):
                    for cg in range(2):
                        sp_, spid = psSp[cg], "psSp%d" % cg
                        for cc in range(4):
                            c = cg * 4 + cc
                            MM(sp_[:, cc, :], vnt[:, c * 128:(c + 1) * 128], wsb[:, c, :], True, False,
                               [vnid, "wsb"], [spid], skip_group_check=True)
                            MM(sp_[:, cc, :], ones_bf[0:1, :], bsb[0:1, c * 128:(c + 1) * 128], False, True,
                               ["ones_bf", "bsb"], [spid], skip_group_check=True)
                        TT("dve", at[:, cg * 4:(cg + 1) * 4, blk * 128:(blk + 1) * 128], sp_[:],
                           ut[:, cg * 4:(cg + 1) * 4, blk * 128:(blk + 1) * 128], ALU.mult, [spid, uid], [aid])

                def spatial_h(g, vnt, vnid):
                    for c in range(8):
                        MM(psSh[:, c, :], vnt[:, c * 128:(c + 1) * 128], wsb[:, c, 126:128], True, False,
                           [vnid, "wsb"], ["psSh"], skip_group_check=True)
                        MM(psSh[:, c, :], ones_bf[0:1, :], bsb[0:1, c * 128 + 126:c * 128 + 128], False, True,
                           ["ones_bf", "bsb"], ["psSh"], skip_group_check=True)
                    TT("dve", at[:, :, 2 * g:2 * g + 2], psSh[:], ut[:, :, 2 * g:2 * g + 2], ALU.mult,
                       ["psSh", uid], [aid])

                if T < 4:
                    v0 = vpath(xb[:, :, 0:128], xid)
                    do_u(range(0, 8))
                    v1 = vpath(xb[:, :, 128:256], xid)
                    do_q(range(0, 8))
                    spatial(0, *v0)
                    v2 = vpath(xb[:, :, 256:384], xid)
                    do_qi()
                    spatial(1, *v1)
                    v3 = vpath(xb[:, :, 384:512], xid)
                    spatial(2, *v2)
                    spatial(3, *v3)
                else:
                    def load_prev(g):
                        DMA("pool", xpB[g % 2][:], wview(xT_prev, g * 128, 128), "xP%d" % (g % 2), [],
                            ["xpB%d" % (g % 2)])
                    load_prev(0)
                    load_prev(1)
                    vcur = vpath(xpB[0][:], "xpB0")
                    do_u(range(0, 8))
                    do_q(range(0, 8))
                    do_qi()
                    for g in range(16):
                        vnx = None
                        if g + 1 < 16:
                            vnx = vpath(xpB[(g + 1) % 2][:], "xpB%d" % ((g + 1) % 2))
                            if g + 2 < 16:
                                load_prev(g + 2)
                        spatial_h(g, *vcur)
                        vcur = vnx
                DMA("sp", aT_s[:, :, tok0:tok0 + n], at[:, :, 0:n], "sa%d" % (T % 2), [aid], ["aT_s"])
            S.barrier()
        bw_scope.__exit__(None, None, None)

        slot_nch = [g + 1 for g in range(16)] + [16]
        slot_nt = [128] * 16 + [32]
        slot_blk0 = []
        acc = 0
        for g in range(17):
            slot_blk0.append(acc)
            acc += 4 * slot_nch[g]
        assert acc == NBLK_MASK
        with S.scope():
            Irow = [S.sbuf("Irow%d" % i, [128, 8192], F32) for i in range(2)]
            maskb = [S.sbuf("maskb%d" % i, [128, 8192], BF16) for i in range(2)]
            Rb = [S.sbuf("Rb%d" % i, [128, 2, 512], BF16) for i in range(8)]
            Dm = [S.sbuf("Dm%d" % i, [128, 8, 128], BF16) for i in range(2)]
            identf = S.sbuf("identf", [128, 128], F32)
            penl = S.sbuf("penl", [128, 512], F32)
            penh = [S.sbuf("penh%d" % i, [32, 512], F32) for i in range(2)]
            pow2 = S.sbuf("pow2", [128, NBIS], F32)
            cmin = [S.sbuf("cmin%d" % i, [128, 16], F32) for i in range(2)]
            sm = S.sbuf("sm", [128, 16], F32)
            hwv = S.sbuf("hwv", [128, NBIS], F32)
            cand = [S.sbuf("cand%d" % i, [128, 1], F32) for i in range(2)]
            cnt = S.sbuf("cnt", [128, 1], F32)
            sv = S.sbuf("sv", [128, 1], F32)
            tau = S.sbuf("tau", [128, 1], F32)
            mTo = [S.sbuf("mTo%d" % i, [128, 8, 128], BF16) for i in range(2)]
            zps = [S.psum("zps%d" % i, [128, 2, 512]) for i in range(2)]
            Ips = [S.psum("Ips%d" % i, [128, 512]) for i in range(2)]
            trp = [S.psum("trp%d" % i, [128, 8, 128], BF16) for i in range(2)]

            DMA("sp", identf[:], ident_d, "c2", [], ["identf"])
            DMA("sp", penl[:], pen_last_d, "c3", [], ["penl"])
            DMA("sp", pow2[:], pow2_d, "c5", [], ["pow2"])
            junkA = S.sbuf("junkA", [128, 4096], BF16)
            sB = S.sbuf("sB", [128, 1], F32)
            tmpc = S.sbuf("tmpc", [128, 1], F32)
            rrot = Rot([(Rb[i], "Rb%d" % i) for i in range(8)])
            trc = [0]

            slot_order = [1, 3, 5, 7, 9, 11, 13, 15, 16, 14, 12, 10, 8, 6, 4, 2, 0]
            slot_pos = {g_: i_ for i_, g_ in enumerate(slot_order)}

            def stageI(g):
                nt, nch = slot_nt[g], slot_nch[g]
                tok0 = g * 128
                pq = slot_pos[g] % 2
                Ir, Iid = Irow[pq], "Irow%d" % pq
                dm, dmid = Dm[pq], "Dm%d" % pq
                cm, cmid = cmin[pq], "cmin%d" % pq
                for h in range(8):
                    TS("pool", dm[0:nt, h, 0:nt], identf[0:nt, 0:nt], sgn[0:nt, g, h:h + 1], 1.0, ALU.mult, ALU.mult,
                       ["identf", "sgn"], [dmid])

                def acc(c, rl):
                    ip, ipid = Ips[c % 2], "Ips%d" % (c % 2)
                    for h in range(8):
                        rb, rid = rl[h]
                        MM(ip[0:nt, :], dm[0:nt, h, 0:nt], rb[0:nt, :], h == 0, h == 7, [dmid, rid], [ipid])
                    TS("dve", Ir[0:nt, c * 512:(c + 1) * 512], ip[0:nt, :], 1.0, None, ALU.mult, ALU.min,
                       [ipid], [Iid, cmid], accum=cm[0:nt, c:c + 1])
                    if g == 16:
                        DMA("pool", penh[c % 2][:], pen_halo_d[:, c * 512:(c + 1) * 512], "ph%d" % (c % 2), [],
                            ["penh%d" % (c % 2)])
                        TT("pool", Ir[0:nt, c * 512:(c + 1) * 512], Ir[0:nt, c * 512:(c + 1) * 512],
                           penh[c % 2][0:nt, :], ALU.add, [Iid, "penh%d" % (c % 2)], [Iid])
                    elif c == nch - 1:
                        TT("pool", Ir[0:nt, c * 512:(c + 1) * 512], Ir[0:nt, c * 512:(c + 1) * 512], penl[0:nt, :],
                           ALU.add, [Iid, "penl"], [Iid])

                prev = None
                for c in range(nch):
                    if prev is not None:
                        acc(*prev)
                    rl = []
                    for pr in range(4):
                        zz, zid = zps[pr % 2], "zps%d" % (pr % 2)
                        MM(zz[0:nt, 0, :], qiT[0:64, pr, tok0:tok0 + nt], kiT2[0:64, c * 512:(c + 1) * 512], True, True,
                           ["qiT", "kiT2"], [zid], skip_group_check=True)
                        MM(zz[0:nt, 1, :], qiT[64:128, pr, tok0:tok0 + nt], kiT2[64:128, c * 512:(c + 1) * 512],
                           True, True, ["qiT", "kiT2"], [zid], tile_position=(64, 0), skip_group_check=True)
                        rb, rid = rrot.next()
                        ACT(rb[0:nt, :, :], zz[0:nt, :, :], AF.Relu, [zid], [rid])
                        rl.append((rb[:, 0, :], rid))
                        rl.append((rb[:, 1, :], rid))
                    prev = (c, rl)
                    yield
                acc(*prev)
                yield

            def advance(gen, k):
                if gen is None:
                    return
                for _ in range(k):
                    try:
                        next(gen)
                    except StopIteration:
                        return

            def stageII(g, nxt, nxt_n):
                nt, nch = slot_nt[g], slot_nch[g]
                n = nch * 512
                nA = 128 * max(1, int(round(0.42 * 4 * nch)))
                nD = n - nA
                pq = slot_pos[g] % 2
                Ir, Iid = Irow[pq], "Irow%d" % pq
                mk, mkid = maskb[pq], "maskb%d" % pq
                cm, cmid = cmin[pq], "cmin%d" % pq
                S.op("dve", lambda e: e.tensor_reduce(out=sm[0:nt, 0:1], in_=Ir[0:nt, 0:n], axis=AX.X, op=ALU.max),
                     r=[Iid], w=["sm"])
                S.op("dve", lambda e: e.tensor_reduce(out=sm[0:nt, 1:2], in_=cm[0:nt, 0:nch], axis=AX.X, op=ALU.min),
                     r=[cmid], w=["sm"])
                TT("dve", sm[0:nt, 2:3], sm[0:nt, 0:1], sm[0:nt, 1:2], ALU.subtract, ["sm"], ["sm"])
                TS("dve", sm[0:nt, 3:4], sm[0:nt, 2:3], 1.02, 2e-6, ALU.mult, ALU.add, ["sm"], ["sm"])
                TS("dve", hwv[0:nt, :], pow2[0:nt, :], sm[0:nt, 3:4], None, ALU.mult, None, ["pow2", "sm"], ["hwv"])
                TS("dve", sm[0:nt, 4:5], sm[0:nt, 2:3], -0.01, -1e-6, ALU.mult, ALU.add, ["sm"], ["sm"])
                TT("dve", sm[0:nt, 5:6], sm[0:nt, 4:5], sm[0:nt, 1:2], ALU.add, ["sm"], ["sm"])
                TT("dve", cand[0][0:nt, :], sm[0:nt, 5:6], hwv[0:nt, 0:1], ALU.add, ["sm", "hwv"], ["cand0"])
                done = 0
                for k in range(1, NBIS + 1):
                    ci, co = cand[(k - 1) % 2], cand[k % 2]
                    cid, coid = "cand%d" % ((k - 1) % 2), "cand%d" % (k % 2)
                    ACT(junkA[0:nt, 0:nA], Ir[0:nt, nD:n], AF.Sign, [Iid, cid], ["junkA", "sB"],
                        bias=ci[0:nt, 0:1], scale=-1.0, accum_out=sB[0:nt, 0:1])
                    TS("dve", mk[0:nt, 0:nD], Ir[0:nt, 0:nD], ci[0:nt, 0:1], None, ALU.is_ge, ALU.add,
                       [Iid, cid], [mkid, "cnt"], accum=cnt[0:nt, 0:1])
                    STT(tmpc[0:nt, :], cnt[0:nt, :], 2.0, sB[0:nt, :], ALU.mult, ALU.subtract, ["cnt", "sB"], ["tmpc"])
                    last = (k == NBIS)
                    TS("dve", sv[0:nt, :], tmpc[0:nt, :], 511.0 - nA, 1.0 if last else 0.5, ALU.is_ge, ALU.subtract,
                       ["tmpc"], ["sv"])
                    STT((tau if last else co)[0:nt, :], sv[0:nt, :], hwv[0:nt, k - 1:k], ci[0:nt, :], ALU.mult,
                        ALU.add, ["sv", "hwv", cid], ["tau" if last else coid])
                    want = (nxt_n * k) // NBIS
                    advance(nxt, want - done)
                    done = want
                TS("dve", mk[0:nt, 0:n], Ir[0:nt, 0:n], tau[0:nt, 0:1], None, ALU.is_ge, None, [Iid, "tau"], [mkid])
                nsb = n // 128
                for sb0 in range(0, nsb, 8):
                    tc_ = trc[0]
                    tp, tpid = trp[tc_ % 2], "trp%d" % (tc_ % 2)
                    mo, moid = mTo[tc_ % 2], "mTo%d" % (tc_ % 2)
                    nb8 = min(8, nsb - sb0)
                    for q in range(nb8):
                        sb = sb0 + q
                        TR(tp[:, q, 0:nt], mk[0:nt, sb * 128:(sb + 1) * 128], ident[0:nt, 0:nt], [mkid, "ident"], [tpid])
                    CP("act", mo[:, 0:nb8, 0:nt], tp[:, 0:nb8, 0:nt], [tpid], [moid])
                    b0 = slot_blk0[g] + sb0
                    DMA("sp", mT_s[:, b0:b0 + nb8, 0:nt], mo[:, 0:nb8, 0:nt], "sm%d" % (tc_ % 2), [moid], ["mT_s"])
                    trc[0] += 1
                advance(nxt, 1000)

            advance(stageI(slot_order[0]), 1000)
            for i_ in range(17):
                g = slot_order[i_]
                gn = slot_order[i_ + 1] if i_ + 1 < 17 else None
                nxt = stageI(gn) if gn is not None else None
                stageII(g, nxt, (slot_nch[gn] + 1) if gn is not None else 0)
            S.barrier()
        idx_scope.__exit__(None, None, None)

        d1w_scope = S.scope()
        d1w_scope.__enter__()
        wa = S.sbuf("wa", [128, 8, 1024], BF16)
        wb = S.sbuf("wb", [128, 8, 1024], BF16)
        wo = S.sbuf("wo", [128, 8, 1024], BF16)
        for p in range(2):
            with S.scope():
                KT = S.sbuf("KT", [128, 4, 8192], BF16)
                Vv = S.sbuf("Vv", [128, 64, 4, 129], BF16)
                QT = [S.sbuf("QT%d" % i, [128, 4, 128], BF16) for i in range(2)]
                mTi = [S.sbuf("mTi%d" % i, [128, 8, 128], BF16) for i in range(3)]
                Eb = [S.sbuf("Eb%d" % i, [128, 4, 128], BF16) for i in range(3)]
                Pb = [S.sbuf("Pb%d" % i, [128, 4, 128], BF16) for i in range(3)]
                rinv = S.sbuf("rinv", [128, 4], F32)
                bo = [S.sbuf("bo%d" % i, [128, 4, 128], BF16) for i in range(2)]
                boTt = [S.sbuf("boTt%d" % i, [128, 4, 128], BF16) for i in range(2)]
                Sps = [S.psum("Sps%d" % i, [128, 4, 128]) for i in range(3)]
                Ops = [S.psum("Ops%d" % i, [128, 2, 129]) for i in range(4)]
                trb = S.psum("trb", [128, 4, 128], BF16)
                for c in range(16):
                    DMA("pool", KT[:, :, c * 512:(c + 1) * 512], KT_s[:, 4 * p:4 * p + 4, c * 512:(c + 1) * 512],
                        "lk%d" % (c % 4), [], ["KT%d" % c])
                    DMA("pool", Vv[:, 4 * c:4 * c + 4, :, :].rearrange("p b h d -> p b (h d)"),
                        V_s[:, 4 * c:4 * c + 4, 4 * p * 129:(4 * p + 4) * 129], "lv%d" % (c % 4), [], ["Vv%d" % c])
                iters = [(g, sb) for g in range(17) for sb in range(slot_nch[g] * 4)]
                groups = [(g, sb) for (g, sb) in iters if sb % 8 == 0]
                gidx = {gs: i for i, gs in enumerate(groups)}
                LOOK = 2
                deferred = []

                def load_q(g):
                    nt = slot_nt[g]
                    DMA("pool", QT[g % 2][:, :, 0:nt], qT_s[:, 4 * p:4 * p + 4, g * 128:g * 128 + nt],
                        "lq%d" % (g % 2), [], ["QT%d" % (g % 2)])

                def load_m(i):
                    g, sb = groups[i]
                    nt = slot_nt[g]
                    nb8 = min(8, slot_nch[g] * 4 - sb)
                    b0 = slot_blk0[g] + sb
                    DMA("pool", mTi[i % 3][:, 0:nb8, 0:nt], mT_s[:, b0:b0 + nb8, 0:nt], "lm%d" % (i % 3), [],
                        ["mTi%d" % (i % 3)])

                def stage1(it):
                    g, sb = iters[it]
                    nt = slot_nt[g]
                    c = sb // 4
                    if sb == 0:
                        if g == 0:
                            load_q(0)
                        if g + 1 < 17:
                            load_q(g + 1)
                        if p == 1 and g == 9:
                            DMA("pool", wa[:], wview(w_a, 0, 1024), "w0", [], ["wa"])
                            DMA("pool", wb[:], wview(w_b, 0, 1024), "w1", [], ["wb"])
                            DMA("pool", wo[:], wview(w_o, 0, 1024), "w4", [], ["wo"])
                    if sb % 8 == 0:
                        i = gidx[(g, sb)]
                        if i == 0:
                            load_m(0)
                        if i + 1 < len(groups):
                            load_m(i + 1)
                    i = gidx[(g, sb - sb % 8)]
                    mt, mtid = mTi[i % 3], "mTi%d" % (i % 3)
                    qt, qid = QT[g % 2], "QT%d" % (g % 2)
                    sp_, spid = Sps[it % 3], "Sps%d" % (it % 3)
                    eb, ebid = Eb[it % 3], "Eb%d" % (it % 3)
                    pb, pbid = Pb[it % 3], "Pb%d" % (it % 3)
                    for h in range(4):
                        MM(sp_[:, h, 0:nt], KT[:, h, sb * 128:(sb + 1) * 128], qt[:, h, 0:nt], True, True,
                           ["KT%d" % c, qid], [spid], skip_group_check=True)
                    ACT(eb[:, :, 0:nt], sp_[:, :, 0:nt], AF.Exp, [spid], [ebid])
                    TT("dve", pb[:, :, 0:nt], eb[:, :, 0:nt],
                       mt[:, sb % 8, 0:nt].unsqueeze(1).to_broadcast([128, 4, nt]), ALU.mult, [ebid, mtid], [pbid])

                def stage2(it):
                    g, sb = iters[it]
                    nt = slot_nt[g]
                    nsb = slot_nch[g] * 4
                    c = sb // 4
                    pb, pbid = Pb[it % 3], "Pb%d" % (it % 3)
                    o0, o0id = Ops[(g % 2) * 2], "Ops%d" % ((g % 2) * 2)
                    o1, o1id = Ops[(g % 2) * 2 + 1], "Ops%d" % ((g % 2) * 2 + 1)
                    for h in range(4):
                        ob, obid = (o0, o0id) if h < 2 else (o1, o1id)
                        MM(ob[0:nt, h % 2, :], pb[:, h, 0:nt], Vv[:, sb, h, :], sb == 0 and h % 2 == 0,
                           sb == nsb - 1, [pbid, "Vv%d" % c], [obid], skip_group_check=True)
                    if sb == nsb - 1:
                        bt, btid = bo[g % 2], "bo%d" % (g % 2)
                        btT, btTid = boTt[g % 2], "boTt%d" % (g % 2)
                        S.op("dve", lambda e: e.reciprocal(rinv[0:nt, 0:2], o0[0:nt, :, 128]), r=[o0id], w=["rinv"])
                        S.op("dve", lambda e: e.reciprocal(rinv[0:nt, 2:4], o1[0:nt, :, 128]), r=[o1id], w=["rinv"])
                        for h in range(4):
                            ob, obid = (o0, o0id) if h < 2 else (o1, o1id)
                            TS("dve", bt[0:nt, h, :], ob[0:nt, h % 2, 0:128], rinv[0:nt, h:h + 1], None, ALU.mult,
                               None, [obid, "rinv"], [btid])

                        def fin():
                            for h in range(4):
                                TR(trb[:, h, 0:nt], bt[0:nt, h, :], ident[0:nt, 0:nt], [btid, "ident"], ["trb"])
                            CP("act", btT[:, :, 0:nt], trb[:, :, 0:nt], ["trb"], [btTid])
                            DMA("sp", boT_s[:, 4 * p:4 * p + 4, g * 128:g * 128 + nt], btT[:, :, 0:nt],
                                "sb%d" % (g % 2), [btTid], ["boT_s"])
                        deferred.append((it + 6, fin))

                for k in range(len(iters) + LOOK):
                    if k < len(iters):
                        stage1(k)
                    if k - LOOK >= 0:
                        stage2(k - LOOK)
                    while deferred and deferred[0][0] <= k - LOOK:
                        deferred.pop(0)[1]()
                while deferred:
                    deferred.pop(0)[1]()
                S.barrier()

        def layernorm_fm(z, zid, n, gcol, bcol, outt, outid, sq, sqid, psm, psq, tmp):
            mean_sb, m2, rstd = tmp
            for kc in range(8):
                ACT(sq[:, kc, 0:n], z[:, kc, 0:n], AF.Square, [zid], [sqid])
            for kc in range(8):
                MM(psm[:, 0:n], ones_f[:], z[:, kc, 0:n], kc == 0, kc == 7, ["ones_f", zid], ["psm"])
            for kc in range(8):
                MM(psq[:, 0:n], ones_f[:], sq[:, kc, 0:n], kc == 0, kc == 7, ["ones_f", sqid], ["psq"])
            CP("act", mean_sb[:, 0:n], psm[:, 0:n], ["psm"], ["mean_sb"])
            TT("pool", m2[:, 0:n], mean_sb[:, 0:n], mean_sb[:, 0:n], ALU.mult, ["mean_sb"], ["m2"])
            TT("dve", rstd[:, 0:n], psq[:, 0:n], m2[:, 0:n], ALU.subtract, ["psq", "m2"], ["rstd"])
            TS("dve", rstd[:, 0:n], rstd[:, 0:n], 0.0, EPS, ALU.max, ALU.add, ["rstd"], ["rstd"])
            S.op("act", lambda e: e.sqrt(rstd[:, 0:n], rstd[:, 0:n]), r=["rstd"], w=["rstd"])
            S.op("dve", lambda e: e.reciprocal(rstd[:, 0:n], rstd[:, 0:n]), r=["rstd"], w=["rstd"])
            for kc in range(8):
                eng = "dve" if kc % 2 == 0 else "pool"
                TT(eng, sq[:, kc, 0:n], z[:, kc, 0:n], mean_sb[:, 0:n], ALU.subtract, [zid, "mean_sb"], [sqid])
                TT(eng, sq[:, kc, 0:n], sq[:, kc, 0:n], rstd[:, 0:n], ALU.mult, [sqid, "rstd"], [sqid])
                TS(eng, outt[:, kc, 0:n], sq[:, kc, 0:n], fvec[:, gcol + kc:gcol + kc + 1],
                   fvec[:, bcol + kc:bcol + kc + 1], ALU.mult, ALU.add, [sqid, "fvec"], [outid])

        with S.scope():
            wga = S.sbuf("wga", [128, 8, 1024], BF16)
            wgb = S.sbuf("wgb", [128, 8, 1024], BF16)
            aTi = [S.sbuf("aTi%d" % i, [128, 8, 512], BF16) for i in range(2)]
            bTi = [S.sbuf("bTi%d" % i, [128, 8, 512], BF16) for i in range(2)]
            xbD = [S.sbuf("xbD%d" % i, [128, 8, 512], BF16) for i in range(2)]
            xfD = [S.sbuf("xfD0", [128, 8, 512], F32)] * 2
            sga2 = [S.sbuf("sga%d" % i, [128, 512], F32) for i in range(2)]
            sgb2 = [S.sbuf("sgb%d" % i, [128, 512], F32) for i in range(2)]
            t12 = [S.sbuf("t1%d" % i, [128, 512], F32) for i in range(2)]
            t22 = [S.sbuf("t2%d" % i, [128, 512], F32) for i in range(2)]
            mixT = S.sbuf("mixT", [128, 8, 512], BF16)
            z1 = S.sbuf("z1", [128, 8, 512], F32)
            sq1 = S.sbuf("sq1", [128, 8, 512], F32)
            mean_sb = S.sbuf("mean_sb", [128, 512], F32)
            m2 = S.sbuf("m2", [128, 512], F32)
            rstd = S.sbuf("rstd", [128, 512], F32)
            psD = [S.psum("psD%d" % i, [128, 512]) for i in range(6)]
            psm = S.psum("psm", [128, 512])
            psq = S.psum("psq", [128, 512])
            DMA("pool", wga[:], wview(w_in, C_GA, 1024), "w2", [], ["wga"])
            DMA("pool", wgb[:], wview(w_in, C_GB, 1024), "w3", [], ["wgb"])
            protD = Rot([(psD[i], "psD%d" % i) for i in range(6)])
            pending1 = None
            for T in range(5):
                n = 512 if T < 4 else 32
                tok0 = T * 512
                i2 = T % 2
                DMA("pool", aTi[i2][:, :, 0:n], aT_s[:, :, tok0:tok0 + n], "la%d" % i2, [], ["aTi%d" % i2])
                DMA("pool", bTi[i2][:, :, 0:n], boT_s[:, :, tok0:tok0 + n], "lb%d" % i2, [], ["bTi%d" % i2])
                DMA("pool", xbD[i2][:, :, 0:n], wview(xT_own, tok0, n), "xA%d" % i2, [], ["xbD%d" % i2])
                DMA("sp", xfD[i2][:, :, 0:n], wview(xT_own, tok0, n), "xF0", [], ["xfD0"])
                for m in range(8):
                    pA, pAid = protD.next()
                    pBb, pBid = protD.next()
                    pGA, pGAid = protD.next()
                    pGB, pGBid = protD.next()
                    for kc in range(8):
                        MM(pA[:, 0:n], wa[:, kc, m * 128:(m + 1) * 128], aTi[i2][:, kc, 0:n], kc == 0, kc == 7,
                           ["wa", "aTi%d" % i2], [pAid])
                    for kc in range(8):
                        MM(pBb[:, 0:n], wb[:, kc, m * 128:(m + 1) * 128], bTi[i2][:, kc, 0:n], kc == 0, kc == 7,
                           ["wb", "bTi%d" % i2], [pBid])
                    for kc in range(8):
                        MM(pGA[:, 0:n], wga[:, kc, m * 128:(m + 1) * 128], xbD[i2][:, kc, 0:n], kc == 0, kc == 7,
                           ["wga", "xbD%d" % i2], [pGAid])
                    for kc in range(8):
                        MM(pGB[:, 0:n], wgb[:, kc, m * 128:(m + 1) * 128], xbD[i2][:, kc, 0:n], kc == 0, kc == 7,
                           ["wgb", "xbD%d" % i2], [pGBid])
                    sga, sgb, t1, t2 = sga2[m % 2], sgb2[m % 2], t12[m % 2], t22[m % 2]
                    sfx = str(m % 2)
                    ACT(sga[:, 0:n], pGA[:, 0:n], AF.Sigmoid, [pGAid], ["sga" + sfx])
                    ACT(sgb[:, 0:n], pGB[:, 0:n], AF.Sigmoid, [pGBid], ["sgb" + sfx])
                    TT("dve", t1[:, 0:n], pA[:, 0:n], sga[:, 0:n], ALU.mult, [pAid, "sga" + sfx], ["t1" + sfx])
                    TT("dve", t2[:, 0:n], pBb[:, 0:n], sgb[:, 0:n], ALU.mult, [pBid, "sgb" + sfx], ["t2" + sfx])
                    TT("pool", mixT[:, m, 0:n], t1[:, 0:n], t2[:, 0:n], ALU.add, ["t1" + sfx, "t2" + sfx], ["mixT"])
                    if m == 1 and pending1 is not None:
                        pending1()
                        pending1 = None
                for nn in range(8):
                    pY, pYid = protD.next()
                    for kc in range(8):
                        MM(pY[:, 0:n], wo[:, kc, nn * 128:(nn + 1) * 128], mixT[:, kc, 0:n], kc == 0, kc == 7,
                           ["wo", "mixT"], [pYid])
                    STT(z1[:, nn, 0:n], xfD[i2][:, nn, 0:n], ALPHA, pY[:, 0:n], ALU.mult, ALU.add,
                        ["xfD0", pYid], ["z1"])

                def fin1(n=n, tok0=tok0):
                    layernorm_fm(z1, "z1", n, 0, 8, sq1, "sq1", sq1, "sq1", psm, psq, (mean_sb, m2, rstd))
                    DMA("sp", h1T_s[:, :, tok0:tok0 + n], sq1[:, :, 0:n], "sh0", ["sq1"], ["h1T_s"])
                pending1 = fin1
            pending1()
            S.barrier()
        d1w_scope.__exit__(None, None, None)

        with S.scope():
            TW = 256
            wup = S.sbuf("wup", [128, 8, 5632], BF16)
            wdn = S.sbuf("wdn", [128, 22, 1024], BF16)
            upH = S.sbuf("upH", [128, 44, 32], F32)
            validH = S.sbuf("validH", [128, 32], F32)
            h1f2 = [S.sbuf("h1f%d" % i, [128, 8, TW], F32) for i in range(2)]
            h1b2 = [S.sbuf("h1b%d" % i, [128, 8, TW], BF16) for i in range(2)]
            h1f, h1b = h1f2[1], h1b2[1]
            ext = [S.sbuf("ext%d" % i, [128, 2, 130], F32) for i in range(4)]
            yg2 = [S.sbuf("yg%d" % i, [128, 2, 128], F32) for i in range(2)]
            yv2 = [S.sbuf("yv%d" % i, [128, 2, 128], F32) for i in range(2)]
            sgt2 = [S.sbuf("sgt%d" % i, [128, 2, 128], F32) for i in range(2)]
            aF = S.sbuf("aF", [128, 22, TW], BF16)
            sq2 = S.sbuf("sq2", [128, 8, TW], F32)
            mean_sb = S.sbuf("mean_sb2", [128, TW], F32)
            m2 = S.sbuf("m2b", [128, TW], F32)
            rstd = S.sbuf("rstd2", [128, TW], F32)
            psU = [S.psum("psU%d" % i, [128, 512]) for i in range(4)]
            psF = [S.psum("psF%d" % i, [128, 512]) for i in range(2)]
            psm = S.psum("psm2", [128, 512])
            psq = S.psum("psq2", [128, 512])
            for q4 in (0, 2, 1, 3):
                DMA("pool", wup[:, :, q4 * 1408:(q4 + 1) * 1408], wview(w_up, q4 * 1408, 1408), "w%d" % q4, [],
                    ["wup%d" % q4])

            def wupid(ch):
                return "wup%d" % (ch // 11)
            DMA("pool", wdn[:], w_dn.rearrange("(kc p) n -> p kc n", p=128), "w4", [], ["wdn"])
            DMA("sp", validH[:], validH_d, "c2", [], ["validH"])
            protU = Rot([(psU[i], "psU%d" % i) for i in range(4)])
            erot = Rot([(ext[i], "ext%d" % i) for i in range(4)])
            DMA("sp", h1f[:, :, 0:32], h1T_s[:, :, 2048:2080], "lh1", [], ["h1f1"])
            CP("dve", h1b[:, :, 0:32], h1f[:, :, 0:32], ["h1f1"], ["h1b1"])

            def halo_chunk(ch):
                ps, pid = protU.next()
                for kc in range(8):
                    MM(ps[:, 0:32], wup[:, kc, ch * 128:(ch + 1) * 128], h1b[:, kc, 0:32], kc == 0, kc == 7,
                       [wupid(ch), "h1b1"], [pid])
                TT("dve", upH[:, ch, :], ps[:, 0:32], validH[:], ALU.mult, [pid, "validH"], ["upH%d" % ch])
            halo_chunk(0)
            halo_chunk(22)
            NT2 = 2048 // TW

            def load_h1(T):
                DMA("sp", h1f2[T % 2][:], h1T_s[:, :, T * TW:(T + 1) * TW], "lh%d" % (T % 2), [], ["h1f%d" % (T % 2)])

            load_h1(0)
            pending = None
            for T in range(NT2):
                hf, hfid = h1f2[T % 2], "h1f%d" % (T % 2)
                hb, hbid = h1b2[T % 2], "h1b%d" % (T % 2)
                for kc in range(8):
                    CP("pool" if kc % 2 else "dve", hb[:, kc, :], hf[:, kc, :], [hfid], [hbid])
                for i in range(22):
                    yg, yv, sgt = yg2[i % 2], yv2[i % 2], sgt2[i % 2]
                    sfx = str(i % 2)
                    if T == 0 and i + 1 < 22:
                        halo_chunk(i + 1)
                        halo_chunk(22 + i + 1)
                    for which, ytile, yid in ((0, yg, "yg" + sfx), (1, yv, "yv" + sfx)):
                        ch = i + 22 * which
                        ps, pid = protU.next()
                        for kc in range(8):
                            MM(ps[:, 0:TW], wup[:, kc, ch * 128:(ch + 1) * 128], hb[:, kc, :], kc == 0, kc == 7,
                               [wupid(ch), hbid], [pid])
                        ex, exid = erot.next()
                        psv = ps[:, 0:TW].rearrange("p (b t) -> p b t", b=2)
                        CP("act", ex[:, :, 2:130], psv, [pid], [exid])
                        ACT(ytile[:], psv, AF.Identity, [pid, "fvec"], [yid], scale=fvec[:, 120 + ch:121 + ch],
                            bias=fvec[:, 164 + ch:165 + ch])
                        CP("pool", ex[:, :, 0:2], upH[:, ch, 4 * T:4 * T + 4].rearrange("p (b t) -> p b t", b=2),
                           ["upH%d" % ch], [exid])
                        STT(ytile[:], ex[:, :, 1:129], fvec[:, 76 + ch:77 + ch], ytile[:], ALU.mult, ALU.add,
                            [exid, "fvec", yid], [yid])
                        STT(ytile[:], ex[:, :, 0:128], fvec[:, 32 + ch:33 + ch], ytile[:], ALU.mult, ALU.add,
                            [exid, "fvec", yid], [yid])
                    ACT(sgt[:], yg[:], AF.Silu, ["yg" + sfx], ["sgt" + sfx])
                    TT("pool", aF[:, i, :].rearrange("p (b t) -> p b t", b=2), sgt[:], yv[:], ALU.mult,
                       ["sgt" + sfx, "yv" + sfx], ["aF"])
                    if i == 2 and pending is not None:
                        pending()
                        pending = None
                        if T + 1 < NT2:
                            load_h1(T + 1)
                if T == 0 and NT2 > 1:
                    load_h1(1)
                for nn in range(8):
                    pf, pfid = psF[nn % 2], "psF%d" % (nn % 2)
                    for kc in range(22):
                        MM(pf[:, 0:TW], wdn[:, kc, nn * 128:(nn + 1) * 128], aF[:, kc, :], kc == 0, kc == 21,
                           ["wdn", "aF"], [pfid])
                    STT(hf[:, nn, :], hf[:, nn, :], ALPHA, pf[:, 0:TW], ALU.mult, ALU.add, [hfid, pfid], [hfid])

                def fin(T=T, hf=hf, hfid=hfid):
                    layernorm_fm(hf, hfid, TW, 16, 24, sq2, "sq2", sq2, "sq2", psm, psq, (mean_sb, m2, rstd))
                    DMA("sp", outT[:, :, T * TW:(T + 1) * TW], sq2[:], "so0", ["sq2"], ["outT"])
                pending = fin
            pending()
            S.barrier()
        S.wait_all("sp")
        S.emit()
    return nc


_NC_CACHE = {}


def _host_inputs(x, w_in, ln_v_g, ln_v_b, w_spatial, b_spatial, ln_kidx_g, ln_kidx_b,
                 w_branch_a, w_branch_b, w_o, ln1_g, ln1_b, w_up, conv_w, conv_b, w_down, ln2_g, ln2_b):
    f32 = np.float32
    x = np.asarray(x, f32)

    def fm(v, kc):
        return np.asarray(v, f32).reshape(kc, 128).T

    fvec = np.zeros((128, 208), f32)
    fvec[:, 0:8] = fm(ln1_g[0], 8)
    fvec[:, 8:16] = fm(ln1_b[0], 8)
    fvec[:, 16:24] = fm(ln2_g[0], 8)
    fvec[:, 24:32] = fm(ln2_b[0], 8)
    for j in range(3):
        fvec[:, 32 + 44 * j:76 + 44 * j] = fm(conv_w[0, j], 44)
    fvec[:, 164:208] = fm(conv_b[0], 44)
    bvec = np.zeros((128, 2560), f32)
    bvec[:, 0:1024] = np.asarray(ln_v_g[0], f32)[None, :]
    bvec[:, 1024:2048] = np.asarray(ln_v_b[0], f32)[None, :]
    bvec[:, 2048:2304] = np.tile(np.asarray(ln_kidx_g[0], f32), 4)[None, :]
    bvec[:, 2304:2560] = np.tile(np.asarray(ln_kidx_b[0], f32), 4)[None, :]
    wsT = np.ascontiguousarray(np.transpose(np.asarray(w_spatial[0], f32), (2, 0, 1)))
    s_i = np.arange(128)
    trimask = (s_i[:, None] <= s_i[None, :]).astype(f32)
    common = {
        "w_in": np.ascontiguousarray(np.asarray(w_in[0], f32)),
        "w_a": np.ascontiguousarray(np.asarray(w_branch_a[0], f32)),
        "w_b": np.ascontiguousarray(np.asarray(w_branch_b[0], f32)),
        "w_o": np.ascontiguousarray(np.asarray(w_o[0], f32)),
        "w_up": np.ascontiguousarray(np.asarray(w_up[0], f32)),
        "w_dn": np.ascontiguousarray(np.asarray(w_down[0], f32)),
        "wsT": wsT,
        "bs_row": np.ascontiguousarray(np.asarray(b_spatial[0], f32).reshape(1, 1024)),
        "fvec": fvec, "bvec": bvec,
        "ident": np.eye(128, dtype=f32),
        "sel": np.stack([np.stack([(np.arange(128) // 64 + 2 * pr == h).astype(f32) for pr in range(4)])
                         for h in range(8)]),
        "trimask": trimask,
        "pow2": np.tile((2.0 ** -np.arange(1, NBIS + 1)).astype(f32)[None, :], (128, 1)),
    }
    xT = [np.ascontiguousarray(x[b].T) for b in range(2)]
    in_maps = []
    for c in range(8):
        b, j = c // 4, c % 4
        own_cols = []
        for g in range(16):
            t0 = (4 * g + j) * 128
            own_cols.append(np.arange(t0, t0 + 128))
        halo_pos = []
        for g in range(16):
            t0 = (4 * g + j) * 128
            halo_pos += [t0 - 2, t0 - 1]
        halo_pos = np.array(halo_pos)
        own_idx = np.concatenate(own_cols)
        xT_own = np.zeros((1024, NOWN), f32)
        xT_own[:, 0:2048] = xT[b][:, own_idx]
        hv = halo_pos >= 0
        xT_own[:, 2048:][:, hv] = xT[b][:, halo_pos[hv]]
        xT_prev = np.zeros((1024, 2048), f32)
        for g in range(16):
            pb = 4 * g + j - 1
            if pb >= 0:
                xT_prev[:, g * 128:(g + 1) * 128] = xT[b][:, pb * 128:(pb + 1) * 128]
        kk = np.arange(512)
        rblk, ks = kk // 128, kk % 128
        t = np.arange(128)
        allowed = (rblk[None, :] < j) | ((rblk[None, :] == j) & (ks[None, :] <= t[:, None]))
        pen_last = np.where(allowed, 0.0, NEG).astype(f32)
        sidx = np.arange(8192)
        hp = np.where(halo_pos >= 0, halo_pos, 0)
        pen_halo = np.where(sidx[None, :] <= hp[:, None], 0.0, NEG).astype(f32)
        validH = np.tile(hv.astype(f32)[None, :], (128, 1))
        m = dict(common)
        m.update({"xT_all": xT[b], "xT_own": xT_own, "xT_prev": xT_prev, "pen_last": pen_last,
                  "pen_halo": pen_halo, "validH": validH})
        in_maps.append(m)
    return in_maps


def kernel(**inputs):
    in_maps = _host_inputs(**inputs)
    if "nc" not in _NC_CACHE:
        _NC_CACHE["nc"] = build_program()
    nc = _NC_CACHE["nc"]
    res = run_bass_kernel_spmd(nc, in_maps, core_ids=list(range(8)))
    out = np.zeros((2, 8192, 1024), np.float32)
    for c in range(8):
        b, j = c // 4, c % 4
        o = np.asarray(res.results[c]["outT"], np.float32)
        o = o.transpose(2, 1, 0).reshape(2048, 1024)
        for g in range(16):
            t0 = (4 * g + j) * 128
            out[b, t0:t0 + 128, :] = o[g * 128:(g + 1) * 128, :]
    return out
```
